# Optimizing a Trainium2 kernel written in Bass

```python
import math
import jax, jax.numpy as jnp
from jax import lax
import numpy as np

D_MODEL = 1024
BATCH = 2
SEQ = 16384
DEPTH = 4

CHUNK = 64
Q_BLOCK = 128
CONV_K = 4
EPS = 1e-6

GLA_HEADS = 4
GLA_DK = 64
GLA_DV = 128
GLA_QK = GLA_HEADS * GLA_DK
GLA_WIDTH = GLA_HEADS * GLA_DV
GLA_GATE_RANK = 16
GLA_GATE_TEMP = 16.0

DIFF_HEADS = 4
DIFF_DH = 64
DIFF_DV = 2 * DIFF_DH
DIFF_QK = DIFF_HEADS * 2 * DIFF_DH
DIFF_WIDTH = DIFF_HEADS * DIFF_DV

MIX_WIDTH = GLA_WIDTH + DIFF_WIDTH
GLA_CONV_WIDTH = 2 * GLA_QK + GLA_WIDTH
SPLIT_SIZES = (GLA_QK, GLA_QK, GLA_WIDTH, GLA_GATE_RANK, GLA_WIDTH,
               DIFF_QK, DIFF_QK, DIFF_WIDTH, DIFF_WIDTH)
IN_WIDTH = 2 * GLA_QK + 2 * GLA_WIDTH + GLA_GATE_RANK + 2 * DIFF_QK + 2 * DIFF_WIDTH

kernel_name = "hybrid_gla_diffattn_adaln_trunk"


def rms_norm(x, g):
    xf = x.astype(jnp.float32)
    y = xf * lax.rsqrt(jnp.mean(xf * xf, axis=-1, keepdims=True) + EPS)
    return (y * g.astype(jnp.float32)).astype(x.dtype)


def split_columns(p):
    idx = np.cumsum(np.array(SPLIT_SIZES))[:-1].tolist()
    return jnp.split(p, idx, axis=-1)


def causal_depthwise_conv(u, w):
    S = u.shape[1]
    up = jnp.pad(u, ((0, 0), (CONV_K - 1, 0), (0, 0)))
    out = up[:, 0:S] * w[0]
    for j in range(1, CONV_K):
        out = out + up[:, j:j + S] * w[j]
    return out


def gla_branch(q, k, v, glr, z, w_gk, b_gk, g_out):
    B, S, _ = q.shape
    nc = S // CHUNK
    f32 = jnp.float32
    q = q.astype(f32).reshape(B, nc, CHUNK, GLA_HEADS, GLA_DK) * (GLA_DK ** -0.5)
    k = k.astype(f32).reshape(B, nc, CHUNK, GLA_HEADS, GLA_DK)
    v = v.astype(f32).reshape(B, nc, CHUNK, GLA_HEADS, GLA_DV)
    log_a = jax.nn.log_sigmoid((glr @ w_gk + b_gk).astype(f32)) / GLA_GATE_TEMP
    log_a = log_a.reshape(B, nc, CHUNK, GLA_HEADS, GLA_DK)
    b = jnp.cumsum(log_a, axis=2)
    b_end = b[:, :, -1]
    k_dec = k * jnp.exp(b_end[:, :, None] - b)
    u = jnp.einsum('bnchk,bnchv->nbhkv', k_dec, v)
    a = jnp.exp(b_end).transpose(1, 0, 2, 3)

    def step(state, inp):
        a_c, u_c = inp
        state = a_c[..., None] * state + u_c
        return state, state

    s0 = jnp.zeros((B, GLA_HEADS, GLA_DK, GLA_DV), f32)
    _, states = lax.scan(step, s0, (a, u))
    o = jnp.einsum('bnchk,nbhkv->bnchv', q, states)
    o = rms_norm(o, g_out).reshape(B, S, GLA_WIDTH)
    return o * jax.nn.silu(z.astype(f32))


def diff_branch(q, k, v, z, qn_g, kn_g, lam, lam_init, g_out):
    B, S, _ = q.shape
    f32 = jnp.float32
    q = rms_norm(q.reshape(B, S, DIFF_HEADS, 2, DIFF_DH), qn_g) * (DIFF_DH ** -0.5)
    k = rms_norm(k.reshape(B, S, DIFF_HEADS, 2, DIFF_DH), kn_g)
    v = v.astype(f32).reshape(B, S, DIFF_HEADS, DIFF_DV)
    key_chunk = jnp.arange(S) // CHUNK
    nb = S // Q_BLOCK
    qb = q.reshape(B, nb, Q_BLOCK, DIFF_HEADS, 2, DIFF_DH).transpose(1, 0, 2, 3, 4, 5)
    q_chunk = key_chunk.reshape(nb, Q_BLOCK)

    def block(args):
        qi, qc = args
        s = jnp.einsum('bqhmd,bkhmd->bhmqk', qi, k).astype(f32)
        mask = key_chunk[None, :] <= qc[:, None]
        s = jnp.where(mask, s, -jnp.inf)
        p = jax.nn.softmax(s, axis=-1)
        attn = p[:, :, 0] - lam * p[:, :, 1]
        return jnp.einsum('bhqk,bkhe->bqhe', attn, v)

    o = lax.map(block, (qb, q_chunk))
    o = o.transpose(1, 0, 2, 3, 4).reshape(B, S, DIFF_HEADS, DIFF_DV)
    o = rms_norm(o, g_out) * (1.0 - lam_init)
    return o.reshape(B, S, DIFF_WIDTH) * jax.nn.silu(z.astype(f32))


def setup_inputs(seed: int = 0) -> dict:
    key = jax.random.key(seed)
    ks = jax.random.split(key, 18)
    f32 = jnp.float32
    L, D = DEPTH, D_MODEL
    n = lambda k, s, sc: (jax.random.normal(k, s, f32) * sc)
    return {
        "x": n(ks[0], (BATCH, SEQ, D), 1.0),
        "c": n(ks[1], (BATCH, D), 1.0),
        "w_ada": n(ks[2], (L, D, 3 * D), 0.5 * D ** -0.5),
        "b_ada": n(ks[3], (L, 3 * D), 0.02),
        "norm_g": 1.0 + n(ks[4], (L, D), 0.02),
        "w_in": n(ks[5], (L, D, IN_WIDTH), D ** -0.5),
        "conv_w": n(ks[6], (L, CONV_K, GLA_CONV_WIDTH), CONV_K ** -0.5),
        "w_gk": n(ks[7], (L, GLA_GATE_RANK, GLA_QK), GLA_GATE_RANK ** -0.5),
        "b_gk": n(ks[8], (L, GLA_QK), 0.1),
        "gla_norm_g": 1.0 + n(ks[9], (L, GLA_DV), 0.02),
        "qn_g": 1.0 + n(ks[10], (L, DIFF_DH), 0.02),
        "kn_g": 1.0 + n(ks[11], (L, DIFF_DH), 0.02),
        "lam_q1": n(ks[12], (L, DIFF_DH), 0.1),
        "lam_k1": n(ks[13], (L, DIFF_DH), 0.1),
        "lam_q2": n(ks[14], (L, DIFF_DH), 0.1),
        "lam_k2": n(ks[15], (L, DIFF_DH), 0.1),
        "diff_norm_g": 1.0 + n(ks[16], (L, DIFF_DV), 0.02),
        "w_out": n(ks[17], (L, MIX_WIDTH, D), MIX_WIDTH ** -0.5),
    }


def reference(x, c, w_ada, b_ada, norm_g, w_in, conv_w, w_gk, b_gk, gla_norm_g,
              qn_g, kn_g, lam_q1, lam_k1, lam_q2, lam_k2, diff_norm_g, w_out):
    f32 = jnp.float32
    c_act = jax.nn.silu(c)
    for l in range(DEPTH):
        mod = c_act @ w_ada[l] + b_ada[l]
        shift, scale, gate = jnp.split(mod, 3, axis=-1)
        h = rms_norm(x, norm_g[l]) * (1.0 + scale[:, None]) + shift[:, None]
        proj = h @ w_in[l]
        gq, gk, gv, glr, gz, dq, dk, dv, dz = split_columns(proj)
        gqkv = jax.nn.silu(causal_depthwise_conv(jnp.concatenate([gq, gk, gv], axis=-1), conv_w[l]))
        gq, gk, gv = jnp.split(gqkv, [GLA_QK, 2 * GLA_QK], axis=-1)
        o_gla = gla_branch(gq, gk, gv, glr, gz, w_gk[l], b_gk[l], gla_norm_g[l])
        lam_init = 0.8 - 0.6 * math.exp(-0.3 * l)
        lam = (jnp.exp(jnp.sum(lam_q1[l].astype(f32) * lam_k1[l].astype(f32)))
               - jnp.exp(jnp.sum(lam_q2[l].astype(f32) * lam_k2[l].astype(f32))) + lam_init)
        o_diff = diff_branch(dq, dk, dv, dz, qn_g[l], kn_g[l], lam, lam_init, diff_norm_g[l])
        y = jnp.concatenate([o_gla, o_diff], axis=-1).astype(x.dtype) @ w_out[l]
        x = x + gate[:, None] * y
    return x
```

```python
import contextlib
import math

import ml_dtypes
import numpy as np

import concourse.bass as bass
import concourse.mybir as mybir
from concourse.bass_utils import run_bass_kernel_spmd

F32 = mybir.dt.float32
BF16 = mybir.dt.bfloat16
AF = mybir.ActivationFunctionType
ALU = mybir.AluOpType

D = 1024
TQ = 512
NCOL = 912
EPS = 1e-6
DEPTH = 4
C_DQ, C_DK, C_DZ, C_GZ, C_GV, C_GQK, C_DV, C_GLR = 0, 128, 256, 384, 512, 640, 768, 896
CF_ID, CF_U, CF_ONE, CF_CIND = 0, 128, 256, 384
CF_W = 386
CB_X, CB_BLK, CB_D, CB_G, CB_ONE, CB_ID = 0, 128, 256, 384, 512, 640
CB_W = 768
NPP = 24
DBG = set()


class _Ins:
    __slots__ = ("eng", "fn", "deps", "idx", "is_dma", "sig", "tick", "key", "cum", "waits", "ep", "inc")

    def __init__(self, eng, fn, is_dma=False, key=None):
        self.eng = eng
        self.fn = fn
        self.deps = []
        self.idx = -1
        self.is_dma = is_dma
        self.sig = False
        self.tick = 0
        self.key = key
        self.cum = 0
        self.waits = []
        self.ep = 0
        self.inc = 16


class Sched:
    ENGS = ("sp", "pe", "act", "dve", "pool")

    def __init__(self, nc):
        self.nc = nc
        self.q = {e: [] for e in self.ENGS}
        self.res = {}
        self.dma_keys = {}
        self.ep = 0

    def _track(self, ins, reads, writes):
        deps = {}
        for r in reads:
            st = self.res.get(r)
            if st is not None and st[0] is not None:
                deps[id(st[0])] = (st[0], True)
        for w in writes:
            st = self.res.get(w)
            if st is not None:
                if st[0] is not None and id(st[0]) not in deps:
                    deps[id(st[0])] = (st[0], False)
                for rd in st[1]:
                    if id(rd) not in deps:
                        deps[id(rd)] = (rd, False)
        for r in reads:
            st = self.res.setdefault(r, [None, []])
            st[1].append(ins)
        for w in writes:
            self.res[w] = [ins, []]
        deps.pop(id(ins), None)
        ins.deps = list(deps.values())

    def op(self, eng, fn, reads=(), writes=()):
        ins = _Ins(eng, fn)
        ins.idx = len(self.q[eng])
        ins.ep = self.ep
        self._track(ins, reads, writes)
        self.q[eng].append(ins)
        return ins

    def dma(self, queue, fn, key, reads=(), writes=(), inc=16):
        ins = _Ins(queue, fn, is_dma=True, key=key)
        ins.idx = len(self.q[queue])
        ins.ep = self.ep
        ins.inc = inc
        self._track(ins, reads, writes)
        self.dma_keys[key] = self.dma_keys.get(key, 0) + (inc if inc else 1)
        ins.cum = self.dma_keys[key]
        self.q[queue].append(ins)
        return ins

    def emit(self):
        nc = self.nc
        for e in self.ENGS:
            for ins in self.q[e]:
                need = []
                best = {}
                bestk = {}
                for (d, raw) in ins.deps:
                    if d.is_dma:
                        bk = bestk.get(d.key)
                        if bk is None or d.cum > bk.cum:
                            bestk[d.key] = d
                        continue
                    if d.eng == ins.eng and not ins.is_dma:
                        if e == "pe" or not raw:
                            continue
                    b = best.get(d.eng)
                    if b is None or d.idx > b.idx:
                        best[d.eng] = d
                for d in bestk.values():
                    need.append(d)
                for d in best.values():
                    need.append(d)
                ins.waits = need
        for e in self.ENGS:
            seen = {}
            for ins in self.q[e]:
                keep = []
                for d in ins.waits:
                    if d.is_dma:
                        k = ("k", d.key)
                        if seen.get(k, -1) >= d.cum:
                            continue
                        seen[k] = d.cum
                    else:
                        k = ("e", d.eng)
                        if seen.get(k, -1) >= d.idx:
                            continue
                        seen[k] = d.idx
                        d.sig = True
                    keep.append(d)
                ins.waits = keep
        eps = set()
        for e in self.ENGS:
            t = {}
            for ins in self.q[e]:
                if ins.sig and not ins.is_dma:
                    t[ins.ep] = t.get(ins.ep, 0) + 1
                    ins.tick = t[ins.ep]
                    eps.add((e, ins.ep))
        self.max_ticks = {}
        for e in self.ENGS:
            for ins in self.q[e]:
                if ins.tick:
                    self.max_ticks[(e, ins.ep)] = max(self.max_ticks.get((e, ins.ep), 0), ins.tick)
        self.n_ins = {e: len(self.q[e]) for e in self.ENGS}
        stack = contextlib.ExitStack()
        esem = {k: stack.enter_context(nc.semaphore("s_%s_%d" % k)) for k in sorted(eps)}
        ksem = {}
        for n, k in enumerate(self.dma_keys):
            ksem[k] = stack.enter_context(nc.semaphore("k%d" % n))
        q = self.q
        dma_keys = self.dma_keys

        def run(e, h):
            for ins in q[e]:
                for d in ins.waits:
                    if d.is_dma:
                        h.wait_ge(ksem[d.key], d.cum)
                    else:
                        h.wait_ge(esem[(d.eng, d.ep)], d.tick)
                bi = ins.fn(h)
                if ins.is_dma:
                    if ins.inc:
                        bi.then_inc(ksem[ins.key], ins.inc)
                    else:
                        bi.then_inc(ksem[ins.key])
                elif ins.sig:
                    bi.then_inc(esem[(e, ins.ep)], 1)
            if e == "sp":
                for k, v in dma_keys.items():
                    h.wait_ge(ksem[k], v)

        with stack:
            with nc.Block() as block:
                @block.sync
                def _(h):
                    run("sp", h)

                @block.tensor
                def _(h):
                    run("pe", h)

                @block.scalar
                def _(h):
                    run("act", h)

                @block.vector
                def _(h):
                    run("dve", h)

                @block.gpsimd
                def _(h):
                    run("pool", h)


def build_program(S, n_mix, has_prev, do_final, layer_ids, fused=False):
    assert S % TQ == 0
    NT = S // TQ
    n_out = (1 if has_prev else 0) + max(n_mix - 1, 0) + (1 if (do_final and n_mix > 0) else 0)
    if n_mix == 0:
        n_out = 1
    n_ada = n_out + n_mix if not fused else DEPTH
    nc = bass.Bass("TRN2", target_bir_lowering=False)
    dt_in = lambda n, s, d: nc.dram_tensor(n, s, d, kind="ExternalInput").ap()
    dt_out = lambda n, s, d: nc.dram_tensor(n, s, d, kind="ExternalOutput").ap()
    dt_int = lambda n, s, d: nc.dram_tensor(n, s, d, kind="Internal").ap()

    xT_in = dt_in("xT", [D, S], F32)
    cT_in = dt_in("cT", [128, 8], F32)
    cf_in = dt_in("cf", [128, CF_W], F32)
    cb_in = dt_in("cb", [128, CB_W], BF16)
    wada_in = dt_in("w_ada", [n_ada, D, 3 * D], F32)
    bada_in = dt_in("b_adaT", [n_ada, 128, 24], F32)
    if n_mix > 0:
        win_in = dt_in("w_in", [n_mix, D, NCOL], F32)
        pp_in = dt_in("pp", [n_mix, 128, NPP], F32)
        wgk_in = dt_in("wgk", [n_mix, 17, 64], F32)
    if n_out > 0:
        wout_in = dt_in("w_out", [n_out, D, D], F32)
    if has_prev or n_mix == 0:
        oT_in = dt_in("oT_in", [1, D, S], BF16)
    CW = min(2048, S) if fused else S
    NCH = S // CW
    TPC = CW // TQ
    if n_mix > 0:
        if fused:
            ohT = dt_int("ohT", [NCH, 256, CW], BF16)
            oT_fulls = [dt_int("oT_full%d" % k, [NCH, D, CW], BF16) for k in range(2)]
        else:
            ohT = dt_out("ohT", [NCH, 256, CW], BF16)
    if do_final:
        yT = dt_out("yT", [D, S], F32)
    xw = None
    if n_mix > 0 and n_out - (1 if do_final else 0) > 0:
        xw = dt_int("xw", [D, S], F32) if (fused or do_final) else dt_out("xT_out", [D, S], F32)

    def tview(ap, i):
        return ap.rearrange("(kc p) s -> p kc s", p=128)[:, :, i * TQ:(i + 1) * TQ]

    with contextlib.ExitStack() as st:
        def sb(n, s, d):
            return st.enter_context(nc.sbuf_tensor("sb_" + n, s, d))

        xt = sb("xt", [128, 8, TQ], F32)
        ot = sb("ot", [128, 8, TQ], BF16)
        hT = sb("hT", [128, 8, TQ], BF16)
        sqs = sb("sqs", [128, 2, TQ], BF16)
        wout = sb("wout", [128, 8, D], BF16)
        wstage = sb("wstage", [128, 2, D], F32)
        cf = sb("cf", [128, CF_W], F32)
        cb = sb("cb", [128, CB_W], BF16)
        cT = sb("cT", [128, 8], F32)
        cact = sb("cact", [128, 8], F32)
        ctmp = sb("ctmp", [128, 8], F32)
        modT = sb("modT", [128, n_ada, 24], F32)
        badaT = sb("badaT", [128, n_ada, 24], F32)
        FS = [sb("F%d" % k, [128, TQ], F32) for k in range(12)]
        if n_mix > 0:
            win = sb("win", [128, 8, NCOL], BF16)
            KT = sb("KT", [128, S], BF16)
            V = sb("V", [128, S // 128, 129], BF16)
            QT = sb("QT", [128, 2, 2, TQ], BF16)
            FZ = [sb("FZ%d" % k, [128, TQ], F32) for k in range(2)]
            AS = [sb("AS%d" % k, [128, 4, 128], F32) for k in range(3)]
            ON = sb("ON", [128, 4, 128], BF16)
            rc = sb("rc", [128, 2, 4], F32)
            ssq = sb("ssq", [128, 8], F32)
            NPT = 6
            Pt = sb("Pt", [128, NPT, TQ], BF16)
            C0 = sb("C0", [128, TQ + 3], F32)
            C1 = sb("C1", [128, TQ + 3], F32)
            G0 = sb("G0", [32, TQ], F32)
            B0 = sb("B0", [128, TQ], BF16)
            L0 = sb("L0", [128, 256], F32)
            L1 = sb("L1", [128, 256], F32)
            kdec = sb("kdec", [128, 256], BF16)
            vtok = sb("vtok", [128, TQ], BF16)
            a8 = sb("a8", [64, 8], F32)
            Sring = sb("Sring", [64, 8, 128], F32)
            oh = sb("oh", [128, 2, 2, TQ], BF16)
            pp = sb("pp", [128, n_mix, NPP], F32)
            dp = sb("dp", [128, n_mix, 32], F32)
            wgk = sb("wgk", [32, n_mix, 64], F32)
            lamt = sb("lamt", [128, n_mix, 4], F32)
        gates = sb("gates", [128, max(n_out, 1), 8], F32)
        ps = st.enter_context(nc.psum_tensor("ps", [128, 8, TQ], F32))

        SCH = Sched(nc)
        ring = [0]

        def nb():
            ring[0] = (ring[0] + 1) % 4
            return ring[0]

        ringA = [0]
        ringB = [0]

        def nbA():
            ringA[0] = (ringA[0] + 1) % 3
            return ringA[0]

        def nbB():
            ringB[0] = (ringB[0] + 1) % 3
            return 3 + ringB[0]

        def bank(b):
            return ("bank", b)

        def mm(out, lhsT, rhs, start, stop, reads, writes):
            SCH.op("pe", lambda h: h.matmul(out, lhsT=lhsT, rhs=rhs, start=start, stop=stop,
                                            skip_group_check=True), reads, writes)

        def act(out, in_, func, reads, writes, scale=1.0, bias=0.0):
            SCH.op("act", lambda h: h.activation(out=out, in_=in_, func=func, bias=bias, scale=scale),
                   reads, writes)

        def tt(eng, out, in0, in1, op, reads, writes):
            SCH.op(eng, lambda h: h.tensor_tensor(out=out, in0=in0, in1=in1, op=op), reads, writes)

        def ts(eng, out, in0, s1, s2, op0, op1, reads, writes):
            if s2 is None:
                SCH.op(eng, lambda h: h.tensor_scalar(out=out, in0=in0, scalar1=s1, scalar2=None, op0=op0),
                       reads, writes)
            else:
                SCH.op(eng, lambda h: h.tensor_scalar(out=out, in0=in0, scalar1=s1, scalar2=s2, op0=op0, op1=op1),
                       reads, writes)

        def stt(eng, out, in0, scalar, in1, op0, op1, reads, writes):
            SCH.op(eng, lambda h: h.scalar_tensor_tensor(out=out, in0=in0, scalar=scalar, in1=in1, op0=op0, op1=op1),
                   reads, writes)

        def cp(eng, out, in_, reads, writes):
            if eng == "act":
                SCH.op("act", lambda h: h.copy(out=out, in_=in_), reads, writes)
            else:
                SCH.op(eng, lambda h: h.tensor_copy(out=out, in_=in_), reads, writes)

        def rsqrt_mean(dst, src_ps, reads, writes_name):
            act(dst, src_ps, AF.Ln, reads, [writes_name], scale=1.0, bias=EPS)
            act(dst, dst, AF.Exp, [writes_name], [writes_name], scale=-0.5)

        def sigmoid_neg(dst, src, reads, name):
            act(dst, src, AF.Exp, reads, [name], scale=-1.0)
            act(dst, dst, AF.Ln, [name], [name], scale=1.0, bias=1.0)
            act(dst, dst, AF.Exp, [name], [name], scale=-1.0)

        SCH.dma("sp", lambda h: h.dma_start(out=cf[:], in_=cf_in), "c_cf", writes=["cf"])
        SCH.dma("sp", lambda h: h.dma_start(out=cb[:], in_=cb_in), "c_cb", writes=["cb"])
        SCH.dma("sp", lambda h: h.dma_start(out=cT[:], in_=cT_in), "c_cT", writes=["cT"])
        SCH.dma("sp", lambda h: h.dma_start(out=badaT[:], in_=bada_in.rearrange("a p c -> p a c")), "c_bada",
                writes=["badaT"])
        if n_mix > 0:
            SCH.dma("sp", lambda h: h.dma_start(out=pp[:], in_=pp_in.rearrange("l p c -> p l c")), "c_pp", writes=["pp"])
            SCH.op("dve", lambda h: h.memset(wgk[:], 0.0), writes=["wgk"])
            SCH.dma("sp", lambda h: h.dma_start(out=wgk[0:17, :, :], in_=wgk_in.rearrange("l k c -> k l c")), "c_wgk",
                    reads=[], writes=["wgk"])
            SCH.op("dve", lambda h: h.memset(G0[:], 1.0), writes=["G0"])
            for par_ in range(2):
                SCH.op("pool", lambda h, par_=par_: h.memset(QT[:, par_, :, :], 0.0), writes=[("QT", par_)])
            SCH.op("pool", lambda h: h.memset(V[:, :, 128:129], 1.0), writes=["Vones"])
        sigmoid_neg(ctmp[:], cT[:], ["cT"], "ctmp")
        tt("dve", cact[:], cT[:], ctmp[:], ALU.mult, ["cT", "ctmp"], ["cact"])

        def emit_ada(a):
            for blk in range(6):
                SCH.dma("sp", lambda h, blk=blk: h.dma_start(
                    out=xt[:], in_=wada_in[a].rearrange("(kc p) n -> p kc n", p=128)[:, :, blk * 512:(blk + 1) * 512]),
                    "xl", writes=[("xt", dc) for dc in range(8)])
                for cc in range(4):
                    col = blk * 4 + cc
                    for kc in range(8):
                        mm(ps[:, 7, col:col + 1], xt[:, kc, cc * 128:(cc + 1) * 128], cact[:, kc:kc + 1],
                           kc == 0, kc == 7, [("xt", kc), "cact"], [bank(7)])
            tt("dve", modT[:, a, :], ps[:, 7, 0:24], badaT[:, a, :], ALU.add, [bank(7), "badaT"], [("modT", a)])

        def load_weight(dst, src2d, ncols, resname):
            for kc in range(8):
                slot = kc % 2
                SCH.dma("sp", lambda h, kc=kc, slot=slot: h.dma_start(
                    out=wstage[:, slot, 0:ncols], in_=src2d[kc * 128:(kc + 1) * 128, :]),
                    ("wst", slot), writes=[("wstage", slot)])
                cp("pool", dst[:, kc, :], wstage[:, slot, 0:ncols], [("wstage", slot)], [resname])

        def emit_load_x(i, src, src_name):
            SCH.dma("sp", lambda h: h.dma_start(out=xt[:], in_=tview(src, i)), "xl",
                    reads=[(src_name, i)], writes=[("xt", dc) for dc in range(8)])

        def gen_outproj(i, o_src, o_name, gate_idx, dst, dst_name):
            SCH.dma("sp", lambda h: h.dma_start(out=ot[:], in_=tview(o_src[i // TPC], i % TPC)), "ol",
                    reads=[(o_name, i)], writes=["ot"])
            yield
            for dc in range(8):
                b = nbA()
                for kc in range(8):
                    mm(ps[:, b, :], wout[:, kc, dc * 128:(dc + 1) * 128], ot[:, kc, :], kc == 0, kc == 7,
                       ["ot", "wout"], [bank(b)])
                stt("dve", xt[:, dc, :], ps[:, b, :], gates[:, gate_idx, dc:dc + 1], xt[:, dc, :], ALU.mult, ALU.add,
                    [bank(b), ("xt", dc), ("gates", gate_idx)], [("xt", dc)])
                yield
            SCH.dma("pool", lambda h: h.dma_start(out=tview(dst, i), in_=xt[:]), "xs",
                    reads=[("xt", dc) for dc in range(8)], writes=[(dst_name, i)])

        def gen_A(i, li, x_src, x_src_name, prev_args):
            ppl = lambda c0, c1: pp[:, li, c0:c1]
            dpl = lambda c0, c1: dp[:, li, c0:c1]
            t0 = i * TQ
            par = i % 2
            emit_load_x(i, x_src, x_src_name)
            if prev_args is not None:
                yield from gen_outproj(i, *prev_args)
            bx = nbA()
            for kc in range(8):
                sl = kc % 2
                act(sqs[:, sl, :], xt[:, kc, :], AF.Square, [("xt", kc)], [("sqs", sl)])
                mm(ps[:, bx, :], cb[:, CB_X:CB_X + 128], sqs[:, sl, :], kc == 0, kc == 7, [("sqs", sl), "cb"], [bank(bx)])
                if kc % 2 == 1:
                    yield
            rsqrt_mean(FS[0][:], ps[:, bx, :], [bank(bx)], "F0")
            yield
            for kc in range(8):
                fa = 10 + (kc % 2)
                tt("dve", FS[fa][:], xt[:, kc, :], FS[0][:], ALU.mult, [("xt", kc), "F0"], ["F%d" % fa])
                ts("pool", hT[:, kc, :], FS[fa][:], dpl(kc, kc + 1), dpl(8 + kc, 9 + kc), ALU.mult, ALU.add,
                   ["F%d" % fa, ("dp", li)], [("hT", kc)])
                if kc % 2 == 1:
                    yield

            def proj(col0, M, b):
                for kc in range(8):
                    mm(ps[0:M, b, :], win[:, kc, col0:col0 + M], hT[:, kc, :], kc == 0, kc == 7,
                       [("hT", kc), "win"], [bank(b)])

            for which, col0 in (("q", C_DQ), ("k", C_DK)):
                b = nbA()
                proj(col0, 128, b)
                yield
                act(B0[:], ps[:, b, :], AF.Square, [bank(b)], ["B0"])
                b5 = nbA()
                mm(ps[:, b5, :], cb[:, CB_BLK:CB_BLK + 128], B0[:], True, True, ["B0", "cb"], [bank(b5)])
                yield
                rsqrt_mean(FS[4][:], ps[:, b5, :], [bank(b5)], "F4")
                if which == "q":
                    for mq in range(2):
                        rs = slice(64 * mq, 64 * mq + 64)
                        stt("dve", QT[rs, par, mq, :], ps[rs, b, :], dp[rs, li, 24:25], FS[4][rs, :], ALU.mult, ALU.mult,
                            [bank(b), "F4", ("dp", li)], [("QT", par)])
                else:
                    stt("dve", KT[:, t0:t0 + TQ], ps[:, b, :], ppl(18, 19), FS[4][:], ALU.mult, ALU.mult,
                        [bank(b), "F4", "pp"], [("KT", i)])
                yield
            for col0, dst, dname, fe in ((C_DZ, FZ[par], ("FZ", par), 5), (C_GZ, FS[3], "F3", 6)):
                b = nbA()
                proj(col0, 128, b)
                yield
                sigmoid_neg(FS[fe][:], ps[:, b, :], [bank(b)], "F%d" % fe)
                tt("dve", dst[:], ps[:, b, :], FS[fe][:], ALU.mult, [bank(b), "F%d" % fe], [dname])
                yield
            b = nbA()
            proj(C_GV, 128, b)
            cp("dve", C1[:, 3:3 + TQ], ps[:, b, :], [bank(b)], ["C1"])
            yield
            b = nbA()
            proj(C_GQK, 128, b)
            cp("dve", C0[:, 3:3 + TQ], ps[:, b, :], [bank(b)], ["C0"])
            yield
            b = nbA()
            proj(C_GLR, 16, b)
            cp("dve", G0[0:16, :], ps[0:16, b, :], [bank(b)], ["G0"])
            yield
            b = nbA()
            for j in range(4):
                for kc in range(8):
                    mm(ps[:, b, j * 128:(j + 1) * 128], hT[:, kc, j * 128:(j + 1) * 128], win[:, kc, C_DV:C_DV + 128],
                       kc == 0, kc == 7, [("hT", kc), "win"], [bank(b)])
                if j % 2 == 1:
                    yield
            for j in range(4):
                cp("dve", V[:, 4 * i + j, 0:128], ps[:, b, j * 128:(j + 1) * 128], [bank(b)], [("V", i)])
            yield

            for (Cb, cname, w0, fo, fe) in ((C0, "C0", 8, 8, 4), (C1, "C1", 12, 9, 5)):
                fon = "F%d" % fo
                ts("dve", FS[fo][:], Cb[:, 0:TQ], ppl(w0, w0 + 1), None, ALU.mult, None, [cname, "pp"], [fon])
                for j in range(1, 4):
                    stt("dve", FS[fo][:], Cb[:, j:j + TQ], ppl(w0 + j, w0 + j + 1), FS[fo][:], ALU.mult, ALU.add,
                        [cname, "pp", fon], [fon])
                cp("pool", Cb[:, 0:3], Cb[:, TQ:TQ + 3], [cname], [cname])
                yield
                sigmoid_neg(FS[fe][:], FS[fo][:], [fon], "F%d" % fe)
                tt("dve", FS[fo][:], FS[fo][:], FS[fe][:], ALU.mult, [fon, "F%d" % fe], [fon])
                yield
            sqk, sv = FS[8], FS[9]
            bg = nbA()
            for j in range(4):
                mm(ps[:, bg, j * 64:(j + 1) * 64], G0[0:17, j * 128:(j + 1) * 128], wgk[0:17, li, :], True, True,
                   ["G0", "wgk"], [bank(bg)])
            yield
            act(L0[:], ps[:, bg, 0:256], AF.Exp, [bank(bg)], ["L0"], scale=-1.0)
            act(L0[:], L0[:], AF.Ln, ["L0"], ["L0"], scale=1.0, bias=1.0)
            yield
            bd = nbA()
            for j in range(4):
                mm(ps[:, bd, j * 64:(j + 1) * 64], cf[:, CF_U:CF_U + 128], L0[:, j * 64:(j + 1) * 64], True, True,
                   ["cf", "L0"], [bank(bd)])
            yield
            act(L1[:], ps[:, bd, 0:256], AF.Exp, [bank(bd)], ["L1"])
            bk = nbA()
            for j in range(4):
                mm(ps[:, bk, j * 64:(j + 1) * 64], sqk[64:128, j * 128:(j + 1) * 128], cf[64:128, CF_ID + 64:CF_ID + 128],
                   True, True, ["F8", "cf"], [bank(bk)])
            yield
            tt("dve", kdec[:], ps[:, bk, 0:256], L1[:], ALU.mult, [bank(bk), "L1"], ["kdec"])
            bv = nbA()
            for j in range(4):
                mm(ps[:, bv, j * 128:(j + 1) * 128], sv[:, j * 128:(j + 1) * 128], cf[:, CF_ID:CF_ID + 128],
                   True, True, ["F9", "cf"], [bank(bv)])
            yield
            cp("dve", vtok[:], ps[:, bv, :], [bank(bv)], ["vtok"])
            bb = nbA()
            for j in range(4):
                mm(ps[0:64, bb, 2 * j:2 * j + 2], L0[:, j * 64:(j + 1) * 64], cf[:, CF_CIND:CF_CIND + 2], True, True,
                   ["L0", "cf"], [bank(bb)])
            yield
            act(a8[:], ps[0:64, bb, 0:8], AF.Exp, [bank(bb)], ["a8"])
            ubs = [nbA(), nbA()]
            for c in range(8):
                j, hh = c // 2, c % 2
                ub = ubs[c % 2]
                col = (c // 2) * 128
                mm(ps[0:64, ub, col:col + 128], kdec[64 * hh:64 * hh + 64, j * 64:(j + 1) * 64],
                   vtok[64 * hh:64 * hh + 64, j * 128:(j + 1) * 128], True, True, ["kdec", "vtok"], [bank(ub)])
            yield
            obk = nbA()
            for c in range(8):
                ub = ubs[c % 2]
                col = (c // 2) * 128
                pc = (c - 1) % 8
                stt("dve", Sring[:, c, :], Sring[:, pc, :], a8[:, c:c + 1], ps[0:64, ub, col:col + 128], ALU.mult, ALU.add,
                    [("S", pc), "a8", bank(ub)], [("S", c)])
                mm(ps[:, obk, c * 64:(c + 1) * 64], Sring[:, c, :], sqk[0:64, c * 64:(c + 1) * 64], True, True,
                   [("S", c), "F8"], [bank(obk)])
                if c % 2 == 1:
                    yield
            act(B0[:], ps[:, obk, :], AF.Square, [bank(obk)], ["B0"])
            bs = nbA()
            mm(ps[:, bs, :], cb[:, CB_G:CB_G + 128], B0[:], True, True, ["B0", "cb"], [bank(bs)])
            yield
            rsqrt_mean(FS[4][:], ps[:, bs, :], [bank(bs)], "F4")
            stt("dve", FS[5][:], ps[:, obk, :], dpl(26, 27), FS[4][:], ALU.mult, ALU.mult, [bank(obk), "F4", ("dp", li)], ["F5"])
            tt("pool", oh[:, par, 0, :], FS[5][:], FS[3][:], ALU.mult, ["F5", "F3"], [("oh", par, 0)])
            yield

        pslot = [0]

        def gen_B(i, li):
            dpl = lambda c0, c1: dp[:, li, c0:c1]
            par = i % 2
            nkt = 4 * (i + 1)
            LA = 2
            OB, SBK = 6, 7
            infos = [{}, {}]

            def emit_st(m, kt):
                diag = kt >= 4 * i
                q0 = 128 * (kt - 4 * i) if diag else 0
                b = nbB()
                mm(ps[:, b, q0:TQ], KT[:, kt * 128:(kt + 1) * 128],
                   QT[:, par, m, q0:TQ], True, True, [("KT", kt // 4), ("QT", par)], [bank(b)])
                infos[m][kt] = (b, q0, diag)

            OBX = (6, 7)

            def finalize_map(m):
                for qs in range(4):
                    bk, c0 = OBX[qs // 2], (qs % 2) * 129
                    SCH.op("dve", lambda h, bk=bk, c0=c0, qs=qs: h.reciprocal(out=rc[:, m, qs:qs + 1], in_=ps[:, bk, c0 + 128:c0 + 129]),
                           [bank(bk)], [("rc", m)])
                for qs in range(4):
                    bk, c0 = OBX[qs // 2], (qs % 2) * 129
                    ts("dve", AS[m][:, qs, :], ps[:, bk, c0:c0 + 128], rc[:, m, qs:qs + 1], None, ALU.mult, None,
                       [bank(bk), ("rc", m)], ["A%d" % m])

            for kt in range(min(LA, nkt)):
                emit_st(0, kt)
            for m in range(2):
                for kt in range(nkt):
                    b, q0, diag = infos[m].pop(kt)
                    sl = pslot[0]
                    pslot[0] = (sl + 1) % NPT
                    if not diag:
                        act(Pt[:, sl, q0:TQ], ps[:, b, q0:TQ], AF.Exp, [bank(b)], [("Pt", sl)])
                    else:
                        act(Pt[0:64, sl, q0:TQ], ps[0:64, b, q0:TQ], AF.Exp, [bank(b)], [("Pt", sl)])
                        if q0 + 64 < TQ:
                            act(Pt[64:128, sl, q0 + 64:TQ], ps[64:128, b, q0 + 64:TQ], AF.Exp, [bank(b)], [("Pt", sl)])
                        cp("act", Pt[64:128, sl, q0:q0 + 64], cb[64:128, CB_BLK:CB_BLK + 64], ["cb"], [("Pt", sl)])
                    for qs in range(q0 // 128, 4):
                        bk, c0 = OBX[qs // 2], (qs % 2) * 129
                        mm(ps[:, bk, c0:c0 + 129], Pt[:, sl, qs * 128:(qs + 1) * 128], V[:, kt, :],
                           (kt == 0 and qs % 2 == 0), False, [("V", kt // 4), "Vones", ("Pt", sl)], [bank(bk)])
                    if kt + LA < nkt:
                        emit_st(m, kt + LA)
                    elif m == 0 and (kt + LA - nkt) < min(LA, nkt):
                        emit_st(1, kt + LA - nkt)
                    yield
                finalize_map(m)
                yield
            stt("dve", AS[0][:], AS[1][:], dpl(27, 28), AS[0][:], ALU.mult, ALU.add, ["A1", "A0", ("dp", li)], ["A0"])
            tt("dve", AS[2][:], AS[0][:], AS[0][:], ALU.mult, ["A0"], ["A2"])
            SCH.op("dve", lambda h: h.tensor_reduce(out=ssq[:, 0:4], in_=AS[2][:], axis=mybir.AxisListType.X, op=ALU.add),
                   ["A2"], ["ssq"])
            yield
            act(ssq[:, 4:8], ssq[:, 0:4], AF.Ln, ["ssq"], ["ssq2"], scale=1.0 / 128.0, bias=EPS)
            act(ssq[:, 4:8], ssq[:, 4:8], AF.Exp, ["ssq2"], ["ssq2"], scale=-0.5)
            for qs in range(4):
                ts("dve", ON[:, qs, :], AS[0][:, qs, :], ssq[:, 4 + qs:5 + qs], None, ALU.mult, None, ["A0", "ssq2"], ["ON"])
            bs = nbB()
            for qs in range(4):
                mm(ps[:, bs, qs * 128:(qs + 1) * 128], ON[:, qs, :], cb[:, CB_ID:CB_ID + 128], True, True,
                   ["ON", "cb"], [bank(bs)])
            yield
            stt("dve", oh[:, par, 1, :], ps[:, bs, :], dpl(25, 26), FZ[par][:], ALU.mult, ALU.mult,
                [bank(bs), ("dp", li), ("FZ", par)], [("oh", par, 1)])
            tc0 = (i % TPC) * TQ
            SCH.dma("pool", lambda h: h.dma_start(
                out=ohT[i // TPC].rearrange("(g p) s -> p g s", p=128)[:, :, tc0:tc0 + TQ], in_=oh[:, par, :, :]),
                ("ohs", par), reads=[("oh", par, 0), ("oh", par, 1)], writes=[("ohT", i)])
            if fused and (i % TPC) == TPC - 1:
                c = i // TPC
                k = li % 2
                SCH.dma("pool", lambda h, k=k, c=c: h.collective_compute(
                    "AllGather", ALU.bypass, replica_groups=[[0, 1, 2, 3], [4, 5, 6, 7]],
                    ins=[ohT[c]], outs=[oT_fulls[k][c]]),
                    ("cc", li, c), reads=[("ohT", ii) for ii in range(c * TPC, (c + 1) * TPC)],
                    writes=[(("ofull", k), ii) for ii in range(c * TPC, (c + 1) * TPC)], inc=None)
            yield

        NA_EST = 62

        def run_layer_tiles(li, x_src, x_src_name, prev_args):
            for _ in gen_A(0, li, x_src, x_src_name, prev_args):
                pass
            for i in range(NT):
                gB = gen_B(i, li)
                gA = gen_A(i + 1, li, x_src, x_src_name, prev_args) if i + 1 < NT else None
                nB = 8 * (i + 1) + 4
                ratio = (NA_EST + 2.0) / nB
                acc = 0.0
                for _ in gB:
                    if gA is None:
                        continue
                    acc += ratio
                    while acc >= 1.0 and gA is not None:
                        acc -= 1.0
                        try:
                            next(gA)
                        except StopIteration:
                            gA = None
                if gA is not None:
                    for _ in gA:
                        pass

        def run_final_tiles(x_src, x_src_name, prev_args):
            for i in range(NT):
                emit_load_x(i, x_src, x_src_name)
                for _ in gen_outproj(i, *prev_args):
                    pass

        def emit_layer_params(li, a_idx, lam_init):
            ppl = lambda c0, c1: pp[:, li, c0:c1]
            dpl = lambda c0, c1: dp[:, li, c0:c1]
            r = [("dp", li)]
            stt("dve", dpl(0, 8), modT[:, a_idx, 8:16], 1.0, ppl(0, 8), ALU.add, ALU.mult, [("modT", a_idx), "pp"], r)
            cp("dve", dpl(8, 16), modT[:, a_idx, 0:8], [("modT", a_idx)], r)
            ts("dve", dpl(24, 25), ppl(17, 18), 0.125, None, ALU.mult, None, ["pp"], r)
            ts("dve", dpl(25, 26), ppl(19, 20), 1.0 - lam_init, None, ALU.mult, None, ["pp"], r)
            ts("dve", dpl(26, 27), ppl(16, 17), 0.125, None, ALU.mult, None, ["pp"], r)
            tt("dve", lamt[:, li, 0:1], ppl(20, 21), ppl(21, 22), ALU.mult, ["pp"], [("lamt", li)])
            tt("dve", lamt[:, li, 1:2], ppl(22, 23), ppl(23, 24), ALU.mult, ["pp"], [("lamt", li)])
            mm(ps[:, 7, 32:34], cf[:, CF_ONE:CF_ONE + 128], lamt[:, li, 0:2], True, True, [("lamt", li), "cf"], [bank(7)])
            act(lamt[:, li, 2:4], ps[:, 7, 32:34], AF.Exp, [bank(7)], [("lamt2", li)])
            tt("dve", dpl(27, 28), lamt[:, li, 3:4], lamt[:, li, 2:3], ALU.subtract, [("lamt2", li)], r)
            ts("dve", dpl(27, 28), dpl(27, 28), -lam_init, None, ALU.add, None, r, r)

        if fused:
            ada_of = {l: l for l in range(DEPTH)}
            for a in range(DEPTH):
                emit_ada(a)
        else:
            ada_list = []
            if has_prev or n_mix == 0:
                ada_list.append(layer_ids[0] - 1 if n_mix > 0 else layer_ids[0])
            for l in layer_ids[:n_mix]:
                if l not in ada_list:
                    ada_list.append(l)
            ada_of = {l: k for k, l in enumerate(ada_list)}
            assert len(ada_list) <= n_ada, (ada_list, n_ada)
            for k in range(len(ada_list)):
                emit_ada(k)

        out_idx = 0
        x_src, x_src_name = xT_in, "xin"
        for li in range(n_mix):
            SCH.ep = li + 1
            l = layer_ids[li]
            lam_init = 0.8 - 0.6 * math.exp(-0.3 * l)
            prev = (li > 0) or has_prev
            if prev:
                load_weight(wout, wout_in[out_idx], D, "wout")
                cp("dve", gates[:, out_idx, :], modT[:, ada_of[l - 1], 16:24], [("modT", ada_of[l - 1])],
                   [("gates", out_idx)])
                if li == 0:
                    o_src, o_name = oT_in, "oin"
                else:
                    o_src, o_name = oT_fulls[(li - 1) % 2], ("ofull", (li - 1) % 2)
            load_weight(win, win_in[li], NCOL, "win")
            emit_layer_params(li, ada_of[l], lam_init)
            SCH.op("dve", lambda h: h.memset(Sring[:, 7, :], 0.0), writes=[("S", 7)])
            SCH.op("dve", lambda h: h.memset(C0[:, 0:3], 0.0), writes=["C0"])
            SCH.op("dve", lambda h: h.memset(C1[:, 0:3], 0.0), writes=["C1"])
            prev_args = (o_src, o_name, out_idx, xw, "xw") if prev else None
            run_layer_tiles(li, x_src, x_src_name, prev_args)
            if prev:
                out_idx += 1
                x_src, x_src_name = xw, "xw"
        if do_final:
            SCH.ep = n_mix + 1
            l_last = layer_ids[n_mix - 1] if n_mix > 0 else layer_ids[0]
            load_weight(wout, wout_in[out_idx], D, "wout")
            cp("dve", gates[:, out_idx, :], modT[:, ada_of[l_last], 16:24], [("modT", ada_of[l_last])],
               [("gates", out_idx)])
            if n_mix == 0:
                o_src, o_name = oT_in, "oin"
            else:
                o_src, o_name = oT_fulls[(n_mix - 1) % 2], ("ofull", (n_mix - 1) % 2)
            run_final_tiles(x_src, x_src_name, (o_src, o_name, out_idx, yT, "yT"))
        SCH.emit()
    return nc


def _consts():
    cf = np.zeros((128, CF_W), np.float32)
    cf[:, CF_ID:CF_ID + 128] = np.eye(128, dtype=np.float32)
    s = np.arange(128)[:, None]
    t = np.arange(128)[None, :]
    cf[:, CF_U:CF_U + 128] = np.where((s > t) & (s // 64 == t // 64), -1.0 / 16.0, 0.0)
    cf[:, CF_ONE:CF_ONE + 128] = 1.0
    cf[:, CF_CIND:CF_CIND + 2] = np.where(s // 64 == np.arange(2)[None, :], -1.0 / 16.0, 0.0)
    cbm = np.zeros((128, CB_W), np.float32)
    cbm[:, CB_X:CB_X + 128] = 1.0 / 1024.0
    cbm[:, CB_BLK:CB_BLK + 128] = np.where(s // 64 == t // 64, 1.0 / 64.0, 0.0)
    cbm[:, CB_D:CB_D + 128] = 1.0 / 128.0
    cbm[:, CB_G:CB_G + 128] = 1.0 / 8192.0
    cbm[:, CB_ONE:CB_ONE + 128] = 1.0
    cbm[:, CB_ID:CB_ID + 128] = np.eye(128, dtype=np.float32)
    return cf, cbm.astype(ml_dtypes.bfloat16)


def _head_cols(h):
    gq = np.arange(0, 64) + h * 64
    gk = 256 + np.arange(0, 64) + h * 64
    gv = 512 + np.arange(0, 128) + h * 128
    glr = 1024 + np.arange(16)
    gz = 1040 + np.arange(128) + h * 128
    dq = 1552 + np.arange(128) + h * 128
    dk = 2064 + np.arange(128) + h * 128
    dv = 2576 + np.arange(128) + h * 128
    dz = 3088 + np.arange(128) + h * 128
    return np.concatenate([dq, dk, dz, gz, gv, gq, gk, dv, glr])


def _pp(inp, l, h):
    p = np.zeros((128, NPP), np.float32)
    p[:, 0:8] = inp["norm_g"][l].reshape(8, 128).T
    cw = inp["conv_w"][l]
    ch_qk = np.concatenate([np.arange(64) + h * 64, 256 + np.arange(64) + h * 64])
    p[:, 8:12] = cw[:, ch_qk].T
    p[:, 12:16] = cw[:, 512 + h * 128 + np.arange(128)].T
    p[:, 16] = inp["gla_norm_g"][l]
    p[:, 17] = np.tile(inp["qn_g"][l], 2)
    p[:, 18] = np.tile(inp["kn_g"][l], 2)
    p[:, 19] = inp["diff_norm_g"][l]
    p[0:64, 20] = inp["lam_q1"][l]
    p[0:64, 21] = inp["lam_k1"][l]
    p[0:64, 22] = inp["lam_q2"][l]
    p[0:64, 23] = inp["lam_k2"][l]
    return p


def _wgk(inp, l, h):
    w = np.zeros((17, 64), np.float32)
    w[0:16] = inp["w_gk"][l][:, h * 64:(h + 1) * 64]
    w[16] = inp["b_gk"][l][h * 64:(h + 1) * 64]
    return w


def _wout_perm(inp, l):
    rows = np.concatenate([np.concatenate([np.arange(128) + h * 128, 512 + np.arange(128) + h * 128]) for h in range(4)])
    return np.ascontiguousarray(inp["w_out"][l][rows, :])


_PROG_CACHE = {}


def _get_prog(S, n_mix, has_prev, do_final, layer_ids, fused=False):
    key = (S, n_mix, has_prev, do_final, tuple(layer_ids), fused)
    if key not in _PROG_CACHE:
        _PROG_CACHE[key] = build_program(S, n_mix, has_prev, do_final, layer_ids, fused)
    return _PROG_CACHE[key]


def run_unfused(inp, S):
    cf, cbm = _consts()
    B = inp["x"].shape[0]
    xT = [np.ascontiguousarray(inp["x"][b, :S].T) for b in range(B)]
    cT = [np.ascontiguousarray(inp["c"][b].reshape(8, 128).T) for b in range(B)]
    oT = None
    for l in range(DEPTH):
        has_prev = l > 0
        nc = _get_prog(S, 1, has_prev, False, [l])
        in_maps = []
        for core in range(8):
            b, h = core // 4, core % 4
            m = {"xT": xT[b], "cT": cT[b], "cf": cf, "cb": cbm,
                 "w_in": np.ascontiguousarray(inp["w_in"][l][:, _head_cols(h)])[None],
                 "pp": _pp(inp, l, h)[None], "wgk": _wgk(inp, l, h)[None]}
            if has_prev:
                m["w_ada"] = np.ascontiguousarray(inp["w_ada"][l - 1:l + 1])
                m["b_adaT"] = np.stack([inp["b_ada"][k].reshape(24, 128).T for k in (l - 1, l)])
                m["w_out"] = _wout_perm(inp, l - 1)[None]
                m["oT_in"] = oT[b][None]
            else:
                m["w_ada"] = np.ascontiguousarray(inp["w_ada"][l:l + 1])
                m["b_adaT"] = np.stack([inp["b_ada"][l].reshape(24, 128).T])
            in_maps.append(m)
        res = run_bass_kernel_spmd(nc, in_maps, core_ids=list(range(8))).results
        oT = [np.concatenate([res[b * 4 + h]["ohT"][0] for h in range(4)], axis=0) for b in range(B)]
        if has_prev:
            xT = [res[b * 4]["xT_out"] for b in range(B)]
    nc = _get_prog(S, 0, True, True, [DEPTH - 1])
    in_maps = []
    for core in range(8):
        b = core // 4
        in_maps.append({"xT": xT[b], "cT": cT[b], "cf": cf, "cb": cbm,
                        "w_ada": np.ascontiguousarray(inp["w_ada"][DEPTH - 1:DEPTH]),
                        "b_adaT": np.stack([inp["b_ada"][DEPTH - 1].reshape(24, 128).T]),
                        "w_out": _wout_perm(inp, DEPTH - 1)[None], "oT_in": oT[b][None]})
    res = run_bass_kernel_spmd(nc, in_maps, core_ids=list(range(8))).results
    out = np.stack([res[b * 4]["yT"].T for b in range(B)])
    return np.ascontiguousarray(out)


def fused_in_maps(inp, S):
    cf, cbm = _consts()
    B = inp["x"].shape[0]
    xT = [np.ascontiguousarray(inp["x"][b, :S].T) for b in range(B)]
    cT = [np.ascontiguousarray(inp["c"][b].reshape(8, 128).T) for b in range(B)]
    w_ada = np.ascontiguousarray(inp["w_ada"])
    b_adaT = np.stack([inp["b_ada"][l].reshape(24, 128).T for l in range(DEPTH)])
    w_out = np.stack([_wout_perm(inp, l) for l in range(DEPTH)])
    in_maps = []
    for core in range(8):
        b, h = core // 4, core % 4
        cols = _head_cols(h)
        in_maps.append({
            "xT": xT[b], "cT": cT[b], "cf": cf, "cb": cbm, "w_ada": w_ada, "b_adaT": b_adaT,
            "w_in": np.stack([inp["w_in"][l][:, cols] for l in range(DEPTH)]),
            "pp": np.stack([_pp(inp, l, h) for l in range(DEPTH)]),
            "wgk": np.stack([_wgk(inp, l, h) for l in range(DEPTH)]),
            "w_out": w_out})
    return in_maps


def run_fused(inp, S):
    B = inp["x"].shape[0]
    nc = _get_prog(S, DEPTH, False, True, list(range(DEPTH)), True)
    res = run_bass_kernel_spmd(nc, fused_in_maps(inp, S), core_ids=list(range(8))).results
    return np.ascontiguousarray(np.stack([res[b * 4]["yT"].T for b in range(B)]))


def kernel(**inputs):
    inp = {k: np.asarray(v) for k, v in inputs.items()}
    return run_fused(inp, inp["x"].shape[1])
```

```python
import contextlib
import math

import ml_dtypes
import numpy as np

import concourse.bass as bass
import concourse.mybir as mybir
from concourse.bass_utils import run_bass_kernel_spmd

F32 = mybir.dt.float32
BF16 = mybir.dt.bfloat16
AF = mybir.ActivationFunctionType
ALU = mybir.AluOpType

D = 1024
TQ = 512
NCOL = 912
EPS = 1e-6
DEPTH = 4
C_DQ, C_DK, C_DZ, C_GZ, C_GV, C_GQK, C_DV, C_GLR = 0, 128, 256, 384, 512, 640, 768, 896
CF_ID, CF_U, CF_ONE, CF_CIND = 0, 128, 256, 384
CF_W = 386
CB_X, CB_BLK, CB_D, CB_G, CB_ONE, CB_ID = 0, 128, 256, 384, 512, 640
CB_W = 768
NPP = 24
DBG = set()


class _Ins:
    __slots__ = ("eng", "fn", "deps", "idx", "is_dma", "sig", "tick", "key", "cum", "waits", "ep", "inc")

    def __init__(self, eng, fn, is_dma=False, key=None):
        self.eng = eng
        self.fn = fn
        self.deps = []
        self.idx = -1
        self.is_dma = is_dma
        self.sig = False
        self.tick = 0
        self.key = key
        self.cum = 0
        self.waits = []
        self.ep = 0
        self.inc = 16


class Sched:
    ENGS = ("sp", "pe", "act", "dve", "pool")

    def __init__(self, nc):
        self.nc = nc
        self.q = {e: [] for e in self.ENGS}
        self.res = {}
        self.dma_keys = {}
        self.ep = 0

    def _track(self, ins, reads, writes):
        deps = {}
        for r in reads:
            st = self.res.get(r)
            if st is not None and st[0] is not None:
                deps[id(st[0])] = (st[0], True)
        for w in writes:
            st = self.res.get(w)
            if st is not None:
                if st[0] is not None and id(st[0]) not in deps:
                    deps[id(st[0])] = (st[0], False)
                for rd in st[1]:
                    if id(rd) not in deps:
                        deps[id(rd)] = (rd, False)
        for r in reads:
            st = self.res.setdefault(r, [None, []])
            st[1].append(ins)
        for w in writes:
            self.res[w] = [ins, []]
        deps.pop(id(ins), None)
        ins.deps = list(deps.values())

    def op(self, eng, fn, reads=(), writes=()):
        ins = _Ins(eng, fn)
        ins.idx = len(self.q[eng])
        ins.ep = self.ep
        self._track(ins, reads, writes)
        self.q[eng].append(ins)
        return ins

    def dma(self, queue, fn, key, reads=(), writes=(), inc=16):
        ins = _Ins(queue, fn, is_dma=True, key=key)
        ins.idx = len(self.q[queue])
        ins.ep = self.ep
        ins.inc = inc
        self._track(ins, reads, writes)
        self.dma_keys[key] = self.dma_keys.get(key, 0) + (inc if inc else 1)
        ins.cum = self.dma_keys[key]
        self.q[queue].append(ins)
        return ins

    def emit(self):
        nc = self.nc
        for e in self.ENGS:
            for ins in self.q[e]:
                need = []
                best = {}
                bestk = {}
                for (d, raw) in ins.deps:
                    if d.is_dma:
                        bk = bestk.get(d.key)
                        if bk is None or d.cum > bk.cum:
                            bestk[d.key] = d
                        continue
                    if d.eng == ins.eng and not ins.is_dma:
                        if e == "pe" or not raw:
                            continue
                    b = best.get(d.eng)
                    if b is None or d.idx > b.idx:
                        best[d.eng] = d
                for d in bestk.values():
                    need.append(d)
                for d in best.values():
                    need.append(d)
                ins.waits = need
        for e in self.ENGS:
            seen = {}
            for ins in self.q[e]:
                keep = []
                for d in ins.waits:
                    if d.is_dma:
                        k = ("k", d.key)
                        if seen.get(k, -1) >= d.cum:
                            continue
                        seen[k] = d.cum
                    else:
                        k = ("e", d.eng)
                        if seen.get(k, -1) >= d.idx:
                            continue
                        seen[k] = d.idx
                        d.sig = True
                    keep.append(d)
                ins.waits = keep
        eps = set()
        for e in self.ENGS:
            t = {}
            for ins in self.q[e]:
                if ins.sig and not ins.is_dma:
                    t[ins.ep] = t.get(ins.ep, 0) + 1
                    ins.tick = t[ins.ep]
                    eps.add((e, ins.ep))
        self.max_ticks = {}
        for e in self.ENGS:
            for ins in self.q[e]:
                if ins.tick:
                    self.max_ticks[(e, ins.ep)] = max(self.max_ticks.get((e, ins.ep), 0), ins.tick)
        self.n_ins = {e: len(self.q[e]) for e in self.ENGS}
        stack = contextlib.ExitStack()
        esem = {k: stack.enter_context(nc.semaphore("s_%s_%d" % k)) for k in sorted(eps)}
        ksem = {}
        for n, k in enumerate(self.dma_keys):
            ksem[k] = stack.enter_context(nc.semaphore("k%d" % n))
        q = self.q
        dma_keys = self.dma_keys

        def run(e, h):
            for ins in q[e]:
                for d in ins.waits:
                    if d.is_dma:
                        h.wait_ge(ksem[d.key], d.cum)
                    else:
                        h.wait_ge(esem[(d.eng, d.ep)], d.tick)
                bi = ins.fn(h)
                if ins.is_dma:
                    if ins.inc:
                        bi.then_inc(ksem[ins.key], ins.inc)
                    else:
                        bi.then_inc(ksem[ins.key])
                elif ins.sig:
                    bi.then_inc(esem[(e, ins.ep)], 1)
            if e == "sp":
                for k, v in dma_keys.items():
                    h.wait_ge(ksem[k], v)

        with stack:
            with nc.Block() as block:
                @block.sync
                def _(h):
                    run("sp", h)

                @block.tensor
                def _(h):
                    run("pe", h)

                @block.scalar
                def _(h):
                    run("act", h)

                @block.vector
                def _(h):
                    run("dve", h)

                @block.gpsimd
                def _(h):
                    run("pool", h)


def build_program(S, n_mix, has_prev, do_final, layer_ids, fused=False):
    assert S % TQ == 0
    NT = S // TQ
    n_out = (1 if has_prev else 0) + max(n_mix - 1, 0) + (1 if (do_final and n_mix > 0) else 0)
    if n_mix == 0:
        n_out = 1
    n_ada = n_out + n_mix if not fused else DEPTH
    nc = bass.Bass("TRN2", target_bir_lowering=False)
    dt_in = lambda n, s, d: nc.dram_tensor(n, s, d, kind="ExternalInput").ap()
    dt_out = lambda n, s, d: nc.dram_tensor(n, s, d, kind="ExternalOutput").ap()
    dt_int = lambda n, s, d: nc.dram_tensor(n, s, d, kind="Internal").ap()

    xT_in = dt_in("xT", [D, S], F32)
    cT_in = dt_in("cT", [128, 8], F32)
    cf_in = dt_in("cf", [128, CF_W], F32)
    cb_in = dt_in("cb", [128, CB_W], BF16)
    wada_in = dt_in("w_ada", [n_ada, D, 3 * D], F32)
    bada_in = dt_in("b_adaT", [n_ada, 128, 24], F32)
    if n_mix > 0:
        win_in = dt_in("w_in", [n_mix, D, NCOL], F32)
        pp_in = dt_in("pp", [n_mix, 128, NPP], F32)
        wgk_in = dt_in("wgk", [n_mix, 17, 64], F32)
    if n_out > 0:
        wout_in = dt_in("w_out", [n_out, D, D], F32)
    if has_prev or n_mix == 0:
        oT_in = dt_in("oT_in", [1, D, S], BF16)
    CW = min(2048, S) if fused else S
    NCH = S // CW
    TPC = CW // TQ
    if n_mix > 0:
        if fused:
            ohT = dt_int("ohT", [NCH, 256, CW], BF16)
            oT_fulls = [dt_int("oT_full%d" % k, [NCH, D, CW], BF16) for k in range(2)]
        else:
            ohT = dt_out("ohT", [NCH, 256, CW], BF16)
    if do_final:
        yT = dt_out("yT", [D, S], F32)
    xw = None
    if n_mix > 0 and n_out - (1 if do_final else 0) > 0:
        xw = dt_int("xw", [D, S], F32) if (fused or do_final) else dt_out("xT_out", [D, S], F32)

    def tview(ap, i):
        return ap.rearrange("(kc p) s -> p kc s", p=128)[:, :, i * TQ:(i + 1) * TQ]

    with contextlib.ExitStack() as st:
        def sb(n, s, d):
            return st.enter_context(nc.sbuf_tensor("sb_" + n, s, d))

        xt = sb("xt", [128, 8, TQ], F32)
        ot = sb("ot", [128, 8, TQ], BF16)
        hT = sb("hT", [128, 8, TQ], BF16)
        sqs = sb("sqs", [128, 2, TQ], BF16)
        wout = sb("wout", [128, 8, D], BF16)
        wstage = sb("wstage", [128, 2, D], F32)
        cf = sb("cf", [128, CF_W], F32)
        cb = sb("cb", [128, CB_W], BF16)
        cT = sb("cT", [128, 8], F32)
        cact = sb("cact", [128, 8], F32)
        ctmp = sb("ctmp", [128, 8], F32)
        modT = sb("modT", [128, n_ada, 24], F32)
        badaT = sb("badaT", [128, n_ada, 24], F32)
        FS = {k: sb("F%d" % k, [128, TQ], F32) for k in (0, 3, 4, 5, 6, 8, 9, 10, 11)}
        if n_mix > 0:
            win = sb("win", [128, 8, NCOL], BF16)
            KT = sb("KT", [128, S], BF16)
            V = sb("V", [128, S // 128, 129], BF16)
            QT = sb("QT", [128, 2, 2, TQ], BF16)
            FZ = [sb("FZ%d" % k, [128, TQ], F32) for k in range(2)]
            AS = [sb("AS%d" % k, [128, 4, 128], F32) for k in range(3)]
            ON = sb("ON", [128, 4, 128], BF16)
            rc = sb("rc", [128, 2, 4], F32)
            ssq = sb("ssq", [128, 8], F32)
            NPT = 6
            Pt = sb("Pt", [128, NPT, TQ], BF16)
            C0 = sb("C0", [128, TQ + 3], F32)
            C1 = sb("C1", [128, TQ + 3], F32)
            G0 = sb("G0", [32, TQ], F32)
            B0 = sb("B0", [128, TQ], BF16)
            L0 = sb("L0", [128, 256], F32)
            L1 = sb("L1", [128, 256], F32)
            kdec = sb("kdec", [128, 256], BF16)
            vtok = sb("vtok", [128, TQ], BF16)
            a8 = sb("a8", [64, 8], F32)
            Sring = sb("Sring", [64, 8, 128], F32)
            oh = sb("oh", [128, 2, 2, TQ], BF16)
            pp = sb("pp", [128, n_mix, NPP], F32)
            dp = sb("dp", [128, n_mix, 32], F32)
            wgk = sb("wgk", [32, n_mix, 64], F32)
            lamt = sb("lamt", [128, n_mix, 4], F32)
        gates = sb("gates", [128, max(n_out, 1), 8], F32)
        ps = st.enter_context(nc.psum_tensor("ps", [128, 8, TQ], F32))

        SCH = Sched(nc)
        ring = [0]

        def nb():
            ring[0] = (ring[0] + 1) % 4
            return ring[0]

        ringA = [0]
        ringB = [0]

        def nbA():
            ringA[0] = (ringA[0] + 1) % 3
            return ringA[0]

        def nbB():
            ringB[0] = (ringB[0] + 1) % 3
            return 3 + ringB[0]

        def bank(b):
            return ("bank", b)

        def mm(out, lhsT, rhs, start, stop, reads, writes):
            SCH.op("pe", lambda h: h.matmul(out, lhsT=lhsT, rhs=rhs, start=start, stop=stop,
                                            skip_group_check=True), reads, writes)

        def act(out, in_, func, reads, writes, scale=1.0, bias=0.0):
            SCH.op("act", lambda h: h.activation(out=out, in_=in_, func=func, bias=bias, scale=scale),
                   reads, writes)

        def tt(eng, out, in0, in1, op, reads, writes):
            SCH.op(eng, lambda h: h.tensor_tensor(out=out, in0=in0, in1=in1, op=op), reads, writes)

        def ts(eng, out, in0, s1, s2, op0, op1, reads, writes):
            if s2 is None:
                SCH.op(eng, lambda h: h.tensor_scalar(out=out, in0=in0, scalar1=s1, scalar2=None, op0=op0),
                       reads, writes)
            else:
                SCH.op(eng, lambda h: h.tensor_scalar(out=out, in0=in0, scalar1=s1, scalar2=s2, op0=op0, op1=op1),
                       reads, writes)

        def stt(eng, out, in0, scalar, in1, op0, op1, reads, writes):
            SCH.op(eng, lambda h: h.scalar_tensor_tensor(out=out, in0=in0, scalar=scalar, in1=in1, op0=op0, op1=op1),
                   reads, writes)

        def cp(eng, out, in_, reads, writes):
            if eng == "act":
                SCH.op("act", lambda h: h.copy(out=out, in_=in_), reads, writes)
            else:
                SCH.op(eng, lambda h: h.tensor_copy(out=out, in_=in_), reads, writes)

        def rsqrt_mean(dst, src_ps, reads, writes_name):
            act(dst, src_ps, AF.Ln, reads, [writes_name], scale=1.0, bias=EPS)
            act(dst, dst, AF.Exp, [writes_name], [writes_name], scale=-0.5)

        def sigmoid_neg(dst, src, reads, name):
            act(dst, src, AF.Exp, reads, [name], scale=-1.0)
            act(dst, dst, AF.Ln, [name], [name], scale=1.0, bias=1.0)
            act(dst, dst, AF.Exp, [name], [name], scale=-1.0)

        SCH.dma("sp", lambda h: h.dma_start(out=cf[:], in_=cf_in), "c_cf", writes=["cf"])
        SCH.dma("sp", lambda h: h.dma_start(out=cb[:], in_=cb_in), "c_cb", writes=["cb"])
        SCH.dma("sp", lambda h: h.dma_start(out=cT[:], in_=cT_in), "c_cT", writes=["cT"])
        SCH.dma("sp", lambda h: h.dma_start(out=badaT[:], in_=bada_in.rearrange("a p c -> p a c")), "c_bada",
                writes=["badaT"])
        if n_mix > 0:
            SCH.dma("sp", lambda h: h.dma_start(out=pp[:], in_=pp_in.rearrange("l p c -> p l c")), "c_pp", writes=["pp"])
            SCH.op("dve", lambda h: h.memset(wgk[:], 0.0), writes=["wgk"])
            SCH.dma("sp", lambda h: h.dma_start(out=wgk[0:17, :, :], in_=wgk_in.rearrange("l k c -> k l c")), "c_wgk",
                    reads=[], writes=["wgk"])
            SCH.op("dve", lambda h: h.memset(G0[:], 1.0), writes=["G0"])
            for par_ in range(2):
                SCH.op("pool", lambda h, par_=par_: h.memset(QT[:, par_, :, :], 0.0), writes=[("QT", par_)])
            SCH.op("pool", lambda h: h.memset(V[:, :, 128:129], 1.0), writes=["Vones"])
        sigmoid_neg(ctmp[:], cT[:], ["cT"], "ctmp")
        tt("dve", cact[:], cT[:], ctmp[:], ALU.mult, ["cT", "ctmp"], ["cact"])

        def emit_ada(a):
            for blk in range(6):
                SCH.dma("sp", lambda h, blk=blk: h.dma_start(
                    out=xt[:], in_=wada_in[a].rearrange("(kc p) n -> p kc n", p=128)[:, :, blk * 512:(blk + 1) * 512]),
                    "xl", writes=[("xt", dc) for dc in range(8)])
                for cc in range(4):
                    col = blk * 4 + cc
                    for kc in range(8):
                        mm(ps[:, 7, col:col + 1], xt[:, kc, cc * 128:(cc + 1) * 128], cact[:, kc:kc + 1],
                           kc == 0, kc == 7, [("xt", kc), "cact"], [bank(7)])
            tt("dve", modT[:, a, :], ps[:, 7, 0:24], badaT[:, a, :], ALU.add, [bank(7), "badaT"], [("modT", a)])

        astage = sb("astage", [128, 1, 8, 128], F32)

        def gen_ada_bg(a, cc):
            slot = 0
            SCH.dma("sp", lambda h: h.dma_start(
                out=astage[:, slot, :, :], in_=wada_in[a].rearrange("(kc p) n -> p kc n", p=128)[:, :, cc * 128:(cc + 1) * 128]),
                ("ast", slot), writes=[("astage", slot)])
            yield
            b = nbA()
            for kc in range(8):
                mm(ps[:, b, 0:1], astage[:, slot, kc, :], cact[:, kc:kc + 1], kc == 0, kc == 7,
                   [("astage", slot), "cact"], [bank(b)])
            tt("dve", modT[:, a, cc:cc + 1], ps[:, b, 0:1], badaT[:, a, cc:cc + 1], ALU.add, [bank(b), "badaT"], [("modT", a)])
            yield

        def load_weight(dst, src2d, ncols, resname):
            for kc in range(8):
                slot = kc % 2
                SCH.dma("sp", lambda h, kc=kc, slot=slot: h.dma_start(
                    out=wstage[:, slot, 0:ncols], in_=src2d[kc * 128:(kc + 1) * 128, :]),
                    ("wst", slot), writes=[("wstage", slot)])
                cp("pool", dst[:, kc, :], wstage[:, slot, 0:ncols], [("wstage", slot)], [resname])

        def emit_load_x(i, src, src_name):
            SCH.dma("sp", lambda h: h.dma_start(out=xt[:], in_=tview(src, i)), "xl",
                    reads=[(src_name, i)], writes=[("xt", dc) for dc in range(8)])

        def gen_outproj(i, o_src, o_name, gate_idx, dst, dst_name):
            SCH.dma("sp", lambda h: h.dma_start(out=ot[:], in_=tview(o_src[i // TPC], i % TPC)), "ol",
                    reads=[(o_name, i)], writes=["ot"])
            yield
            for dc in range(8):
                b = nbA()
                for kc in range(8):
                    mm(ps[:, b, :], wout[:, kc, dc * 128:(dc + 1) * 128], ot[:, kc, :], kc == 0, kc == 7,
                       ["ot", "wout"], [bank(b)])
                stt("dve", xt[:, dc, :], ps[:, b, :], gates[:, gate_idx, dc:dc + 1], xt[:, dc, :], ALU.mult, ALU.add,
                    [bank(b), ("xt", dc), ("gates", gate_idx)], [("xt", dc)])
                yield
            SCH.dma("pool", lambda h: h.dma_start(out=tview(dst, i), in_=xt[:]), "xs",
                    reads=[("xt", dc) for dc in range(8)], writes=[(dst_name, i)])

        def gen_A(i, li, x_src, x_src_name, prev_args, extra=None):
            ppl = lambda c0, c1: pp[:, li, c0:c1]
            dpl = lambda c0, c1: dp[:, li, c0:c1]
            t0 = i * TQ
            par = i % 2
            emit_load_x(i, x_src, x_src_name)
            if prev_args is not None:
                yield from gen_outproj(i, *prev_args)
            bx = nbA()
            for kc in range(8):
                sl = kc % 2
                tt("pool", sqs[:, sl, :], xt[:, kc, :], xt[:, kc, :], ALU.mult, [("xt", kc)], [("sqs", sl)])
                mm(ps[:, bx, :], cb[:, CB_X:CB_X + 128], sqs[:, sl, :], kc == 0, kc == 7, [("sqs", sl), "cb"], [bank(bx)])
                if kc % 2 == 1:
                    yield
            rsqrt_mean(FS[0][:], ps[:, bx, :], [bank(bx)], "F0")
            yield
            for kc in range(8):
                fa = 10 + (kc % 2)
                tt("dve", FS[fa][:], xt[:, kc, :], FS[0][:], ALU.mult, [("xt", kc), "F0"], ["F%d" % fa])
                ts("pool", hT[:, kc, :], FS[fa][:], dpl(kc, kc + 1), dpl(8 + kc, 9 + kc), ALU.mult, ALU.add,
                   ["F%d" % fa, ("dp", li)], [("hT", kc)])
                if kc % 2 == 1:
                    yield

            def proj(col0, M, b):
                for kc in range(8):
                    mm(ps[0:M, b, :], win[:, kc, col0:col0 + M], hT[:, kc, :], kc == 0, kc == 7,
                       [("hT", kc), "win"], [bank(b)])

            for which, col0 in (("q", C_DQ), ("k", C_DK)):
                b = nbA()
                proj(col0, 128, b)
                yield
                act(B0[:], ps[:, b, :], AF.Square, [bank(b)], ["B0"])
                b5 = nbA()
                mm(ps[:, b5, :], cb[:, CB_BLK:CB_BLK + 128], B0[:], True, True, ["B0", "cb"], [bank(b5)])
                yield
                rsqrt_mean(FS[4][:], ps[:, b5, :], [bank(b5)], "F4")
                if which == "q":
                    for mq in range(2):
                        rs = slice(64 * mq, 64 * mq + 64)
                        stt("dve", QT[rs, par, mq, :], ps[rs, b, :], dp[rs, li, 24:25], FS[4][rs, :], ALU.mult, ALU.mult,
                            [bank(b), "F4", ("dp", li)], [("QT", par)])
                else:
                    stt("dve", KT[:, t0:t0 + TQ], ps[:, b, :], ppl(18, 19), FS[4][:], ALU.mult, ALU.mult,
                        [bank(b), "F4", "pp"], [("KT", i)])
                yield
            for col0, dst, dname, fe in ((C_DZ, FZ[par], ("FZ", par), 5), (C_GZ, FS[3], "F3", 6)):
                b = nbA()
                proj(col0, 128, b)
                yield
                sigmoid_neg(FS[fe][:], ps[:, b, :], [bank(b)], "F%d" % fe)
                tt("dve", dst[:], ps[:, b, :], FS[fe][:], ALU.mult, [bank(b), "F%d" % fe], [dname])
                yield
            b = nbA()
            proj(C_GV, 128, b)
            cp("dve", C1[:, 3:3 + TQ], ps[:, b, :], [bank(b)], ["C1"])
            yield
            b = nbA()
            proj(C_GQK, 128, b)
            cp("dve", C0[:, 3:3 + TQ], ps[:, b, :], [bank(b)], ["C0"])
            yield
            b = nbA()
            proj(C_GLR, 16, b)
            cp("dve", G0[0:16, :], ps[0:16, b, :], [bank(b)], ["G0"])
            yield
            b = nbA()
            for j in range(4):
                for kc in range(8):
                    mm(ps[:, b, j * 128:(j + 1) * 128], hT[:, kc, j * 128:(j + 1) * 128], win[:, kc, C_DV:C_DV + 128],
                       kc == 0, kc == 7, [("hT", kc), "win"], [bank(b)])
                if j % 2 == 1:
                    yield
            for j in range(4):
                cp("dve", V[:, 4 * i + j, 0:128], ps[:, b, j * 128:(j + 1) * 128], [bank(b)], [("V", i)])
            yield

            for (Cb, cname, w0, fo, fe) in ((C0, "C0", 8, 8, 4), (C1, "C1", 12, 9, 5)):
                fon = "F%d" % fo
                ts("dve", FS[fo][:], Cb[:, 0:TQ], ppl(w0, w0 + 1), None, ALU.mult, None, [cname, "pp"], [fon])
                for j in range(1, 4):
                    stt("dve", FS[fo][:], Cb[:, j:j + TQ], ppl(w0 + j, w0 + j + 1), FS[fo][:], ALU.mult, ALU.add,
                        [cname, "pp", fon], [fon])
                cp("pool", Cb[:, 0:3], Cb[:, TQ:TQ + 3], [cname], [cname])
                yield
                sigmoid_neg(FS[fe][:], FS[fo][:], [fon], "F%d" % fe)
                tt("dve", FS[fo][:], FS[fo][:], FS[fe][:], ALU.mult, [fon, "F%d" % fe], [fon])
                yield
            sqk, sv = FS[8], FS[9]
            bg = nbA()
            for j in range(4):
                mm(ps[:, bg, j * 64:(j + 1) * 64], G0[0:17, j * 128:(j + 1) * 128], wgk[0:17, li, :], True, True,
                   ["G0", "wgk"], [bank(bg)])
            yield
            act(L0[:], ps[:, bg, 0:256], AF.Exp, [bank(bg)], ["L0"], scale=-1.0)
            act(L0[:], L0[:], AF.Ln, ["L0"], ["L0"], scale=1.0, bias=1.0)
            yield
            bd = nbA()
            for j in range(4):
                mm(ps[:, bd, j * 64:(j + 1) * 64], cf[:, CF_U:CF_U + 128], L0[:, j * 64:(j + 1) * 64], True, True,
                   ["cf", "L0"], [bank(bd)])
            yield
            act(L1[:], ps[:, bd, 0:256], AF.Exp, [bank(bd)], ["L1"])
            bk = nbA()
            for j in range(4):
                mm(ps[:, bk, j * 64:(j + 1) * 64], sqk[64:128, j * 128:(j + 1) * 128], cf[64:128, CF_ID + 64:CF_ID + 128],
                   True, True, ["F8", "cf"], [bank(bk)])
            yield
            tt("dve", kdec[:], ps[:, bk, 0:256], L1[:], ALU.mult, [bank(bk), "L1"], ["kdec"])
            bv = nbA()
            for j in range(4):
                mm(ps[:, bv, j * 128:(j + 1) * 128], sv[:, j * 128:(j + 1) * 128], cf[:, CF_ID:CF_ID + 128],
                   True, True, ["F9", "cf"], [bank(bv)])
            yield
            cp("dve", vtok[:], ps[:, bv, :], [bank(bv)], ["vtok"])
            bb = nbA()
            for j in range(4):
                mm(ps[0:64, bb, 2 * j:2 * j + 2], L0[:, j * 64:(j + 1) * 64], cf[:, CF_CIND:CF_CIND + 2], True, True,
                   ["L0", "cf"], [bank(bb)])
            yield
            act(a8[:], ps[0:64, bb, 0:8], AF.Exp, [bank(bb)], ["a8"])
            ubs = [nbA(), nbA()]
            for c in range(8):
                j, hh = c // 2, c % 2
                ub = ubs[c % 2]
                col = (c // 2) * 128
                mm(ps[0:64, ub, col:col + 128], kdec[64 * hh:64 * hh + 64, j * 64:(j + 1) * 64],
                   vtok[64 * hh:64 * hh + 64, j * 128:(j + 1) * 128], True, True, ["kdec", "vtok"], [bank(ub)])
            yield
            obk = nbA()
            for c in range(8):
                ub = ubs[c % 2]
                col = (c // 2) * 128
                pc = (c - 1) % 8
                stt("dve", Sring[:, c, :], Sring[:, pc, :], a8[:, c:c + 1], ps[0:64, ub, col:col + 128], ALU.mult, ALU.add,
                    [("S", pc), "a8", bank(ub)], [("S", c)])
                mm(ps[:, obk, c * 64:(c + 1) * 64], Sring[:, c, :], sqk[0:64, c * 64:(c + 1) * 64], True, True,
                   [("S", c), "F8"], [bank(obk)])
                if c % 2 == 1:
                    yield
            act(B0[:], ps[:, obk, :], AF.Square, [bank(obk)], ["B0"])
            bs = nbA()
            mm(ps[:, bs, :], cb[:, CB_G:CB_G + 128], B0[:], True, True, ["B0", "cb"], [bank(bs)])
            yield
            rsqrt_mean(FS[4][:], ps[:, bs, :], [bank(bs)], "F4")
            stt("dve", FS[5][:], ps[:, obk, :], dpl(26, 27), FS[4][:], ALU.mult, ALU.mult, [bank(obk), "F4", ("dp", li)], ["F5"])
            tt("pool", oh[:, par, 0, :], FS[5][:], FS[3][:], ALU.mult, ["F5", "F3"], [("oh", par, 0)])
            yield
            if extra is not None:
                yield from extra

        pslot = [0]

        def gen_B(i, li):
            dpl = lambda c0, c1: dp[:, li, c0:c1]
            par = i % 2
            nkt = 4 * (i + 1)
            LA = 2
            OB, SBK = 6, 7
            infos = [{}, {}]

            def emit_st(m, kt):
                diag = kt >= 4 * i
                q0 = 128 * (kt - 4 * i) if diag else 0
                b = nbB()
                mm(ps[:, b, q0:TQ], KT[:, kt * 128:(kt + 1) * 128],
                   QT[:, par, m, q0:TQ], True, True, [("KT", kt // 4), ("QT", par)], [bank(b)])
                infos[m][kt] = (b, q0, diag)

            OBX = (6, 7)

            def finalize_map(m):
                for qs in range(4):
                    bk, c0 = OBX[qs // 2], (qs % 2) * 129
                    SCH.op("dve", lambda h, bk=bk, c0=c0, qs=qs: h.reciprocal(out=rc[:, m, qs:qs + 1], in_=ps[:, bk, c0 + 128:c0 + 129]),
                           [bank(bk)], [("rc", m)])
                for qs in range(4):
                    bk, c0 = OBX[qs // 2], (qs % 2) * 129
                    ts("dve", AS[m][:, qs, :], ps[:, bk, c0:c0 + 128], rc[:, m, qs:qs + 1], None, ALU.mult, None,
                       [bank(bk), ("rc", m)], ["A%d" % m])

            for kt in range(min(LA, nkt)):
                emit_st(0, kt)
            for m in range(2):
                for kt in range(nkt):
                    b, q0, diag = infos[m].pop(kt)
                    sl = pslot[0]
                    pslot[0] = (sl + 1) % NPT
                    if not diag:
                        act(Pt[:, sl, q0:TQ], ps[:, b, q0:TQ], AF.Exp, [bank(b)], [("Pt", sl)])
                    else:
                        act(Pt[0:64, sl, q0:TQ], ps[0:64, b, q0:TQ], AF.Exp, [bank(b)], [("Pt", sl)])
                        if q0 + 64 < TQ:
                            act(Pt[64:128, sl, q0 + 64:TQ], ps[64:128, b, q0 + 64:TQ], AF.Exp, [bank(b)], [("Pt", sl)])
                        cp("act", Pt[64:128, sl, q0:q0 + 64], cb[64:128, CB_BLK:CB_BLK + 64], ["cb"], [("Pt", sl)])
                    for qs in range(q0 // 128, 4):
                        bk, c0 = OBX[qs // 2], (qs % 2) * 129
                        mm(ps[:, bk, c0:c0 + 129], Pt[:, sl, qs * 128:(qs + 1) * 128], V[:, kt, :],
                           (kt == 0 and qs % 2 == 0), False, [("V", kt // 4), "Vones", ("Pt", sl)], [bank(bk)])
                    if kt + LA < nkt:
                        emit_st(m, kt + LA)
                    elif m == 0 and (kt + LA - nkt) < min(LA, nkt):
                        emit_st(1, kt + LA - nkt)
                    yield
                finalize_map(m)
                yield
            stt("dve", AS[0][:], AS[1][:], dpl(27, 28), AS[0][:], ALU.mult, ALU.add, ["A1", "A0", ("dp", li)], ["A0"])
            tt("dve", AS[2][:], AS[0][:], AS[0][:], ALU.mult, ["A0"], ["A2"])
            SCH.op("dve", lambda h: h.tensor_reduce(out=ssq[:, 0:4], in_=AS[2][:], axis=mybir.AxisListType.X, op=ALU.add),
                   ["A2"], ["ssq"])
            yield
            act(ssq[:, 4:8], ssq[:, 0:4], AF.Ln, ["ssq"], ["ssq2"], scale=1.0 / 128.0, bias=EPS)
            act(ssq[:, 4:8], ssq[:, 4:8], AF.Exp, ["ssq2"], ["ssq2"], scale=-0.5)
            for qs in range(4):
                ts("dve", ON[:, qs, :], AS[0][:, qs, :], ssq[:, 4 + qs:5 + qs], None, ALU.mult, None, ["A0", "ssq2"], ["ON"])
            bs = nbB()
            for qs in range(4):
                mm(ps[:, bs, qs * 128:(qs + 1) * 128], ON[:, qs, :], cb[:, CB_ID:CB_ID + 128], True, True,
                   ["ON", "cb"], [bank(bs)])
            yield
            stt("dve", oh[:, par, 1, :], ps[:, bs, :], dpl(25, 26), FZ[par][:], ALU.mult, ALU.mult,
                [bank(bs), ("dp", li), ("FZ", par)], [("oh", par, 1)])
            tc0 = (i % TPC) * TQ
            SCH.dma("pool", lambda h: h.dma_start(
                out=ohT[i // TPC].rearrange("(g p) s -> p g s", p=128)[:, :, tc0:tc0 + TQ], in_=oh[:, par, :, :]),
                ("ohs", par), reads=[("oh", par, 0), ("oh", par, 1)], writes=[("ohT", i)])
            if fused and (i % TPC) == TPC - 1:
                c = i // TPC
                k = li % 2
                SCH.dma("pool", lambda h, k=k, c=c: h.collective_compute(
                    "AllGather", ALU.bypass, replica_groups=[[0, 1, 2, 3], [4, 5, 6, 7]],
                    ins=[ohT[c]], outs=[oT_fulls[k][c]]),
                    ("cc", li, c), reads=[("ohT", ii) for ii in range(c * TPC, (c + 1) * TPC)],
                    writes=[(("ofull", k), ii) for ii in range(c * TPC, (c + 1) * TPC)], inc=None)
            yield

        NA_EST = [62]

        def drain(g):
            n = 0
            for _ in g:
                n += 1
            return n

        def run_layer_tiles(li, x_src, x_src_name, prev_args, extra_of, tail_gen):
            for i in range(NT):
                gB = gen_B(i, li)
                if i + 1 < NT:
                    gA = gen_A(i + 1, li, x_src, x_src_name, prev_args, extra_of(i + 1))
                else:
                    gA = tail_gen
                nB = 8 * (i + 1) + 4
                hold = 0
                if i + 1 == NT:
                    hold = 4 * (i + 1) + 6
                ratio = (NA_EST[0] + 12.0) / max(nB - hold, 1)
                acc = 0.0
                step = 0
                for _ in gB:
                    step += 1
                    if gA is None or step <= hold:
                        continue
                    acc += ratio
                    while acc >= 1.0 and gA is not None:
                        acc -= 1.0
                        try:
                            next(gA)
                        except StopIteration:
                            gA = None
                if gA is not None:
                    drain(gA)

        def emit_layer_params(li, a_idx, lam_init):
            ppl = lambda c0, c1: pp[:, li, c0:c1]
            dpl = lambda c0, c1: dp[:, li, c0:c1]
            r = [("dp", li)]
            stt("dve", dpl(0, 8), modT[:, a_idx, 8:16], 1.0, ppl(0, 8), ALU.add, ALU.mult, [("modT", a_idx), "pp"], r)
            cp("dve", dpl(8, 16), modT[:, a_idx, 0:8], [("modT", a_idx)], r)
            ts("dve", dpl(24, 25), ppl(17, 18), 0.125, None, ALU.mult, None, ["pp"], r)
            ts("dve", dpl(25, 26), ppl(19, 20), 1.0 - lam_init, None, ALU.mult, None, ["pp"], r)
            ts("dve", dpl(26, 27), ppl(16, 17), 0.125, None, ALU.mult, None, ["pp"], r)
            tt("dve", lamt[:, li, 0:1], ppl(20, 21), ppl(21, 22), ALU.mult, ["pp"], [("lamt", li)])
            tt("dve", lamt[:, li, 1:2], ppl(22, 23), ppl(23, 24), ALU.mult, ["pp"], [("lamt", li)])
            bl = nbA()
            mm(ps[:, bl, 32:34], cf[:, CF_ONE:CF_ONE + 128], lamt[:, li, 0:2], True, True, [("lamt", li), "cf"], [bank(bl)])
            act(lamt[:, li, 2:4], ps[:, bl, 32:34], AF.Exp, [bank(bl)], [("lamt2", li)])
            tt("dve", dpl(27, 28), lamt[:, li, 3:4], lamt[:, li, 2:3], ALU.subtract, [("lamt2", li)], r)
            ts("dve", dpl(27, 28), dpl(27, 28), -lam_init, None, ALU.add, None, r, r)

        assert fused and n_mix >= 1 and do_final and not has_prev
        emit_ada(layer_ids[0])

        def layer_ctx(li):
            l = layer_ids[li]
            if li == 0:
                return xT_in, "xin", None
            o_src, o_name = oT_fulls[(li - 1) % 2], ("ofull", (li - 1) % 2)
            xs, xn = (xT_in, "xin") if li == 1 else (xw, "xw")
            return xs, xn, (o_src, o_name, li - 1, xw, "xw")

        def extra_of_layer(li):
            def f(i):
                if li + 1 >= n_mix:
                    return None
                lo, hi = (i * 24) // NT, ((i + 1) * 24) // NT
                if lo == hi:
                    return None

                def g():
                    for cc in range(lo, hi):
                        yield from gen_ada_bg(layer_ids[li + 1], cc)
                return g()
            return f

        def gen_layer_start(li):
            l = layer_ids[li]
            lam_init = 0.8 - 0.6 * math.exp(-0.3 * l)
            xs, xn, prev_args = layer_ctx(li)
            if prev_args is not None:
                load_weight(wout, wout_in[li - 1], D, "wout")
                cp("dve", gates[:, li - 1, :], modT[:, layer_ids[li - 1], 16:24], [("modT", layer_ids[li - 1])],
                   [("gates", li - 1)])
                yield
            load_weight(win, win_in[li], NCOL, "win")
            yield
            emit_layer_params(li, l, lam_init)
            SCH.op("dve", lambda h: h.memset(Sring[:, 7, :], 0.0), writes=[("S", 7)])
            SCH.op("dve", lambda h: h.memset(C0[:, 0:3], 0.0), writes=["C0"])
            SCH.op("dve", lambda h: h.memset(C1[:, 0:3], 0.0), writes=["C1"])
            yield
            yield from gen_A(0, li, xs, xn, prev_args, extra_of_layer(li)(0))

        def gen_final_start():
            l_last = layer_ids[n_mix - 1]
            load_weight(wout, wout_in[n_mix - 1], D, "wout")
            cp("dve", gates[:, n_mix - 1, :], modT[:, l_last, 16:24], [("modT", l_last)], [("gates", n_mix - 1)])
            yield
            xs, xn = (xT_in, "xin") if n_mix == 1 else (xw, "xw")
            emit_load_x(0, xs, xn)
            yield from gen_outproj(0, oT_fulls[(n_mix - 1) % 2], ("ofull", (n_mix - 1) % 2), n_mix - 1, yT, "yT")

        SCH.ep = 1
        n0 = drain(gen_layer_start(0))
        NA_EST[0] = max(n0 - 3, 40)
        for li in range(n_mix):
            SCH.ep = li + 1
            xs, xn, prev_args = layer_ctx(li)
            tail = gen_layer_start(li + 1) if li + 1 < n_mix else gen_final_start()
            if NCH > 1 and 'nooverlap' not in DBG:
                run_layer_tiles(li, xs, xn, prev_args, extra_of_layer(li), tail)
            else:
                run_layer_tiles(li, xs, xn, prev_args, extra_of_layer(li), None)
                drain(tail)
        SCH.ep = n_mix + 1
        xs, xn = (xT_in, "xin") if n_mix == 1 else (xw, "xw")
        fin_args = (oT_fulls[(n_mix - 1) % 2], ("ofull", (n_mix - 1) % 2), n_mix - 1, yT, "yT")
        for i in range(1, NT):
            emit_load_x(i, xs, xn)
            drain(gen_outproj(i, *fin_args))
        SCH.emit()
    return nc


def _consts():
    cf = np.zeros((128, CF_W), np.float32)
    cf[:, CF_ID:CF_ID + 128] = np.eye(128, dtype=np.float32)
    s = np.arange(128)[:, None]
    t = np.arange(128)[None, :]
    cf[:, CF_U:CF_U + 128] = np.where((s > t) & (s // 64 == t // 64), -1.0 / 16.0, 0.0)
    cf[:, CF_ONE:CF_ONE + 128] = 1.0
    cf[:, CF_CIND:CF_CIND + 2] = np.where(s // 64 == np.arange(2)[None, :], -1.0 / 16.0, 0.0)
    cbm = np.zeros((128, CB_W), np.float32)
    cbm[:, CB_X:CB_X + 128] = 1.0 / 1024.0
    cbm[:, CB_BLK:CB_BLK + 128] = np.where(s // 64 == t // 64, 1.0 / 64.0, 0.0)
    cbm[:, CB_D:CB_D + 128] = 1.0 / 128.0
    cbm[:, CB_G:CB_G + 128] = 1.0 / 8192.0
    cbm[:, CB_ONE:CB_ONE + 128] = 1.0
    cbm[:, CB_ID:CB_ID + 128] = np.eye(128, dtype=np.float32)
    return cf, cbm.astype(ml_dtypes.bfloat16)


def _head_cols(h):
    gq = np.arange(0, 64) + h * 64
    gk = 256 + np.arange(0, 64) + h * 64
    gv = 512 + np.arange(0, 128) + h * 128
    glr = 1024 + np.arange(16)
    gz = 1040 + np.arange(128) + h * 128
    dq = 1552 + np.arange(128) + h * 128
    dk = 2064 + np.arange(128) + h * 128
    dv = 2576 + np.arange(128) + h * 128
    dz = 3088 + np.arange(128) + h * 128
    return np.concatenate([dq, dk, dz, gz, gv, gq, gk, dv, glr])


def _pp(inp, l, h):
    p = np.zeros((128, NPP), np.float32)
    p[:, 0:8] = inp["norm_g"][l].reshape(8, 128).T
    cw = inp["conv_w"][l]
    ch_qk = np.concatenate([np.arange(64) + h * 64, 256 + np.arange(64) + h * 64])
    p[:, 8:12] = cw[:, ch_qk].T
    p[:, 12:16] = cw[:, 512 + h * 128 + np.arange(128)].T
    p[:, 16] = inp["gla_norm_g"][l]
    p[:, 17] = np.tile(inp["qn_g"][l], 2)
    p[:, 18] = np.tile(inp["kn_g"][l], 2)
    p[:, 19] = inp["diff_norm_g"][l]
    p[0:64, 20] = inp["lam_q1"][l]
    p[0:64, 21] = inp["lam_k1"][l]
    p[0:64, 22] = inp["lam_q2"][l]
    p[0:64, 23] = inp["lam_k2"][l]
    return p


def _wgk(inp, l, h):
    w = np.zeros((17, 64), np.float32)
    w[0:16] = inp["w_gk"][l][:, h * 64:(h + 1) * 64]
    w[16] = inp["b_gk"][l][h * 64:(h + 1) * 64]
    return w


def _wout_perm(inp, l):
    rows = np.concatenate([np.concatenate([np.arange(128) + h * 128, 512 + np.arange(128) + h * 128]) for h in range(4)])
    return np.ascontiguousarray(inp["w_out"][l][rows, :])


_PROG_CACHE = {}


def _get_prog(S, n_mix, has_prev, do_final, layer_ids, fused=False):
    key = (S, n_mix, has_prev, do_final, tuple(layer_ids), fused)
    if key not in _PROG_CACHE:
        _PROG_CACHE[key] = build_program(S, n_mix, has_prev, do_final, layer_ids, fused)
    return _PROG_CACHE[key]


def run_unfused(inp, S):
    cf, cbm = _consts()
    B = inp["x"].shape[0]
    xT = [np.ascontiguousarray(inp["x"][b, :S].T) for b in range(B)]
    cT = [np.ascontiguousarray(inp["c"][b].reshape(8, 128).T) for b in range(B)]
    oT = None
    for l in range(DEPTH):
        has_prev = l > 0
        nc = _get_prog(S, 1, has_prev, False, [l])
        in_maps = []
        for core in range(8):
            b, h = core // 4, core % 4
            m = {"xT": xT[b], "cT": cT[b], "cf": cf, "cb": cbm,
                 "w_in": np.ascontiguousarray(inp["w_in"][l][:, _head_cols(h)])[None],
                 "pp": _pp(inp, l, h)[None], "wgk": _wgk(inp, l, h)[None]}
            if has_prev:
                m["w_ada"] = np.ascontiguousarray(inp["w_ada"][l - 1:l + 1])
                m["b_adaT"] = np.stack([inp["b_ada"][k].reshape(24, 128).T for k in (l - 1, l)])
                m["w_out"] = _wout_perm(inp, l - 1)[None]
                m["oT_in"] = oT[b][None]
            else:
                m["w_ada"] = np.ascontiguousarray(inp["w_ada"][l:l + 1])
                m["b_adaT"] = np.stack([inp["b_ada"][l].reshape(24, 128).T])
            in_maps.append(m)
        res = run_bass_kernel_spmd(nc, in_maps, core_ids=list(range(8))).results
        oT = [np.concatenate([res[b * 4 + h]["ohT"][0] for h in range(4)], axis=0) for b in range(B)]
        if has_prev:
            xT = [res[b * 4]["xT_out"] for b in range(B)]
    nc = _get_prog(S, 0, True, True, [DEPTH - 1])
    in_maps = []
    for core in range(8):
        b = core // 4
        in_maps.append({"xT": xT[b], "cT": cT[b], "cf": cf, "cb": cbm,
                        "w_ada": np.ascontiguousarray(inp["w_ada"][DEPTH - 1:DEPTH]),
                        "b_adaT": np.stack([inp["b_ada"][DEPTH - 1].reshape(24, 128).T]),
                        "w_out": _wout_perm(inp, DEPTH - 1)[None], "oT_in": oT[b][None]})
    res = run_bass_kernel_spmd(nc, in_maps, core_ids=list(range(8))).results
    out = np.stack([res[b * 4]["yT"].T for b in range(B)])
    return np.ascontiguousarray(out)


def fused_in_maps(inp, S):
    cf, cbm = _consts()
    B = inp["x"].shape[0]
    xT = [np.ascontiguousarray(inp["x"][b, :S].T) for b in range(B)]
    cT = [np.ascontiguousarray(inp["c"][b].reshape(8, 128).T) for b in range(B)]
    w_ada = np.ascontiguousarray(inp["w_ada"])
    b_adaT = np.stack([inp["b_ada"][l].reshape(24, 128).T for l in range(DEPTH)])
    w_out = np.stack([_wout_perm(inp, l) for l in range(DEPTH)])
    in_maps = []
    for core in range(8):
        b, h = core // 4, core % 4
        cols = _head_cols(h)
        in_maps.append({
            "xT": xT[b], "cT": cT[b], "cf": cf, "cb": cbm, "w_ada": w_ada, "b_adaT": b_adaT,
            "w_in": np.stack([inp["w_in"][l][:, cols] for l in range(DEPTH)]),
            "pp": np.stack([_pp(inp, l, h) for l in range(DEPTH)]),
            "wgk": np.stack([_wgk(inp, l, h) for l in range(DEPTH)]),
            "w_out": w_out})
    return in_maps


def run_fused(inp, S):
    B = inp["x"].shape[0]
    nc = _get_prog(S, DEPTH, False, True, list(range(DEPTH)), True)
    res = run_bass_kernel_spmd(nc, fused_in_maps(inp, S), core_ids=list(range(8))).results
    return np.ascontiguousarray(np.stack([res[b * 4]["yT"].T for b in range(B)]))


def kernel(**inputs):
    inp = {k: np.asarray(v) for k, v in inputs.items()}
    return run_fused(inp, inp["x"].shape[1])
```

```python
import contextlib
import math

import ml_dtypes
import numpy as np

import concourse.bass as bass
import concourse.mybir as mybir
from concourse.bass_utils import run_bass_kernel_spmd

F32 = mybir.dt.float32
BF16 = mybir.dt.bfloat16
AF = mybir.ActivationFunctionType
ALU = mybir.AluOpType

D = 1024
TQ = 512
NCOL = 912
EPS = 1e-6
DEPTH = 4
C_DQ, C_DK, C_DZ, C_GZ, C_GV, C_GQK, C_DV, C_GLR = 0, 128, 256, 384, 512, 640, 768, 896
CF_ID, CF_U, CF_ONE, CF_CIND = 0, 128, 256, 384
CF_W = 386
CB_X, CB_BLK, CB_D, CB_G, CB_ONE, CB_ID = 0, 128, 256, 384, 512, 640
CB_W = 768
NPP = 24
DBG = set()


class _Ins:
    __slots__ = ("eng", "fn", "deps", "idx", "is_dma", "sig", "tick", "key", "cum", "waits", "ep", "inc")

    def __init__(self, eng, fn, is_dma=False, key=None):
        self.eng = eng
        self.fn = fn
        self.deps = []
        self.idx = -1
        self.is_dma = is_dma
        self.sig = False
        self.tick = 0
        self.key = key
        self.cum = 0
        self.waits = []
        self.ep = 0
        self.inc = 16


class Sched:
    ENGS = ("sp", "pe", "act", "dve", "pool")

    def __init__(self, nc):
        self.nc = nc
        self.q = {e: [] for e in self.ENGS}
        self.res = {}
        self.dma_keys = {}
        self.ep = 0
        self.defer = False
        self.now = 0
        self.delta = 3
        self.pending = {e: [] for e in self.ENGS}
        self.rs_of = {}
        self.slots = {}
        self.last_rs = {}

    def _track(self, ins, reads, writes):
        deps = {}
        for r in reads:
            st = self.res.get(r)
            if st is not None and st[0] is not None:
                deps[id(st[0])] = (st[0], True)
        for w in writes:
            st = self.res.get(w)
            if st is not None:
                if st[0] is not None and id(st[0]) not in deps:
                    deps[id(st[0])] = (st[0], False)
                for rd in st[1]:
                    if id(rd) not in deps:
                        deps[id(rd)] = (rd, False)
        for r in reads:
            st = self.res.setdefault(r, [None, []])
            st[1].append(ins)
        for w in writes:
            self.res[w] = [ins, []]
        deps.pop(id(ins), None)
        ins.deps = list(deps.values())

    RATE = {"pe": 2, "act": 1}

    def _place(self, ins):
        if not self.defer:
            ins.idx = len(self.q[ins.eng])
            self.q[ins.eng].append(ins)
            return
        e = ins.eng
        rs = max(self.now + 1, self.last_rs.get(e, 0))
        for (d, raw) in ins.deps:
            drs = self.rs_of.get(id(d))
            if drs is not None:
                rs = max(rs, drs + (0 if d.eng == e and not d.is_dma else self.delta))
        lim = self.RATE.get(e)
        if lim is not None:
            while self.slots.get((e, rs), 0) >= lim:
                rs += 1
            self.slots[(e, rs)] = self.slots.get((e, rs), 0) + 1
        self.last_rs[e] = rs
        self.rs_of[id(ins)] = rs
        self.pending[e].append((rs, ins))

    def step(self):
        self.now += 1
        for e in self.ENGS:
            p = self.pending[e]
            k = 0
            while k < len(p) and p[k][0] <= self.now:
                ins = p[k][1]
                ins.idx = len(self.q[e])
                self.q[e].append(ins)
                k += 1
            if k:
                del p[:k]

    def flush(self):
        while any(self.pending[e] for e in self.ENGS):
            self.step()
        self.rs_of = {}
        self.slots = {}
        self.last_rs = {}

    def op(self, eng, fn, reads=(), writes=()):
        ins = _Ins(eng, fn)
        ins.ep = self.ep
        self._track(ins, reads, writes)
        self._place(ins)
        return ins

    def dma(self, queue, fn, key, reads=(), writes=(), inc=16):
        ins = _Ins(queue, fn, is_dma=True, key=key)
        ins.ep = self.ep
        ins.inc = inc
        self._track(ins, reads, writes)
        self.dma_keys[key] = self.dma_keys.get(key, 0) + (inc if inc else 1)
        ins.cum = self.dma_keys[key]
        self._place(ins)
        return ins

    def emit(self):
        nc = self.nc
        for e in self.ENGS:
            for ins in self.q[e]:
                need = []
                best = {}
                bestk = {}
                for (d, raw) in ins.deps:
                    if d.is_dma:
                        bk = bestk.get(d.key)
                        if bk is None or d.cum > bk.cum:
                            bestk[d.key] = d
                        continue
                    if d.eng == ins.eng and not ins.is_dma:
                        assert d.idx < ins.idx, "same-engine dependency placed after its consumer"
                        if e == "pe" or not raw:
                            continue
                    b = best.get(d.eng)
                    if b is None or d.idx > b.idx:
                        best[d.eng] = d
                for d in bestk.values():
                    need.append(d)
                for d in best.values():
                    need.append(d)
                ins.waits = need
        for e in self.ENGS:
            seen = {}
            for ins in self.q[e]:
                keep = []
                for d in ins.waits:
                    if d.is_dma:
                        k = ("k", d.key)
                        if seen.get(k, -1) >= d.cum:
                            continue
                        seen[k] = d.cum
                    else:
                        k = ("e", d.eng)
                        if seen.get(k, -1) >= d.idx:
                            continue
                        seen[k] = d.idx
                        d.sig = True
                    keep.append(d)
                ins.waits = keep
        eps = set()
        for e in self.ENGS:
            t = {}
            for ins in self.q[e]:
                if ins.sig and not ins.is_dma:
                    t[ins.ep] = t.get(ins.ep, 0) + 1
                    ins.tick = t[ins.ep]
                    eps.add((e, ins.ep))
        self.max_ticks = {}
        for e in self.ENGS:
            for ins in self.q[e]:
                if ins.tick:
                    self.max_ticks[(e, ins.ep)] = max(self.max_ticks.get((e, ins.ep), 0), ins.tick)
        self.n_ins = {e: len(self.q[e]) for e in self.ENGS}
        stack = contextlib.ExitStack()
        esem = {k: stack.enter_context(nc.semaphore("s_%s_%d" % k)) for k in sorted(eps)}
        ksem = {}
        for n, k in enumerate(self.dma_keys):
            ksem[k] = stack.enter_context(nc.semaphore("k%d" % n))
        q = self.q
        dma_keys = self.dma_keys

        def run(e, h):
            for ins in q[e]:
                for d in ins.waits:
                    if d.is_dma:
                        h.wait_ge(ksem[d.key], d.cum)
                    else:
                        h.wait_ge(esem[(d.eng, d.ep)], d.tick)
                bi = ins.fn(h)
                if ins.is_dma:
                    if ins.inc:
                        bi.then_inc(ksem[ins.key], ins.inc)
                    else:
                        bi.then_inc(ksem[ins.key])
                elif ins.sig:
                    bi.then_inc(esem[(e, ins.ep)], 1)
            if e == "sp":
                for k, v in dma_keys.items():
                    h.wait_ge(ksem[k], v)

        with stack:
            with nc.Block() as block:
                @block.sync
                def _(h):
                    run("sp", h)

                @block.tensor
                def _(h):
                    run("pe", h)

                @block.scalar
                def _(h):
                    run("act", h)

                @block.vector
                def _(h):
                    run("dve", h)

                @block.gpsimd
                def _(h):
                    run("pool", h)


def build_program(S, n_mix, has_prev, do_final, layer_ids, fused=False):
    assert S % TQ == 0
    NT = S // TQ
    n_out = (1 if has_prev else 0) + max(n_mix - 1, 0) + (1 if (do_final and n_mix > 0) else 0)
    if n_mix == 0:
        n_out = 1
    n_ada = n_out + n_mix if not fused else DEPTH
    nc = bass.Bass("TRN2", target_bir_lowering=False)
    dt_in = lambda n, s, d: nc.dram_tensor(n, s, d, kind="ExternalInput").ap()
    dt_out = lambda n, s, d: nc.dram_tensor(n, s, d, kind="ExternalOutput").ap()
    dt_int = lambda n, s, d: nc.dram_tensor(n, s, d, kind="Internal").ap()

    xT_in = dt_in("xT", [D, S], F32)
    cT_in = dt_in("cT", [128, 8], F32)
    cf_in = dt_in("cf", [128, CF_W], F32)
    cb_in = dt_in("cb", [128, CB_W], BF16)
    wada_in = dt_in("w_ada", [n_ada, D, 3 * D], F32)
    bada_in = dt_in("b_adaT", [n_ada, 128, 24], F32)
    if n_mix > 0:
        win_in = dt_in("w_in", [n_mix, D, NCOL], F32)
        pp_in = dt_in("pp", [n_mix, 128, NPP], F32)
        wgk_in = dt_in("wgk", [n_mix, 17, 64], F32)
    if n_out > 0:
        wout_in = dt_in("w_out", [n_out, D, D], F32)
    if has_prev or n_mix == 0:
        oT_in = dt_in("oT_in", [1, D, S], BF16)
    CW = min(2048, S) if fused else S
    NCH = S // CW
    TPC = CW // TQ
    if n_mix > 0:
        if fused:
            ohT = dt_int("ohT", [NCH, 256, CW], BF16)
            oT_fulls = [dt_int("oT_full%d" % k, [NCH, D, CW], BF16) for k in range(2)]
        else:
            ohT = dt_out("ohT", [NCH, 256, CW], BF16)
    if do_final:
        yT = dt_out("yT", [D, S], F32)
    xw = None
    if n_mix > 0 and n_out - (1 if do_final else 0) > 0:
        xw = dt_int("xw", [D, S], F32) if (fused or do_final) else dt_out("xT_out", [D, S], F32)

    def tview(ap, i):
        return ap.rearrange("(kc p) s -> p kc s", p=128)[:, :, i * TQ:(i + 1) * TQ]

    with contextlib.ExitStack() as st:
        def sb(n, s, d):
            return st.enter_context(nc.sbuf_tensor("sb_" + n, s, d))

        xt = sb("xt", [128, 8, TQ], F32)
        ot = sb("ot", [128, 8, TQ], BF16)
        hT = sb("hT", [128, 8, TQ], BF16)
        sqs = sb("sqs", [128, 2, TQ], BF16)
        wout = sb("wout", [128, 8, D], BF16)
        wstage = sb("wstage", [128, 2, D], F32)
        cf = sb("cf", [128, CF_W], F32)
        cb = sb("cb", [128, CB_W], BF16)
        cT = sb("cT", [128, 8], F32)
        cact = sb("cact", [128, 8], F32)
        ctmp = sb("ctmp", [128, 8], F32)
        modT = sb("modT", [128, n_ada, 24], F32)
        badaT = sb("badaT", [128, n_ada, 24], F32)
        FS = {k: sb("F%d" % k, [128, TQ], F32) for k in (0, 3, 4, 5, 6, 8, 9, 10, 11)}
        if n_mix > 0:
            win = sb("win", [128, 8, NCOL], BF16)
            KT = sb("KT", [128, S], BF16)
            V = sb("V", [128, S // 128, 129], BF16)
            QT = sb("QT", [128, 2, 2, TQ], BF16)
            FZ = [sb("FZ%d" % k, [128, TQ], F32) for k in range(2)]
            AS = [sb("AS%d" % k, [128, 4, 128], F32) for k in range(3)]
            ON = sb("ON", [128, 4, 128], BF16)
            rc = sb("rc", [128, 2, 4], F32)
            ssq = sb("ssq", [128, 8], F32)
            NPT = 6
            Pt = sb("Pt", [128, NPT, TQ], BF16)
            C0 = sb("C0", [128, TQ + 3], F32)
            C1 = sb("C1", [128, TQ + 3], F32)
            G0 = sb("G0", [32, TQ], F32)
            B0 = sb("B0", [128, TQ], BF16)
            L0 = sb("L0", [128, 256], F32)
            L1 = sb("L1", [128, 256], F32)
            kdec = sb("kdec", [128, 256], BF16)
            vtok = sb("vtok", [128, TQ], BF16)
            a8 = sb("a8", [64, 8], F32)
            Sring = sb("Sring", [64, 8, 128], F32)
            oh = sb("oh", [128, 2, 2, TQ], BF16)
            pp = sb("pp", [128, n_mix, NPP], F32)
            dp = sb("dp", [128, n_mix, 32], F32)
            wgk = sb("wgk", [32, n_mix, 64], F32)
            lamt = sb("lamt", [128, n_mix, 4], F32)
        gates = sb("gates", [128, max(n_out, 1), 8], F32)
        ps = st.enter_context(nc.psum_tensor("ps", [128, 8, TQ], F32))

        SCH = Sched(nc)
        ring = [0]

        def nb():
            ring[0] = (ring[0] + 1) % 4
            return ring[0]

        ringA = [0]
        ringB = [0]

        def nbA():
            ringA[0] = (ringA[0] + 1) % 3
            return ringA[0]

        def nbB():
            ringB[0] = (ringB[0] + 1) % 3
            return 3 + ringB[0]

        def bank(b):
            return ("bank", b)

        def mm(out, lhsT, rhs, start, stop, reads, writes):
            SCH.op("pe", lambda h: h.matmul(out, lhsT=lhsT, rhs=rhs, start=start, stop=stop,
                                            skip_group_check=True), reads, writes)

        def act(out, in_, func, reads, writes, scale=1.0, bias=0.0):
            SCH.op("act", lambda h: h.activation(out=out, in_=in_, func=func, bias=bias, scale=scale),
                   reads, writes)

        def tt(eng, out, in0, in1, op, reads, writes):
            SCH.op(eng, lambda h: h.tensor_tensor(out=out, in0=in0, in1=in1, op=op), reads, writes)

        def ts(eng, out, in0, s1, s2, op0, op1, reads, writes):
            if s2 is None:
                SCH.op(eng, lambda h: h.tensor_scalar(out=out, in0=in0, scalar1=s1, scalar2=None, op0=op0),
                       reads, writes)
            else:
                SCH.op(eng, lambda h: h.tensor_scalar(out=out, in0=in0, scalar1=s1, scalar2=s2, op0=op0, op1=op1),
                       reads, writes)

        def stt(eng, out, in0, scalar, in1, op0, op1, reads, writes):
            SCH.op(eng, lambda h: h.scalar_tensor_tensor(out=out, in0=in0, scalar=scalar, in1=in1, op0=op0, op1=op1),
                   reads, writes)

        def cp(eng, out, in_, reads, writes):
            if eng == "act":
                SCH.op("act", lambda h: h.copy(out=out, in_=in_), reads, writes)
            else:
                SCH.op(eng, lambda h: h.tensor_copy(out=out, in_=in_), reads, writes)

        def rsqrt_mean(dst, src_ps, reads, writes_name):
            act(dst, src_ps, AF.Ln, reads, [writes_name], scale=1.0, bias=EPS)
            act(dst, dst, AF.Exp, [writes_name], [writes_name], scale=-0.5)

        def sigmoid_neg(dst, src, reads, name):
            act(dst, src, AF.Exp, reads, [name], scale=-1.0)
            act(dst, dst, AF.Ln, [name], [name], scale=1.0, bias=1.0)
            act(dst, dst, AF.Exp, [name], [name], scale=-1.0)

        SCH.dma("sp", lambda h: h.dma_start(out=cf[:], in_=cf_in), "c_cf", writes=["cf"])
        SCH.dma("sp", lambda h: h.dma_start(out=cb[:], in_=cb_in), "c_cb", writes=["cb"])
        SCH.dma("sp", lambda h: h.dma_start(out=cT[:], in_=cT_in), "c_cT", writes=["cT"])
        SCH.dma("sp", lambda h: h.dma_start(out=badaT[:], in_=bada_in.rearrange("a p c -> p a c")), "c_bada",
                writes=["badaT"])
        if n_mix > 0:
            SCH.dma("sp", lambda h: h.dma_start(out=pp[:], in_=pp_in.rearrange("l p c -> p l c")), "c_pp", writes=["pp"])
            SCH.op("dve", lambda h: h.memset(wgk[:], 0.0), writes=["wgk"])
            SCH.dma("sp", lambda h: h.dma_start(out=wgk[0:17, :, :], in_=wgk_in.rearrange("l k c -> k l c")), "c_wgk",
                    reads=[], writes=["wgk"])
            SCH.op("dve", lambda h: h.memset(G0[:], 1.0), writes=["G0"])
            for par_ in range(2):
                SCH.op("pool", lambda h, par_=par_: h.memset(QT[:, par_, :, :], 0.0), writes=[("QT", par_)])
            SCH.op("pool", lambda h: h.memset(V[:, :, 128:129], 1.0), writes=["Vones"])
        sigmoid_neg(ctmp[:], cT[:], ["cT"], "ctmp")
        tt("dve", cact[:], cT[:], ctmp[:], ALU.mult, ["cT", "ctmp"], ["cact"])

        def emit_ada(a):
            for blk in range(6):
                SCH.dma("sp", lambda h, blk=blk: h.dma_start(
                    out=xt[:], in_=wada_in[a].rearrange("(kc p) n -> p kc n", p=128)[:, :, blk * 512:(blk + 1) * 512]),
                    "xl", writes=[("xt", dc) for dc in range(8)])
                for cc in range(4):
                    col = blk * 4 + cc
                    for kc in range(8):
                        mm(ps[:, 7, col:col + 1], xt[:, kc, cc * 128:(cc + 1) * 128], cact[:, kc:kc + 1],
                           kc == 0, kc == 7, [("xt", kc), "cact"], [bank(7)])
            tt("dve", modT[:, a, :], ps[:, 7, 0:24], badaT[:, a, :], ALU.add, [bank(7), "badaT"], [("modT", a)])

        astage = sb("astage", [128, 1, 8, 128], F32)

        def gen_ada_bg(a, cc):
            slot = 0
            SCH.dma("sp", lambda h: h.dma_start(
                out=astage[:, slot, :, :], in_=wada_in[a].rearrange("(kc p) n -> p kc n", p=128)[:, :, cc * 128:(cc + 1) * 128]),
                ("ast", slot), writes=[("astage", slot)])
            yield
            b = nbA()
            for kc in range(8):
                mm(ps[:, b, 0:1], astage[:, slot, kc, :], cact[:, kc:kc + 1], kc == 0, kc == 7,
                   [("astage", slot), "cact"], [bank(b)])
            tt("dve", modT[:, a, cc:cc + 1], ps[:, b, 0:1], badaT[:, a, cc:cc + 1], ALU.add, [bank(b), "badaT"], [("modT", a)])
            yield

        def load_weight(dst, src2d, ncols, resname):
            for kc in range(8):
                slot = kc % 2
                SCH.dma("sp", lambda h, kc=kc, slot=slot: h.dma_start(
                    out=wstage[:, slot, 0:ncols], in_=src2d[kc * 128:(kc + 1) * 128, :]),
                    ("wst", slot), writes=[("wstage", slot)])
                cp("pool", dst[:, kc, :], wstage[:, slot, 0:ncols], [("wstage", slot)], [resname])

        def emit_load_x(i, src, src_name):
            SCH.dma("sp", lambda h: h.dma_start(out=xt[:], in_=tview(src, i)), "xl",
                    reads=[(src_name, i)], writes=[("xt", dc) for dc in range(8)])

        def gen_outproj(i, o_src, o_name, gate_idx, dst, dst_name):
            SCH.dma("sp", lambda h: h.dma_start(out=ot[:], in_=tview(o_src[i // TPC], i % TPC)), "ol",
                    reads=[(o_name, i)], writes=["ot"])
            yield
            for dc in range(8):
                b = nbA()
                for kc in range(8):
                    mm(ps[:, b, :], wout[:, kc, dc * 128:(dc + 1) * 128], ot[:, kc, :], kc == 0, kc == 7,
                       ["ot", "wout"], [bank(b)])
                stt("dve", xt[:, dc, :], ps[:, b, :], gates[:, gate_idx, dc:dc + 1], xt[:, dc, :], ALU.mult, ALU.add,
                    [bank(b), ("xt", dc), ("gates", gate_idx)], [("xt", dc)])
                yield
            SCH.dma("pool", lambda h: h.dma_start(out=tview(dst, i), in_=xt[:]), "xs",
                    reads=[("xt", dc) for dc in range(8)], writes=[(dst_name, i)])

        def gen_A(i, li, x_src, x_src_name, prev_args, extra=None):
            ppl = lambda c0, c1: pp[:, li, c0:c1]
            dpl = lambda c0, c1: dp[:, li, c0:c1]
            t0 = i * TQ
            par = i % 2
            emit_load_x(i, x_src, x_src_name)
            if prev_args is not None:
                yield from gen_outproj(i, *prev_args)
            bx = nbA()
            for kc in range(8):
                sl = kc % 2
                tt("pool", sqs[:, sl, :], xt[:, kc, :], xt[:, kc, :], ALU.mult, [("xt", kc)], [("sqs", sl)])
                mm(ps[:, bx, :], cb[:, CB_X:CB_X + 128], sqs[:, sl, :], kc == 0, kc == 7, [("sqs", sl), "cb"], [bank(bx)])
                if kc % 2 == 1:
                    yield
            rsqrt_mean(FS[0][:], ps[:, bx, :], [bank(bx)], "F0")
            yield
            for kc in range(8):
                fa = 10 + (kc % 2)
                tt("dve", FS[fa][:], xt[:, kc, :], FS[0][:], ALU.mult, [("xt", kc), "F0"], ["F%d" % fa])
                ts("pool", hT[:, kc, :], FS[fa][:], dpl(kc, kc + 1), dpl(8 + kc, 9 + kc), ALU.mult, ALU.add,
                   ["F%d" % fa, ("dp", li)], [("hT", kc)])
                if kc % 2 == 1:
                    yield

            def proj(col0, M, b):
                for kc in range(8):
                    mm(ps[0:M, b, :], win[:, kc, col0:col0 + M], hT[:, kc, :], kc == 0, kc == 7,
                       [("hT", kc), "win"], [bank(b)])

            for which, col0 in (("q", C_DQ), ("k", C_DK)):
                b = nbA()
                proj(col0, 128, b)
                yield
                act(B0[:], ps[:, b, :], AF.Square, [bank(b)], ["B0"])
                b5 = nbA()
                mm(ps[:, b5, :], cb[:, CB_BLK:CB_BLK + 128], B0[:], True, True, ["B0", "cb"], [bank(b5)])
                yield
                rsqrt_mean(FS[4][:], ps[:, b5, :], [bank(b5)], "F4")
                if which == "q":
                    for mq in range(2):
                        rs = slice(64 * mq, 64 * mq + 64)
                        stt("dve", QT[rs, par, mq, :], ps[rs, b, :], dp[rs, li, 24:25], FS[4][rs, :], ALU.mult, ALU.mult,
                            [bank(b), "F4", ("dp", li)], [("QT", par)])
                else:
                    stt("dve", KT[:, t0:t0 + TQ], ps[:, b, :], ppl(18, 19), FS[4][:], ALU.mult, ALU.mult,
                        [bank(b), "F4", "pp"], [("KT", i)])
                yield
            for col0, dst, dname, fe in ((C_DZ, FZ[par], ("FZ", par), 5), (C_GZ, FS[3], "F3", 6)):
                b = nbA()
                proj(col0, 128, b)
                yield
                sigmoid_neg(FS[fe][:], ps[:, b, :], [bank(b)], "F%d" % fe)
                tt("dve", dst[:], ps[:, b, :], FS[fe][:], ALU.mult, [bank(b), "F%d" % fe], [dname])
                yield
            b = nbA()
            proj(C_GV, 128, b)
            cp("dve", C1[:, 3:3 + TQ], ps[:, b, :], [bank(b)], ["C1"])
            yield
            b = nbA()
            proj(C_GQK, 128, b)
            cp("dve", C0[:, 3:3 + TQ], ps[:, b, :], [bank(b)], ["C0"])
            yield
            b = nbA()
            proj(C_GLR, 16, b)
            cp("dve", G0[0:16, :], ps[0:16, b, :], [bank(b)], ["G0"])
            yield
            b = nbA()
            for j in range(4):
                for kc in range(8):
                    mm(ps[:, b, j * 128:(j + 1) * 128], hT[:, kc, j * 128:(j + 1) * 128], win[:, kc, C_DV:C_DV + 128],
                       kc == 0, kc == 7, [("hT", kc), "win"], [bank(b)])
                if j % 2 == 1:
                    yield
            for j in range(4):
                cp("dve", V[:, 4 * i + j, 0:128], ps[:, b, j * 128:(j + 1) * 128], [bank(b)], [("V", i)])
            yield

            for (Cb, cname, w0, fo, fe) in ((C0, "C0", 8, 8, 4), (C1, "C1", 12, 9, 5)):
                fon = "F%d" % fo
                ts("dve", FS[fo][:], Cb[:, 0:TQ], ppl(w0, w0 + 1), None, ALU.mult, None, [cname, "pp"], [fon])
                for j in range(1, 4):
                    stt("dve", FS[fo][:], Cb[:, j:j + TQ], ppl(w0 + j, w0 + j + 1), FS[fo][:], ALU.mult, ALU.add,
                        [cname, "pp", fon], [fon])
                cp("pool", Cb[:, 0:3], Cb[:, TQ:TQ + 3], [cname], [cname])
                yield
                sigmoid_neg(FS[fe][:], FS[fo][:], [fon], "F%d" % fe)
                tt("dve", FS[fo][:], FS[fo][:], FS[fe][:], ALU.mult, [fon, "F%d" % fe], [fon])
                yield
            sqk, sv = FS[8], FS[9]
            bg = nbA()
            for j in range(4):
                mm(ps[:, bg, j * 64:(j + 1) * 64], G0[0:17, j * 128:(j + 1) * 128], wgk[0:17, li, :], True, True,
                   ["G0", "wgk"], [bank(bg)])
            yield
            act(L0[:], ps[:, bg, 0:256], AF.Exp, [bank(bg)], ["L0"], scale=-1.0)
            act(L0[:], L0[:], AF.Ln, ["L0"], ["L0"], scale=1.0, bias=1.0)
            yield
            bd = nbA()
            for j in range(4):
                mm(ps[:, bd, j * 64:(j + 1) * 64], cf[:, CF_U:CF_U + 128], L0[:, j * 64:(j + 1) * 64], True, True,
                   ["cf", "L0"], [bank(bd)])
            yield
            act(L1[:], ps[:, bd, 0:256], AF.Exp, [bank(bd)], ["L1"])
            bk = nbA()
            for j in range(4):
                mm(ps[:, bk, j * 64:(j + 1) * 64], sqk[64:128, j * 128:(j + 1) * 128], cf[64:128, CF_ID + 64:CF_ID + 128],
                   True, True, ["F8", "cf"], [bank(bk)])
            yield
            tt("dve", kdec[:], ps[:, bk, 0:256], L1[:], ALU.mult, [bank(bk), "L1"], ["kdec"])
            bv = nbA()
            for j in range(4):
                mm(ps[:, bv, j * 128:(j + 1) * 128], sv[:, j * 128:(j + 1) * 128], cf[:, CF_ID:CF_ID + 128],
                   True, True, ["F9", "cf"], [bank(bv)])
            yield
            cp("dve", vtok[:], ps[:, bv, :], [bank(bv)], ["vtok"])
            bb = nbA()
            for j in range(4):
                mm(ps[0:64, bb, 2 * j:2 * j + 2], L0[:, j * 64:(j + 1) * 64], cf[:, CF_CIND:CF_CIND + 2], True, True,
                   ["L0", "cf"], [bank(bb)])
            yield
            act(a8[:], ps[0:64, bb, 0:8], AF.Exp, [bank(bb)], ["a8"])
            ubs = [nbA(), nbA()]
            for c in range(8):
                j, hh = c // 2, c % 2
                ub = ubs[c % 2]
                col = (c // 2) * 128
                mm(ps[0:64, ub, col:col + 128], kdec[64 * hh:64 * hh + 64, j * 64:(j + 1) * 64],
                   vtok[64 * hh:64 * hh + 64, j * 128:(j + 1) * 128], True, True, ["kdec", "vtok"], [bank(ub)])
            yield
            obk = nbA()
            for c in range(8):
                ub = ubs[c % 2]
                col = (c // 2) * 128
                pc = (c - 1) % 8
                stt("dve", Sring[:, c, :], Sring[:, pc, :], a8[:, c:c + 1], ps[0:64, ub, col:col + 128], ALU.mult, ALU.add,
                    [("S", pc), "a8", bank(ub)], [("S", c)])
                mm(ps[:, obk, c * 64:(c + 1) * 64], Sring[:, c, :], sqk[0:64, c * 64:(c + 1) * 64], True, True,
                   [("S", c), "F8"], [bank(obk)])
                if c % 2 == 1:
                    yield
            act(B0[:], ps[:, obk, :], AF.Square, [bank(obk)], ["B0"])
            bs = nbA()
            mm(ps[:, bs, :], cb[:, CB_G:CB_G + 128], B0[:], True, True, ["B0", "cb"], [bank(bs)])
            yield
            rsqrt_mean(FS[4][:], ps[:, bs, :], [bank(bs)], "F4")
            stt("dve", FS[5][:], ps[:, obk, :], dpl(26, 27), FS[4][:], ALU.mult, ALU.mult, [bank(obk), "F4", ("dp", li)], ["F5"])
            tt("pool", oh[:, par, 0, :], FS[5][:], FS[3][:], ALU.mult, ["F5", "F3"], [("oh", par, 0)])
            yield
            if extra is not None:
                yield from extra

        pslot = [0]

        def gen_B(i, li):
            dpl = lambda c0, c1: dp[:, li, c0:c1]
            par = i % 2
            nkt = 4 * (i + 1)
            LA = 2
            OB, SBK = 6, 7
            infos = [{}, {}]

            def emit_st(m, kt):
                diag = kt >= 4 * i
                q0 = 128 * (kt - 4 * i) if diag else 0
                b = nbB()
                mm(ps[:, b, q0:TQ], KT[:, kt * 128:(kt + 1) * 128],
                   QT[:, par, m, q0:TQ], True, True, [("KT", kt // 4), ("QT", par)], [bank(b)])
                infos[m][kt] = (b, q0, diag)

            OBX = (6, 7)

            def finalize_map(m):
                for qs in range(4):
                    bk, c0 = OBX[qs // 2], (qs % 2) * 129
                    SCH.op("dve", lambda h, bk=bk, c0=c0, qs=qs: h.reciprocal(out=rc[:, m, qs:qs + 1], in_=ps[:, bk, c0 + 128:c0 + 129]),
                           [bank(bk)], [("rc", m)])
                for qs in range(4):
                    bk, c0 = OBX[qs // 2], (qs % 2) * 129
                    ts("dve", AS[m][:, qs, :], ps[:, bk, c0:c0 + 128], rc[:, m, qs:qs + 1], None, ALU.mult, None,
                       [bank(bk), ("rc", m)], ["A%d" % m])

            for kt in range(min(LA, nkt)):
                emit_st(0, kt)
            for m in range(2):
                for kt in range(nkt):
                    b, q0, diag = infos[m].pop(kt)
                    sl = pslot[0]
                    pslot[0] = (sl + 1) % NPT
                    if not diag:
                        act(Pt[:, sl, q0:TQ], ps[:, b, q0:TQ], AF.Exp, [bank(b)], [("Pt", sl)])
                    else:
                        act(Pt[0:64, sl, q0:TQ], ps[0:64, b, q0:TQ], AF.Exp, [bank(b)], [("Pt", sl)])
                        if q0 + 64 < TQ:
                            act(Pt[64:128, sl, q0 + 64:TQ], ps[64:128, b, q0 + 64:TQ], AF.Exp, [bank(b)], [("Pt", sl)])
                        cp("act", Pt[64:128, sl, q0:q0 + 64], cb[64:128, CB_BLK:CB_BLK + 64], ["cb"], [("Pt", sl)])
                    for qs in range(q0 // 128, 4):
                        bk, c0 = OBX[qs // 2], (qs % 2) * 129
                        mm(ps[:, bk, c0:c0 + 129], Pt[:, sl, qs * 128:(qs + 1) * 128], V[:, kt, :],
                           (kt == 0 and qs % 2 == 0), False, [("V", kt // 4), "Vones", ("Pt", sl)], [bank(bk)])
                    if kt + LA < nkt:
                        emit_st(m, kt + LA)
                    elif m == 0 and (kt + LA - nkt) < min(LA, nkt):
                        emit_st(1, kt + LA - nkt)
                    yield
                finalize_map(m)
                yield
            stt("dve", AS[0][:], AS[1][:], dpl(27, 28), AS[0][:], ALU.mult, ALU.add, ["A1", "A0", ("dp", li)], ["A0"])
            tt("dve", AS[2][:], AS[0][:], AS[0][:], ALU.mult, ["A0"], ["A2"])
            SCH.op("dve", lambda h: h.tensor_reduce(out=ssq[:, 0:4], in_=AS[2][:], axis=mybir.AxisListType.X, op=ALU.add),
                   ["A2"], ["ssq"])
            yield
            act(ssq[:, 4:8], ssq[:, 0:4], AF.Ln, ["ssq"], ["ssq2"], scale=1.0 / 128.0, bias=EPS)
            act(ssq[:, 4:8], ssq[:, 4:8], AF.Exp, ["ssq2"], ["ssq2"], scale=-0.5)
            for qs in range(4):
                ts("dve", ON[:, qs, :], AS[0][:, qs, :], ssq[:, 4 + qs:5 + qs], None, ALU.mult, None, ["A0", "ssq2"], ["ON"])
            bs = nbB()
            for qs in range(4):
                mm(ps[:, bs, qs * 128:(qs + 1) * 128], ON[:, qs, :], cb[:, CB_ID:CB_ID + 128], True, True,
                   ["ON", "cb"], [bank(bs)])
            yield
            stt("dve", oh[:, par, 1, :], ps[:, bs, :], dpl(25, 26), FZ[par][:], ALU.mult, ALU.mult,
                [bank(bs), ("dp", li), ("FZ", par)], [("oh", par, 1)])
            tc0 = (i % TPC) * TQ
            SCH.dma("pool", lambda h: h.dma_start(
                out=ohT[i // TPC].rearrange("(g p) s -> p g s", p=128)[:, :, tc0:tc0 + TQ], in_=oh[:, par, :, :]),
                ("ohs", par), reads=[("oh", par, 0), ("oh", par, 1)], writes=[("ohT", i)])
            if fused and (i % TPC) == TPC - 1:
                c = i // TPC
                k = li % 2
                SCH.dma("pool", lambda h, k=k, c=c: h.collective_compute(
                    "AllGather", ALU.bypass, replica_groups=[[0, 1, 2, 3], [4, 5, 6, 7]],
                    ins=[ohT[c]], outs=[oT_fulls[k][c]]),
                    ("cc", li, c), reads=[("ohT", ii) for ii in range(c * TPC, (c + 1) * TPC)],
                    writes=[(("ofull", k), ii) for ii in range(c * TPC, (c + 1) * TPC)], inc=None)
            yield

        NA_EST = [62]

        def drain(g):
            n = 0
            for _ in g:
                n += 1
            return n

        def run_layer_tiles(li, x_src, x_src_name, prev_args, extra_of, tail_gen):
            for i in range(NT):
                gB = gen_B(i, li)
                hold = 0
                if i + 1 < NT:
                    gA = gen_A(i + 1, li, x_src, x_src_name, prev_args, extra_of(i + 1))
                else:
                    gA = tail_gen
                    hold = 4 * (i + 1) + 6
                step = 0
                if gA is not None and hold == 0:
                    SCH.defer = True
                    drain(gA)
                    SCH.defer = False
                    gA = None
                for _ in gB:
                    step += 1
                    SCH.step()
                    if gA is not None and step == hold:
                        SCH.defer = True
                        drain(gA)
                        SCH.defer = False
                        gA = None
                if gA is not None:
                    drain(gA)
                SCH.flush()

        def emit_layer_params(li, a_idx, lam_init):
            ppl = lambda c0, c1: pp[:, li, c0:c1]
            dpl = lambda c0, c1: dp[:, li, c0:c1]
            r = [("dp", li)]
            stt("dve", dpl(0, 8), modT[:, a_idx, 8:16], 1.0, ppl(0, 8), ALU.add, ALU.mult, [("modT", a_idx), "pp"], r)
            cp("dve", dpl(8, 16), modT[:, a_idx, 0:8], [("modT", a_idx)], r)
            ts("dve", dpl(24, 25), ppl(17, 18), 0.125, None, ALU.mult, None, ["pp"], r)
            ts("dve", dpl(25, 26), ppl(19, 20), 1.0 - lam_init, None, ALU.mult, None, ["pp"], r)
            ts("dve", dpl(26, 27), ppl(16, 17), 0.125, None, ALU.mult, None, ["pp"], r)
            tt("dve", lamt[:, li, 0:1], ppl(20, 21), ppl(21, 22), ALU.mult, ["pp"], [("lamt", li)])
            tt("dve", lamt[:, li, 1:2], ppl(22, 23), ppl(23, 24), ALU.mult, ["pp"], [("lamt", li)])
            bl = nbA()
            mm(ps[:, bl, 32:34], cf[:, CF_ONE:CF_ONE + 128], lamt[:, li, 0:2], True, True, [("lamt", li), "cf"], [bank(bl)])
            act(lamt[:, li, 2:4], ps[:, bl, 32:34], AF.Exp, [bank(bl)], [("lamt2", li)])
            tt("dve", dpl(27, 28), lamt[:, li, 3:4], lamt[:, li, 2:3], ALU.subtract, [("lamt2", li)], r)
            ts("dve", dpl(27, 28), dpl(27, 28), -lam_init, None, ALU.add, None, r, r)

        assert fused and n_mix >= 1 and do_final and not has_prev
        emit_ada(layer_ids[0])

        def layer_ctx(li):
            l = layer_ids[li]
            if li == 0:
                return xT_in, "xin", None
            o_src, o_name = oT_fulls[(li - 1) % 2], ("ofull", (li - 1) % 2)
            xs, xn = (xT_in, "xin") if li == 1 else (xw, "xw")
            return xs, xn, (o_src, o_name, li - 1, xw, "xw")

        def extra_of_layer(li):
            def f(i):
                if li + 1 >= n_mix:
                    return None
                lo, hi = (i * 24) // NT, ((i + 1) * 24) // NT
                if lo == hi:
                    return None

                def g():
                    for cc in range(lo, hi):
                        yield from gen_ada_bg(layer_ids[li + 1], cc)
                return g()
            return f

        def gen_layer_start(li):
            l = layer_ids[li]
            lam_init = 0.8 - 0.6 * math.exp(-0.3 * l)
            xs, xn, prev_args = layer_ctx(li)
            if prev_args is not None:
                load_weight(wout, wout_in[li - 1], D, "wout")
                cp("dve", gates[:, li - 1, :], modT[:, layer_ids[li - 1], 16:24], [("modT", layer_ids[li - 1])],
                   [("gates", li - 1)])
                yield
            load_weight(win, win_in[li], NCOL, "win")
            yield
            emit_layer_params(li, l, lam_init)
            SCH.op("dve", lambda h: h.memset(Sring[:, 7, :], 0.0), writes=[("S", 7)])
            SCH.op("dve", lambda h: h.memset(C0[:, 0:3], 0.0), writes=["C0"])
            SCH.op("dve", lambda h: h.memset(C1[:, 0:3], 0.0), writes=["C1"])
            yield
            yield from gen_A(0, li, xs, xn, prev_args, extra_of_layer(li)(0))

        def gen_final_start():
            l_last = layer_ids[n_mix - 1]
            load_weight(wout, wout_in[n_mix - 1], D, "wout")
            cp("dve", gates[:, n_mix - 1, :], modT[:, l_last, 16:24], [("modT", l_last)], [("gates", n_mix - 1)])
            yield
            xs, xn = (xT_in, "xin") if n_mix == 1 else (xw, "xw")
            emit_load_x(0, xs, xn)
            yield from gen_outproj(0, oT_fulls[(n_mix - 1) % 2], ("ofull", (n_mix - 1) % 2), n_mix - 1, yT, "yT")

        SCH.ep = 1
        n0 = drain(gen_layer_start(0))
        NA_EST[0] = max(n0 - 3, 40)
        for li in range(n_mix):
            SCH.ep = li + 1
            xs, xn, prev_args = layer_ctx(li)
            tail = gen_layer_start(li + 1) if li + 1 < n_mix else gen_final_start()
            if NCH > 1 and 'nooverlap' not in DBG:
                run_layer_tiles(li, xs, xn, prev_args, extra_of_layer(li), tail)
            else:
                run_layer_tiles(li, xs, xn, prev_args, extra_of_layer(li), None)
                drain(tail)
        SCH.ep = n_mix + 1
        xs, xn = (xT_in, "xin") if n_mix == 1 else (xw, "xw")
        fin_args = (oT_fulls[(n_mix - 1) % 2], ("ofull", (n_mix - 1) % 2), n_mix - 1, yT, "yT")
        for i in range(1, NT):
            emit_load_x(i, xs, xn)
            drain(gen_outproj(i, *fin_args))
        SCH.emit()
    return nc


def _consts():
    cf = np.zeros((128, CF_W), np.float32)
    cf[:, CF_ID:CF_ID + 128] = np.eye(128, dtype=np.float32)
    s = np.arange(128)[:, None]
    t = np.arange(128)[None, :]
    cf[:, CF_U:CF_U + 128] = np.where((s > t) & (s // 64 == t // 64), -1.0 / 16.0, 0.0)
    cf[:, CF_ONE:CF_ONE + 128] = 1.0
    cf[:, CF_CIND:CF_CIND + 2] = np.where(s // 64 == np.arange(2)[None, :], -1.0 / 16.0, 0.0)
    cbm = np.zeros((128, CB_W), np.float32)
    cbm[:, CB_X:CB_X + 128] = 1.0 / 1024.0
    cbm[:, CB_BLK:CB_BLK + 128] = np.where(s // 64 == t // 64, 1.0 / 64.0, 0.0)
    cbm[:, CB_D:CB_D + 128] = 1.0 / 128.0
    cbm[:, CB_G:CB_G + 128] = 1.0 / 8192.0
    cbm[:, CB_ONE:CB_ONE + 128] = 1.0
    cbm[:, CB_ID:CB_ID + 128] = np.eye(128, dtype=np.float32)
    return cf, cbm.astype(ml_dtypes.bfloat16)


def _head_cols(h):
    gq = np.arange(0, 64) + h * 64
    gk = 256 + np.arange(0, 64) + h * 64
    gv = 512 + np.arange(0, 128) + h * 128
    glr = 1024 + np.arange(16)
    gz = 1040 + np.arange(128) + h * 128
    dq = 1552 + np.arange(128) + h * 128
    dk = 2064 + np.arange(128) + h * 128
    dv = 2576 + np.arange(128) + h * 128
    dz = 3088 + np.arange(128) + h * 128
    return np.concatenate([dq, dk, dz, gz, gv, gq, gk, dv, glr])


def _pp(inp, l, h):
    p = np.zeros((128, NPP), np.float32)
    p[:, 0:8] = inp["norm_g"][l].reshape(8, 128).T
    cw = inp["conv_w"][l]
    ch_qk = np.concatenate([np.arange(64) + h * 64, 256 + np.arange(64) + h * 64])
    p[:, 8:12] = cw[:, ch_qk].T
    p[:, 12:16] = cw[:, 512 + h * 128 + np.arange(128)].T
    p[:, 16] = inp["gla_norm_g"][l]
    p[:, 17] = np.tile(inp["qn_g"][l], 2)
    p[:, 18] = np.tile(inp["kn_g"][l], 2)
    p[:, 19] = inp["diff_norm_g"][l]
    p[0:64, 20] = inp["lam_q1"][l]
    p[0:64, 21] = inp["lam_k1"][l]
    p[0:64, 22] = inp["lam_q2"][l]
    p[0:64, 23] = inp["lam_k2"][l]
    return p


def _wgk(inp, l, h):
    w = np.zeros((17, 64), np.float32)
    w[0:16] = inp["w_gk"][l][:, h * 64:(h + 1) * 64]
    w[16] = inp["b_gk"][l][h * 64:(h + 1) * 64]
    return w


def _wout_perm(inp, l):
    rows = np.concatenate([np.concatenate([np.arange(128) + h * 128, 512 + np.arange(128) + h * 128]) for h in range(4)])
    return np.ascontiguousarray(inp["w_out"][l][rows, :])


_PROG_CACHE = {}


def _get_prog(S, n_mix, has_prev, do_final, layer_ids, fused=False):
    key = (S, n_mix, has_prev, do_final, tuple(layer_ids), fused)
    if key not in _PROG_CACHE:
        _PROG_CACHE[key] = build_program(S, n_mix, has_prev, do_final, layer_ids, fused)
    return _PROG_CACHE[key]


def run_unfused(inp, S):
    cf, cbm = _consts()
    B = inp["x"].shape[0]
    xT = [np.ascontiguousarray(inp["x"][b, :S].T) for b in range(B)]
    cT = [np.ascontiguousarray(inp["c"][b].reshape(8, 128).T) for b in range(B)]
    oT = None
    for l in range(DEPTH):
        has_prev = l > 0
        nc = _get_prog(S, 1, has_prev, False, [l])
        in_maps = []
        for core in range(8):
            b, h = core // 4, core % 4
            m = {"xT": xT[b], "cT": cT[b], "cf": cf, "cb": cbm,
                 "w_in": np.ascontiguousarray(inp["w_in"][l][:, _head_cols(h)])[None],
                 "pp": _pp(inp, l, h)[None], "wgk": _wgk(inp, l, h)[None]}
            if has_prev:
                m["w_ada"] = np.ascontiguousarray(inp["w_ada"][l - 1:l + 1])
                m["b_adaT"] = np.stack([inp["b_ada"][k].reshape(24, 128).T for k in (l - 1, l)])
                m["w_out"] = _wout_perm(inp, l - 1)[None]
                m["oT_in"] = oT[b][None]
            else:
                m["w_ada"] = np.ascontiguousarray(inp["w_ada"][l:l + 1])
                m["b_adaT"] = np.stack([inp["b_ada"][l].reshape(24, 128).T])
            in_maps.append(m)
        res = run_bass_kernel_spmd(nc, in_maps, core_ids=list(range(8))).results
        oT = [np.concatenate([res[b * 4 + h]["ohT"][0] for h in range(4)], axis=0) for b in range(B)]
        if has_prev:
            xT = [res[b * 4]["xT_out"] for b in range(B)]
    nc = _get_prog(S, 0, True, True, [DEPTH - 1])
    in_maps = []
    for core in range(8):
        b = core // 4
        in_maps.append({"xT": xT[b], "cT": cT[b], "cf": cf, "cb": cbm,
                        "w_ada": np.ascontiguousarray(inp["w_ada"][DEPTH - 1:DEPTH]),
                        "b_adaT": np.stack([inp["b_ada"][DEPTH - 1].reshape(24, 128).T]),
                        "w_out": _wout_perm(inp, DEPTH - 1)[None], "oT_in": oT[b][None]})
    res = run_bass_kernel_spmd(nc, in_maps, core_ids=list(range(8))).results
    out = np.stack([res[b * 4]["yT"].T for b in range(B)])
    return np.ascontiguousarray(out)


def fused_in_maps(inp, S):
    cf, cbm = _consts()
    B = inp["x"].shape[0]
    xT = [np.ascontiguousarray(inp["x"][b, :S].T) for b in range(B)]
    cT = [np.ascontiguousarray(inp["c"][b].reshape(8, 128).T) for b in range(B)]
    w_ada = np.ascontiguousarray(inp["w_ada"])
    b_adaT = np.stack([inp["b_ada"][l].reshape(24, 128).T for l in range(DEPTH)])
    w_out = np.stack([_wout_perm(inp, l) for l in range(DEPTH)])
    in_maps = []
    for core in range(8):
        b, h = core // 4, core % 4
        cols = _head_cols(h)
        in_maps.append({
            "xT": xT[b], "cT": cT[b], "cf": cf, "cb": cbm, "w_ada": w_ada, "b_adaT": b_adaT,
            "w_in": np.stack([inp["w_in"][l][:, cols] for l in range(DEPTH)]),
            "pp": np.stack([_pp(inp, l, h) for l in range(DEPTH)]),
            "wgk": np.stack([_wgk(inp, l, h) for l in range(DEPTH)]),
            "w_out": w_out})
    return in_maps


def run_fused(inp, S):
    B = inp["x"].shape[0]
    nc = _get_prog(S, DEPTH, False, True, list(range(DEPTH)), True)
    res = run_bass_kernel_spmd(nc, fused_in_maps(inp, S), core_ids=list(range(8))).results
    return np.ascontiguousarray(np.stack([res[b * 4]["yT"].T for b in range(B)]))


def kernel(**inputs):
    inp = {k: np.asarray(v) for k, v in inputs.items()}
    return run_fused(inp, inp["x"].shape[1])
```

```python
import contextlib
import math

import ml_dtypes
import numpy as np

import concourse.bass as bass
import concourse.mybir as mybir
from concourse.bass_utils import run_bass_kernel_spmd

F32 = mybir.dt.float32
BF16 = mybir.dt.bfloat16
AF = mybir.ActivationFunctionType
ALU = mybir.AluOpType

D = 1024
TQ = 512
NCOL = 912
EPS = 1e-6
DEPTH = 4
C_DQ, C_DK, C_DZ, C_GZ, C_GV, C_GQK, C_DV, C_GLR = 0, 128, 256, 384, 512, 640, 768, 896
CF_ID, CF_U, CF_ONE, CF_CIND = 0, 128, 256, 384
CF_W = 386
CB_X, CB_BLK, CB_D, CB_G, CB_ONE, CB_ID = 0, 128, 256, 384, 512, 640
CB_W = 768
NPP = 24
DBG = set()


class _Ins:
    __slots__ = ("eng", "fn", "deps", "idx", "is_dma", "sig", "tick", "key", "cum", "waits", "ep", "inc")

    def __init__(self, eng, fn, is_dma=False, key=None):
        self.eng = eng
        self.fn = fn
        self.deps = []
        self.idx = -1
        self.is_dma = is_dma
        self.sig = False
        self.tick = 0
        self.key = key
        self.cum = 0
        self.waits = []
        self.ep = 0
        self.inc = 16


class Sched:
    ENGS = ("sp", "pe", "act", "dve", "pool")

    def __init__(self, nc):
        self.nc = nc
        self.q = {e: [] for e in self.ENGS}
        self.res = {}
        self.dma_keys = {}
        self.ep = 0
        self.defer = False
        self.now = 0
        self.delta = 3
        self.pending = {e: [] for e in self.ENGS}
        self.rs_of = {}
        self.slots = {}
        self.last_rs = {}

    def _track(self, ins, reads, writes):
        deps = {}
        for r in reads:
            st = self.res.get(r)
            if st is not None and st[0] is not None:
                deps[id(st[0])] = (st[0], True)
        for w in writes:
            st = self.res.get(w)
            if st is not None:
                if st[0] is not None and id(st[0]) not in deps:
                    deps[id(st[0])] = (st[0], False)
                for rd in st[1]:
                    if id(rd) not in deps:
                        deps[id(rd)] = (rd, False)
        for r in reads:
            st = self.res.setdefault(r, [None, []])
            st[1].append(ins)
        for w in writes:
            self.res[w] = [ins, []]
        deps.pop(id(ins), None)
        ins.deps = list(deps.values())

    RATE = {"pe": 2, "act": 1}

    def _place(self, ins):
        if not self.defer:
            ins.idx = len(self.q[ins.eng])
            self.q[ins.eng].append(ins)
            return
        e = ins.eng
        rs = max(self.now + 1, self.last_rs.get(e, 0))
        for (d, raw) in ins.deps:
            drs = self.rs_of.get(id(d))
            if drs is not None:
                rs = max(rs, drs + (0 if d.eng == e and not d.is_dma else self.delta))
        lim = self.RATE.get(e)
        if lim is not None:
            while self.slots.get((e, rs), 0) >= lim:
                rs += 1
            self.slots[(e, rs)] = self.slots.get((e, rs), 0) + 1
        self.last_rs[e] = rs
        self.rs_of[id(ins)] = rs
        self.pending[e].append((rs, ins))

    def step(self):
        self.now += 1
        for e in self.ENGS:
            p = self.pending[e]
            k = 0
            while k < len(p) and p[k][0] <= self.now:
                ins = p[k][1]
                ins.idx = len(self.q[e])
                self.q[e].append(ins)
                k += 1
            if k:
                del p[:k]

    def flush(self):
        while any(self.pending[e] for e in self.ENGS):
            self.step()
        self.rs_of = {}
        self.slots = {}
        self.last_rs = {}

    def op(self, eng, fn, reads=(), writes=()):
        ins = _Ins(eng, fn)
        ins.ep = self.ep
        self._track(ins, reads, writes)
        self._place(ins)
        return ins

    def dma(self, queue, fn, key, reads=(), writes=(), inc=16):
        ins = _Ins(queue, fn, is_dma=True, key=key)
        ins.ep = self.ep
        ins.inc = inc
        self._track(ins, reads, writes)
        self.dma_keys[key] = self.dma_keys.get(key, 0) + (inc if inc else 1)
        ins.cum = self.dma_keys[key]
        self._place(ins)
        return ins

    def emit(self):
        nc = self.nc
        for e in self.ENGS:
            for ins in self.q[e]:
                need = []
                best = {}
                bestk = {}
                for (d, raw) in ins.deps:
                    if d.is_dma:
                        bk = bestk.get(d.key)
                        if bk is None or d.cum > bk.cum:
                            bestk[d.key] = d
                        continue
                    if d.eng == ins.eng and not ins.is_dma:
                        assert d.idx < ins.idx, "same-engine dependency placed after its consumer"
                        if e == "pe" or not raw:
                            continue
                    b = best.get(d.eng)
                    if b is None or d.idx > b.idx:
                        best[d.eng] = d
                for d in bestk.values():
                    need.append(d)
                for d in best.values():
                    need.append(d)
                ins.waits = need
        for e in self.ENGS:
            seen = {}
            for ins in self.q[e]:
                keep = []
                for d in ins.waits:
                    if d.is_dma:
                        k = ("k", d.key)
                        if seen.get(k, -1) >= d.cum:
                            continue
                        seen[k] = d.cum
                    else:
                        k = ("e", d.eng)
                        if seen.get(k, -1) >= d.idx:
                            continue
                        seen[k] = d.idx
                        d.sig = True
                    keep.append(d)
                ins.waits = keep
        eps = set()
        for e in self.ENGS:
            t = {}
            for ins in self.q[e]:
                if ins.sig and not ins.is_dma:
                    t[ins.ep] = t.get(ins.ep, 0) + 1
                    ins.tick = t[ins.ep]
                    eps.add((e, ins.ep))
        self.max_ticks = {}
        for e in self.ENGS:
            for ins in self.q[e]:
                if ins.tick:
                    self.max_ticks[(e, ins.ep)] = max(self.max_ticks.get((e, ins.ep), 0), ins.tick)
        self.n_ins = {e: len(self.q[e]) for e in self.ENGS}
        stack = contextlib.ExitStack()
        esem = {k: stack.enter_context(nc.semaphore("s_%s_%d" % k)) for k in sorted(eps)}
        ksem = {}
        for n, k in enumerate(self.dma_keys):
            ksem[k] = stack.enter_context(nc.semaphore("k%d" % n))
        q = self.q
        dma_keys = self.dma_keys

        def run(e, h):
            for ins in q[e]:
                for d in ins.waits:
                    if d.is_dma:
                        h.wait_ge(ksem[d.key], d.cum)
                    else:
                        h.wait_ge(esem[(d.eng, d.ep)], d.tick)
                bi = ins.fn(h)
                if ins.is_dma:
                    if ins.inc:
                        bi.then_inc(ksem[ins.key], ins.inc)
                    else:
                        bi.then_inc(ksem[ins.key])
                elif ins.sig:
                    bi.then_inc(esem[(e, ins.ep)], 1)
            if e == "sp":
                for k, v in dma_keys.items():
                    h.wait_ge(ksem[k], v)

        with stack:
            with nc.Block() as block:
                @block.sync
                def _(h):
                    run("sp", h)

                @block.tensor
                def _(h):
                    run("pe", h)

                @block.scalar
                def _(h):
                    run("act", h)

                @block.vector
                def _(h):
                    run("dve", h)

                @block.gpsimd
                def _(h):
                    run("pool", h)


def build_program(S, n_mix, has_prev, do_final, layer_ids, fused=False):
    assert S % TQ == 0
    NT = S // TQ
    n_out = (1 if has_prev else 0) + max(n_mix - 1, 0) + (1 if (do_final and n_mix > 0) else 0)
    if n_mix == 0:
        n_out = 1
    n_ada = n_out + n_mix if not fused else DEPTH
    nc = bass.Bass("TRN2", target_bir_lowering=False)
    dt_in = lambda n, s, d: nc.dram_tensor(n, s, d, kind="ExternalInput").ap()
    dt_out = lambda n, s, d: nc.dram_tensor(n, s, d, kind="ExternalOutput").ap()
    dt_int = lambda n, s, d: nc.dram_tensor(n, s, d, kind="Internal").ap()

    xT_in = dt_in("xT", [D, S], F32)
    cT_in = dt_in("cT", [128, 8], F32)
    cf_in = dt_in("cf", [128, CF_W], F32)
    cb_in = dt_in("cb", [128, CB_W], BF16)
    wada_in = dt_in("w_ada", [n_ada, D, 3 * D], F32)
    bada_in = dt_in("b_adaT", [n_ada, 128, 24], F32)
    if n_mix > 0:
        win_in = dt_in("w_in", [n_mix, D, NCOL], F32)
        pp_in = dt_in("pp", [n_mix, 128, NPP], F32)
        wgk_in = dt_in("wgk", [n_mix, 17, 64], F32)
    if n_out > 0:
        wout_in = dt_in("w_out", [n_out, D, D], F32)
    if has_prev or n_mix == 0:
        oT_in = dt_in("oT_in", [1, D, S], BF16)
    CW = min(2048, S) if fused else S
    NCH = S // CW
    TPC = CW // TQ
    if n_mix > 0:
        if fused:
            ohT = dt_int("ohT", [NCH, 256, CW], BF16)
            oT_fulls = [dt_int("oT_full%d" % k, [NCH, D, CW], BF16) for k in range(2)]
        else:
            ohT = dt_out("ohT", [NCH, 256, CW], BF16)
    if do_final:
        yT = dt_out("yT", [D, S], F32)
    xw = None
    if n_mix > 0 and n_out - (1 if do_final else 0) > 0:
        xw = dt_int("xw", [D, S], F32) if (fused or do_final) else dt_out("xT_out", [D, S], F32)

    def tview(ap, i):
        return ap.rearrange("(kc p) s -> p kc s", p=128)[:, :, i * TQ:(i + 1) * TQ]

    with contextlib.ExitStack() as st:
        def sb(n, s, d):
            return st.enter_context(nc.sbuf_tensor("sb_" + n, s, d))

        xt = sb("xt", [128, 8, TQ], F32)
        ot = sb("ot", [128, 8, TQ], BF16)
        hT = sb("hT", [128, 8, TQ], BF16)
        sqs = sb("sqs", [128, 2, TQ], BF16)
        wout = sb("wout", [128, 8, D], BF16)
        wstage = sb("wstage", [128, 2, D], F32)
        cf = sb("cf", [128, CF_W], F32)
        cb = sb("cb", [128, CB_W], BF16)
        cT = sb("cT", [128, 8], F32)
        cact = sb("cact", [128, 8], F32)
        ctmp = sb("ctmp", [128, 8], F32)
        modT = sb("modT", [128, n_ada, 24], F32)
        badaT = sb("badaT", [128, n_ada, 24], F32)
        FS = {k: sb("F%d" % k, [128, TQ], F32) for k in (0, 3, 4, 5, 6, 8, 9, 10, 11)}
        if n_mix > 0:
            win = sb("win", [128, 8, NCOL], BF16)
            KT = sb("KT", [128, S], BF16)
            V = sb("V", [128, S // 128, 129], BF16)
            QT = sb("QT", [128, 2, 2, TQ], BF16)
            FZ = [sb("FZ%d" % k, [128, TQ], F32) for k in range(2)]
            AS = [sb("AS%d" % k, [128, 4, 128], F32) for k in range(3)]
            ON = sb("ON", [128, 4, 128], BF16)
            rc = sb("rc", [128, 2, 4], F32)
            ssq = sb("ssq", [128, 8], F32)
            NPT = 6
            Pt = sb("Pt", [128, NPT, TQ], BF16)
            C0 = sb("C0", [128, TQ + 3], F32)
            C1 = sb("C1", [128, TQ + 3], F32)
            G0 = sb("G0", [32, TQ], F32)
            B0 = sb("B0", [128, TQ], BF16)
            L0 = sb("L0", [128, 256], F32)
            L1 = sb("L1", [128, 256], F32)
            kdec = sb("kdec", [128, 256], BF16)
            vtok = sb("vtok", [128, TQ], BF16)
            a8 = sb("a8", [64, 8], F32)
            Sring = sb("Sring", [64, 8, 128], F32)
            oh = sb("oh", [128, 2, 2, TQ], BF16)
            pp = sb("pp", [128, n_mix, NPP], F32)
            dp = sb("dp", [128, n_mix, 32], F32)
            wgk = sb("wgk", [32, n_mix, 64], F32)
            lamt = sb("lamt", [128, n_mix, 4], F32)
        gates = sb("gates", [128, max(n_out, 1), 8], F32)
        ps = st.enter_context(nc.psum_tensor("ps", [128, 8, TQ], F32))

        SCH = Sched(nc)
        ring = [0]

        def nb():
            ring[0] = (ring[0] + 1) % 4
            return ring[0]

        ringA = [0]
        ringB = [0]

        def nbA():
            ringA[0] = (ringA[0] + 1) % 3
            return ringA[0]

        def nbB():
            ringB[0] = (ringB[0] + 1) % 3
            return 3 + ringB[0]

        def bank(b):
            return ("bank", b)

        def mm(out, lhsT, rhs, start, stop, reads, writes):
            SCH.op("pe", lambda h: h.matmul(out, lhsT=lhsT, rhs=rhs, start=start, stop=stop,
                                            skip_group_check=True), reads, writes)

        def act(out, in_, func, reads, writes, scale=1.0, bias=0.0):
            SCH.op("act", lambda h: h.activation(out=out, in_=in_, func=func, bias=bias, scale=scale),
                   reads, writes)

        def tt(eng, out, in0, in1, op, reads, writes):
            SCH.op(eng, lambda h: h.tensor_tensor(out=out, in0=in0, in1=in1, op=op), reads, writes)

        def ts(eng, out, in0, s1, s2, op0, op1, reads, writes):
            if s2 is None:
                SCH.op(eng, lambda h: h.tensor_scalar(out=out, in0=in0, scalar1=s1, scalar2=None, op0=op0),
                       reads, writes)
            else:
                SCH.op(eng, lambda h: h.tensor_scalar(out=out, in0=in0, scalar1=s1, scalar2=s2, op0=op0, op1=op1),
                       reads, writes)

        def stt(eng, out, in0, scalar, in1, op0, op1, reads, writes):
            SCH.op(eng, lambda h: h.scalar_tensor_tensor(out=out, in0=in0, scalar=scalar, in1=in1, op0=op0, op1=op1),
                   reads, writes)

        def cp(eng, out, in_, reads, writes):
            if eng == "act":
                SCH.op("act", lambda h: h.copy(out=out, in_=in_), reads, writes)
            else:
                SCH.op(eng, lambda h: h.tensor_copy(out=out, in_=in_), reads, writes)

        def rsqrt_mean(dst, src_ps, reads, writes_name):
            act(dst, src_ps, AF.Ln, reads, [writes_name], scale=1.0, bias=EPS)
            act(dst, dst, AF.Exp, [writes_name], [writes_name], scale=-0.5)

        def sigmoid_neg(dst, src, reads, name):
            act(dst, src, AF.Exp, reads, [name], scale=-1.0)
            act(dst, dst, AF.Ln, [name], [name], scale=1.0, bias=1.0)
            act(dst, dst, AF.Exp, [name], [name], scale=-1.0)

        SCH.dma("sp", lambda h: h.dma_start(out=cf[:], in_=cf_in), "c_cf", writes=["cf"])
        SCH.dma("sp", lambda h: h.dma_start(out=cb[:], in_=cb_in), "c_cb", writes=["cb"])
        SCH.dma("sp", lambda h: h.dma_start(out=cT[:], in_=cT_in), "c_cT", writes=["cT"])
        SCH.dma("sp", lambda h: h.dma_start(out=badaT[:], in_=bada_in.rearrange("a p c -> p a c")), "c_bada",
                writes=["badaT"])
        if n_mix > 0:
            SCH.dma("sp", lambda h: h.dma_start(out=pp[:], in_=pp_in.rearrange("l p c -> p l c")), "c_pp", writes=["pp"])
            SCH.op("dve", lambda h: h.memset(wgk[:], 0.0), writes=["wgk"])
            SCH.dma("sp", lambda h: h.dma_start(out=wgk[0:17, :, :], in_=wgk_in.rearrange("l k c -> k l c")), "c_wgk",
                    reads=[], writes=["wgk"])
            SCH.op("dve", lambda h: h.memset(G0[:], 1.0), writes=["G0"])
            for par_ in range(2):
                SCH.op("pool", lambda h, par_=par_: h.memset(QT[:, par_, :, :], 0.0), writes=[("QT", par_)])
            SCH.op("pool", lambda h: h.memset(V[:, :, 128:129], 1.0), writes=["Vones"])
        sigmoid_neg(ctmp[:], cT[:], ["cT"], "ctmp")
        tt("dve", cact[:], cT[:], ctmp[:], ALU.mult, ["cT", "ctmp"], ["cact"])

        def emit_ada(a):
            for blk in range(6):
                SCH.dma("sp", lambda h, blk=blk: h.dma_start(
                    out=xt[:], in_=wada_in[a].rearrange("(kc p) n -> p kc n", p=128)[:, :, blk * 512:(blk + 1) * 512]),
                    "xl", writes=[("xt", dc) for dc in range(8)])
                for cc in range(4):
                    col = blk * 4 + cc
                    for kc in range(8):
                        mm(ps[:, 7, col:col + 1], xt[:, kc, cc * 128:(cc + 1) * 128], cact[:, kc:kc + 1],
                           kc == 0, kc == 7, [("xt", kc), "cact"], [bank(7)])
            tt("dve", modT[:, a, :], ps[:, 7, 0:24], badaT[:, a, :], ALU.add, [bank(7), "badaT"], [("modT", a)])

        astage = sb("astage", [128, 1, 8, 128], F32)

        def gen_ada_bg(a, cc):
            slot = 0
            SCH.dma("sp", lambda h: h.dma_start(
                out=astage[:, slot, :, :], in_=wada_in[a].rearrange("(kc p) n -> p kc n", p=128)[:, :, cc * 128:(cc + 1) * 128]),
                ("ast", slot), writes=[("astage", slot)])
            yield
            b = nbA()
            for kc in range(8):
                mm(ps[:, b, 0:1], astage[:, slot, kc, :], cact[:, kc:kc + 1], kc == 0, kc == 7,
                   [("astage", slot), "cact"], [bank(b)])
            tt("dve", modT[:, a, cc:cc + 1], ps[:, b, 0:1], badaT[:, a, cc:cc + 1], ALU.add, [bank(b), "badaT"], [("modT", a)])
            yield

        def load_weight(dst, src2d, ncols, resname):
            for kc in range(8):
                slot = kc % 2
                SCH.dma("sp", lambda h, kc=kc, slot=slot: h.dma_start(
                    out=wstage[:, slot, 0:ncols], in_=src2d[kc * 128:(kc + 1) * 128, :]),
                    ("wst", slot), writes=[("wstage", slot)])
                cp("pool", dst[:, kc, :], wstage[:, slot, 0:ncols], [("wstage", slot)], [resname])

        def emit_load_x(i, src, src_name):
            SCH.dma("sp", lambda h: h.dma_start(out=xt[:], in_=tview(src, i)), "xl",
                    reads=[(src_name, i)], writes=[("xt", dc) for dc in range(8)])

        def gen_outproj(i, o_src, o_name, gate_idx, dst, dst_name):
            SCH.dma("sp", lambda h: h.dma_start(out=ot[:], in_=tview(o_src[i // TPC], i % TPC)), "ol",
                    reads=[(o_name, i)], writes=["ot"])
            yield
            for dc in range(8):
                b = nbA()
                for kc in range(8):
                    mm(ps[:, b, :], wout[:, kc, dc * 128:(dc + 1) * 128], ot[:, kc, :], kc == 0, kc == 7,
                       ["ot", "wout"], [bank(b)])
                stt("dve", xt[:, dc, :], ps[:, b, :], gates[:, gate_idx, dc:dc + 1], xt[:, dc, :], ALU.mult, ALU.add,
                    [bank(b), ("xt", dc), ("gates", gate_idx)], [("xt", dc)])
                yield
            SCH.dma("pool", lambda h: h.dma_start(out=tview(dst, i), in_=xt[:]), "xs",
                    reads=[("xt", dc) for dc in range(8)], writes=[(dst_name, i)])

        def gen_A(i, li, x_src, x_src_name, prev_args, extra=None):
            ppl = lambda c0, c1: pp[:, li, c0:c1]
            dpl = lambda c0, c1: dp[:, li, c0:c1]
            t0 = i * TQ
            par = i % 2
            emit_load_x(i, x_src, x_src_name)
            if prev_args is not None:
                yield from gen_outproj(i, *prev_args)
            bx = nbA()
            for kc in range(8):
                sl = kc % 2
                tt("pool", sqs[:, sl, :], xt[:, kc, :], xt[:, kc, :], ALU.mult, [("xt", kc)], [("sqs", sl)])
                mm(ps[:, bx, :], cb[:, CB_X:CB_X + 128], sqs[:, sl, :], kc == 0, kc == 7, [("sqs", sl), "cb"], [bank(bx)])
                if kc % 2 == 1:
                    yield
            rsqrt_mean(FS[0][:], ps[:, bx, :], [bank(bx)], "F0")
            yield
            for kc in range(8):
                fa = 10 + (kc % 2)
                tt("dve", FS[fa][:], xt[:, kc, :], FS[0][:], ALU.mult, [("xt", kc), "F0"], ["F%d" % fa])
                ts("pool", hT[:, kc, :], FS[fa][:], dpl(kc, kc + 1), dpl(8 + kc, 9 + kc), ALU.mult, ALU.add,
                   ["F%d" % fa, ("dp", li)], [("hT", kc)])
                if kc % 2 == 1:
                    yield

            def proj(col0, M, b):
                for kc in range(8):
                    mm(ps[0:M, b, :], win[:, kc, col0:col0 + M], hT[:, kc, :], kc == 0, kc == 7,
                       [("hT", kc), "win"], [bank(b)])

            for which, col0 in (("q", C_DQ), ("k", C_DK)):
                b = nbA()
                proj(col0, 128, b)
                yield
                act(B0[:], ps[:, b, :], AF.Square, [bank(b)], ["B0"])
                b5 = nbA()
                mm(ps[:, b5, :], cb[:, CB_BLK:CB_BLK + 128], B0[:], True, True, ["B0", "cb"], [bank(b5)])
                yield
                rsqrt_mean(FS[4][:], ps[:, b5, :], [bank(b5)], "F4")
                if which == "q":
                    for mq in range(2):
                        rs = slice(64 * mq, 64 * mq + 64)
                        stt("dve", QT[rs, par, mq, :], ps[rs, b, :], dp[rs, li, 24:25], FS[4][rs, :], ALU.mult, ALU.mult,
                            [bank(b), "F4", ("dp", li)], [("QT", par)])
                else:
                    stt("dve", KT[:, t0:t0 + TQ], ps[:, b, :], ppl(18, 19), FS[4][:], ALU.mult, ALU.mult,
                        [bank(b), "F4", "pp"], [("KT", i)])
                yield
            for col0, dst, dname, fe in ((C_DZ, FZ[par], ("FZ", par), 5), (C_GZ, FS[3], "F3", 6)):
                b = nbA()
                proj(col0, 128, b)
                yield
                sigmoid_neg(FS[fe][:], ps[:, b, :], [bank(b)], "F%d" % fe)
                tt("dve", dst[:], ps[:, b, :], FS[fe][:], ALU.mult, [bank(b), "F%d" % fe], [dname])
                yield
            b = nbA()
            proj(C_GV, 128, b)
            cp("dve", C1[:, 3:3 + TQ], ps[:, b, :], [bank(b)], ["C1"])
            yield
            b = nbA()
            proj(C_GQK, 128, b)
            cp("dve", C0[:, 3:3 + TQ], ps[:, b, :], [bank(b)], ["C0"])
            yield
            b = nbA()
            proj(C_GLR, 16, b)
            cp("dve", G0[0:16, :], ps[0:16, b, :], [bank(b)], ["G0"])
            yield
            b = nbA()
            for j in range(4):
                for kc in range(8):
                    mm(ps[:, b, j * 128:(j + 1) * 128], hT[:, kc, j * 128:(j + 1) * 128], win[:, kc, C_DV:C_DV + 128],
                       kc == 0, kc == 7, [("hT", kc), "win"], [bank(b)])
                if j % 2 == 1:
                    yield
            for j in range(4):
                cp("dve", V[:, 4 * i + j, 0:128], ps[:, b, j * 128:(j + 1) * 128], [bank(b)], [("V", i)])
            yield

            for (Cb, cname, w0, fo, fe) in ((C0, "C0", 8, 8, 4), (C1, "C1", 12, 9, 5)):
                fon = "F%d" % fo
                ts("dve", FS[fo][:], Cb[:, 0:TQ], ppl(w0, w0 + 1), None, ALU.mult, None, [cname, "pp"], [fon])
                for j in range(1, 4):
                    stt("dve", FS[fo][:], Cb[:, j:j + TQ], ppl(w0 + j, w0 + j + 1), FS[fo][:], ALU.mult, ALU.add,
                        [cname, "pp", fon], [fon])
                cp("pool", Cb[:, 0:3], Cb[:, TQ:TQ + 3], [cname], [cname])
                yield
                sigmoid_neg(FS[fe][:], FS[fo][:], [fon], "F%d" % fe)
                tt("dve", FS[fo][:], FS[fo][:], FS[fe][:], ALU.mult, [fon, "F%d" % fe], [fon])
                yield
            sqk, sv = FS[8], FS[9]
            bg = nbA()
            for j in range(4):
                mm(ps[:, bg, j * 64:(j + 1) * 64], G0[0:17, j * 128:(j + 1) * 128], wgk[0:17, li, :], True, True,
                   ["G0", "wgk"], [bank(bg)])
            yield
            act(L0[:], ps[:, bg, 0:256], AF.Exp, [bank(bg)], ["L0"], scale=-1.0)
            act(L0[:], L0[:], AF.Ln, ["L0"], ["L0"], scale=1.0, bias=1.0)
            yield
            bd = nbA()
            for j in range(4):
                mm(ps[:, bd, j * 64:(j + 1) * 64], cf[:, CF_U:CF_U + 128], L0[:, j * 64:(j + 1) * 64], True, True,
                   ["cf", "L0"], [bank(bd)])
            yield
            act(L1[:], ps[:, bd, 0:256], AF.Exp, [bank(bd)], ["L1"])
            bk = nbA()
            for j in range(4):
                mm(ps[:, bk, j * 64:(j + 1) * 64], sqk[64:128, j * 128:(j + 1) * 128], cf[64:128, CF_ID + 64:CF_ID + 128],
                   True, True, ["F8", "cf"], [bank(bk)])
            yield
            tt("dve", kdec[:], ps[:, bk, 0:256], L1[:], ALU.mult, [bank(bk), "L1"], ["kdec"])
            bv = nbA()
            for j in range(4):
                mm(ps[:, bv, j * 128:(j + 1) * 128], sv[:, j * 128:(j + 1) * 128], cf[:, CF_ID:CF_ID + 128],
                   True, True, ["F9", "cf"], [bank(bv)])
            yield
            cp("dve", vtok[:], ps[:, bv, :], [bank(bv)], ["vtok"])
            bb = nbA()
            for j in range(4):
                mm(ps[0:64, bb, 2 * j:2 * j + 2], L0[:, j * 64:(j + 1) * 64], cf[:, CF_CIND:CF_CIND + 2], True, True,
                   ["L0", "cf"], [bank(bb)])
            yield
            act(a8[:], ps[0:64, bb, 0:8], AF.Exp, [bank(bb)], ["a8"])
            ubs = [nbA(), nbA()]
            for c in range(8):
                j, hh = c // 2, c % 2
                ub = ubs[c % 2]
                col = (c // 2) * 128
                mm(ps[0:64, ub, col:col + 128], kdec[64 * hh:64 * hh + 64, j * 64:(j + 1) * 64],
                   vtok[64 * hh:64 * hh + 64, j * 128:(j + 1) * 128], True, True, ["kdec", "vtok"], [bank(ub)])
            yield
            obk = nbA()
            for c in range(8):
                ub = ubs[c % 2]
                col = (c // 2) * 128
                pc = (c - 1) % 8
                stt("dve", Sring[:, c, :], Sring[:, pc, :], a8[:, c:c + 1], ps[0:64, ub, col:col + 128], ALU.mult, ALU.add,
                    [("S", pc), "a8", bank(ub)], [("S", c)])
                mm(ps[:, obk, c * 64:(c + 1) * 64], Sring[:, c, :], sqk[0:64, c * 64:(c + 1) * 64], True, True,
                   [("S", c), "F8"], [bank(obk)])
                if c % 2 == 1:
                    yield
            act(B0[:], ps[:, obk, :], AF.Square, [bank(obk)], ["B0"])
            bs = nbA()
            mm(ps[:, bs, :], cb[:, CB_G:CB_G + 128], B0[:], True, True, ["B0", "cb"], [bank(bs)])
            yield
            rsqrt_mean(FS[4][:], ps[:, bs, :], [bank(bs)], "F4")
            stt("dve", FS[5][:], ps[:, obk, :], dpl(26, 27), FS[4][:], ALU.mult, ALU.mult, [bank(obk), "F4", ("dp", li)], ["F5"])
            tt("pool", oh[:, par, 0, :], FS[5][:], FS[3][:], ALU.mult, ["F5", "F3"], [("oh", par, 0)])
            yield
            if extra is not None:
                yield from extra

        pslot = [0]

        def gen_B(i, li):
            dpl = lambda c0, c1: dp[:, li, c0:c1]
            par = i % 2
            nkt = 4 * (i + 1)
            LA = 2
            OB, SBK = 6, 7
            infos = [{}, {}]

            def emit_st(m, kt):
                diag = kt >= 4 * i
                q0 = 128 * (kt - 4 * i) if diag else 0
                b = nbB()
                mm(ps[:, b, q0:TQ], KT[:, kt * 128:(kt + 1) * 128],
                   QT[:, par, m, q0:TQ], True, True, [("KT", kt // 4), ("QT", par)], [bank(b)])
                infos[m][kt] = (b, q0, diag)

            OBX = (6, 7)

            def finalize_map(m):
                for qs in range(4):
                    bk, c0 = OBX[qs // 2], (qs % 2) * 129
                    SCH.op("dve", lambda h, bk=bk, c0=c0, qs=qs: h.reciprocal(out=rc[:, m, qs:qs + 1], in_=ps[:, bk, c0 + 128:c0 + 129]),
                           [bank(bk)], [("rc", m)])
                for qs in range(4):
                    bk, c0 = OBX[qs // 2], (qs % 2) * 129
                    ts("dve", AS[m][:, qs, :], ps[:, bk, c0:c0 + 128], rc[:, m, qs:qs + 1], None, ALU.mult, None,
                       [bank(bk), ("rc", m)], ["A%d" % m])

            for kt in range(min(LA, nkt)):
                emit_st(0, kt)
            for m in range(2):
                for kt in range(nkt):
                    b, q0, diag = infos[m].pop(kt)
                    sl = pslot[0]
                    pslot[0] = (sl + 1) % NPT
                    if not diag:
                        act(Pt[:, sl, q0:TQ], ps[:, b, q0:TQ], AF.Exp, [bank(b)], [("Pt", sl)])
                    else:
                        act(Pt[0:64, sl, q0:TQ], ps[0:64, b, q0:TQ], AF.Exp, [bank(b)], [("Pt", sl)])
                        if q0 + 64 < TQ:
                            act(Pt[64:128, sl, q0 + 64:TQ], ps[64:128, b, q0 + 64:TQ], AF.Exp, [bank(b)], [("Pt", sl)])
                        cp("act", Pt[64:128, sl, q0:q0 + 64], cb[64:128, CB_BLK:CB_BLK + 64], ["cb"], [("Pt", sl)])
                    for qs in range(q0 // 128, 4):
                        bk, c0 = OBX[qs // 2], (qs % 2) * 129
                        mm(ps[:, bk, c0:c0 + 129], Pt[:, sl, qs * 128:(qs + 1) * 128], V[:, kt, :],
                           (kt == 0 and qs % 2 == 0), False, [("V", kt // 4), "Vones", ("Pt", sl)], [bank(bk)])
                    if kt + LA < nkt:
                        emit_st(m, kt + LA)
                    elif m == 0 and (kt + LA - nkt) < min(LA, nkt):
                        emit_st(1, kt + LA - nkt)
                    yield
                finalize_map(m)
                yield
            stt("dve", AS[0][:], AS[1][:], dpl(27, 28), AS[0][:], ALU.mult, ALU.add, ["A1", "A0", ("dp", li)], ["A0"])
            tt("dve", AS[2][:], AS[0][:], AS[0][:], ALU.mult, ["A0"], ["A2"])
            SCH.op("dve", lambda h: h.tensor_reduce(out=ssq[:, 0:4], in_=AS[2][:], axis=mybir.AxisListType.X, op=ALU.add),
                   ["A2"], ["ssq"])
            yield
            act(ssq[:, 4:8], ssq[:, 0:4], AF.Ln, ["ssq"], ["ssq2"], scale=1.0 / 128.0, bias=EPS)
            act(ssq[:, 4:8], ssq[:, 4:8], AF.Exp, ["ssq2"], ["ssq2"], scale=-0.5)
            for qs in range(4):
                ts("dve", ON[:, qs, :], AS[0][:, qs, :], ssq[:, 4 + qs:5 + qs], None, ALU.mult, None, ["A0", "ssq2"], ["ON"])
            bs = nbB()
            for qs in range(4):
                mm(ps[:, bs, qs * 128:(qs + 1) * 128], ON[:, qs, :], cb[:, CB_ID:CB_ID + 128], True, True,
                   ["ON", "cb"], [bank(bs)])
            yield
            stt("dve", oh[:, par, 1, :], ps[:, bs, :], dpl(25, 26), FZ[par][:], ALU.mult, ALU.mult,
                [bank(bs), ("dp", li), ("FZ", par)], [("oh", par, 1)])
            tc0 = (i % TPC) * TQ
            SCH.dma("pool", lambda h: h.dma_start(
                out=ohT[i // TPC].rearrange("(g p) s -> p g s", p=128)[:, :, tc0:tc0 + TQ], in_=oh[:, par, :, :]),
                ("ohs", par), reads=[("oh", par, 0), ("oh", par, 1)], writes=[("ohT", i)])
            if fused and (i % TPC) == TPC - 1:
                c = i // TPC
                k = li % 2
                SCH.dma("pool", lambda h, k=k, c=c: h.collective_compute(
                    "AllGather", ALU.bypass, replica_groups=[[0, 1, 2, 3], [4, 5, 6, 7]],
                    ins=[ohT[c]], outs=[oT_fulls[k][c]]),
                    ("cc", li, c), reads=[("ohT", ii) for ii in range(c * TPC, (c + 1) * TPC)],
                    writes=[(("ofull", k), ii) for ii in range(c * TPC, (c + 1) * TPC)], inc=None)
            yield

        NA_EST = [62]

        def drain(g):
            n = 0
            for _ in g:
                n += 1
            return n

        def run_layer_tiles(li, x_src, x_src_name, prev_args, extra_of, tail_gen):
            for i in range(NT):
                gB = gen_B(i, li)
                hold = 0
                if i + 1 < NT:
                    gA = gen_A(i + 1, li, x_src, x_src_name, prev_args, extra_of(i + 1))
                else:
                    gA = tail_gen
                    hold = 4 * (i + 1) + 6
                step = 0
                nB = 8 * (i + 1) - hold
                SCH.RATE = {"pe": max(1, -(-230 // max(nB, 1))), "act": max(1, -(-60 // max(nB, 1)))}
                if gA is not None and hold == 0:
                    SCH.defer = True
                    drain(gA)
                    SCH.defer = False
                    gA = None
                for _ in gB:
                    step += 1
                    SCH.step()
                    if gA is not None and step == hold:
                        SCH.defer = True
                        drain(gA)
                        SCH.defer = False
                        gA = None
                if gA is not None:
                    drain(gA)
                SCH.flush()

        def emit_layer_params(li, a_idx, lam_init):
            ppl = lambda c0, c1: pp[:, li, c0:c1]
            dpl = lambda c0, c1: dp[:, li, c0:c1]
            r = [("dp", li)]
            stt("dve", dpl(0, 8), modT[:, a_idx, 8:16], 1.0, ppl(0, 8), ALU.add, ALU.mult, [("modT", a_idx), "pp"], r)
            cp("dve", dpl(8, 16), modT[:, a_idx, 0:8], [("modT", a_idx)], r)
            ts("dve", dpl(24, 25), ppl(17, 18), 0.125, None, ALU.mult, None, ["pp"], r)
            ts("dve", dpl(25, 26), ppl(19, 20), 1.0 - lam_init, None, ALU.mult, None, ["pp"], r)
            ts("dve", dpl(26, 27), ppl(16, 17), 0.125, None, ALU.mult, None, ["pp"], r)
            tt("dve", lamt[:, li, 0:1], ppl(20, 21), ppl(21, 22), ALU.mult, ["pp"], [("lamt", li)])
            tt("dve", lamt[:, li, 1:2], ppl(22, 23), ppl(23, 24), ALU.mult, ["pp"], [("lamt", li)])
            bl = nbA()
            mm(ps[:, bl, 32:34], cf[:, CF_ONE:CF_ONE + 128], lamt[:, li, 0:2], True, True, [("lamt", li), "cf"], [bank(bl)])
            act(lamt[:, li, 2:4], ps[:, bl, 32:34], AF.Exp, [bank(bl)], [("lamt2", li)])
            tt("dve", dpl(27, 28), lamt[:, li, 3:4], lamt[:, li, 2:3], ALU.subtract, [("lamt2", li)], r)
            ts("dve", dpl(27, 28), dpl(27, 28), -lam_init, None, ALU.add, None, r, r)

        assert fused and n_mix >= 1 and do_final and not has_prev
        emit_ada(layer_ids[0])

        def layer_ctx(li):
            l = layer_ids[li]
            if li == 0:
                return xT_in, "xin", None
            o_src, o_name = oT_fulls[(li - 1) % 2], ("ofull", (li - 1) % 2)
            xs, xn = (xT_in, "xin") if li == 1 else (xw, "xw")
            return xs, xn, (o_src, o_name, li - 1, xw, "xw")

        def extra_of_layer(li):
            def f(i):
                if li + 1 >= n_mix:
                    return None
                lo, hi = (i * 24) // NT, ((i + 1) * 24) // NT
                if lo == hi:
                    return None

                def g():
                    for cc in range(lo, hi):
                        yield from gen_ada_bg(layer_ids[li + 1], cc)
                return g()
            return f

        def gen_layer_start(li):
            l = layer_ids[li]
            lam_init = 0.8 - 0.6 * math.exp(-0.3 * l)
            xs, xn, prev_args = layer_ctx(li)
            if prev_args is not None:
                load_weight(wout, wout_in[li - 1], D, "wout")
                cp("dve", gates[:, li - 1, :], modT[:, layer_ids[li - 1], 16:24], [("modT", layer_ids[li - 1])],
                   [("gates", li - 1)])
                yield
            load_weight(win, win_in[li], NCOL, "win")
            yield
            emit_layer_params(li, l, lam_init)
            SCH.op("dve", lambda h: h.memset(Sring[:, 7, :], 0.0), writes=[("S", 7)])
            SCH.op("dve", lambda h: h.memset(C0[:, 0:3], 0.0), writes=["C0"])
            SCH.op("dve", lambda h: h.memset(C1[:, 0:3], 0.0), writes=["C1"])
            yield
            yield from gen_A(0, li, xs, xn, prev_args, extra_of_layer(li)(0))

        def gen_final_start():
            l_last = layer_ids[n_mix - 1]
            load_weight(wout, wout_in[n_mix - 1], D, "wout")
            cp("dve", gates[:, n_mix - 1, :], modT[:, l_last, 16:24], [("modT", l_last)], [("gates", n_mix - 1)])
            yield
            xs, xn = (xT_in, "xin") if n_mix == 1 else (xw, "xw")
            emit_load_x(0, xs, xn)
            yield from gen_outproj(0, oT_fulls[(n_mix - 1) % 2], ("ofull", (n_mix - 1) % 2), n_mix - 1, yT, "yT")

        SCH.ep = 1
        n0 = drain(gen_layer_start(0))
        NA_EST[0] = max(n0 - 3, 40)
        for li in range(n_mix):
            SCH.ep = li + 1
            xs, xn, prev_args = layer_ctx(li)
            tail = gen_layer_start(li + 1) if li + 1 < n_mix else gen_final_start()
            if NCH > 1 and 'nooverlap' not in DBG:
                run_layer_tiles(li, xs, xn, prev_args, extra_of_layer(li), tail)
            else:
                run_layer_tiles(li, xs, xn, prev_args, extra_of_layer(li), None)
                drain(tail)
        SCH.ep = n_mix + 1
        xs, xn = (xT_in, "xin") if n_mix == 1 else (xw, "xw")
        fin_args = (oT_fulls[(n_mix - 1) % 2], ("ofull", (n_mix - 1) % 2), n_mix - 1, yT, "yT")
        for i in range(1, NT):
            emit_load_x(i, xs, xn)
            drain(gen_outproj(i, *fin_args))
        SCH.emit()
    return nc


def _consts():
    cf = np.zeros((128, CF_W), np.float32)
    cf[:, CF_ID:CF_ID + 128] = np.eye(128, dtype=np.float32)
    s = np.arange(128)[:, None]
    t = np.arange(128)[None, :]
    cf[:, CF_U:CF_U + 128] = np.where((s > t) & (s // 64 == t // 64), -1.0 / 16.0, 0.0)
    cf[:, CF_ONE:CF_ONE + 128] = 1.0
    cf[:, CF_CIND:CF_CIND + 2] = np.where(s // 64 == np.arange(2)[None, :], -1.0 / 16.0, 0.0)
    cbm = np.zeros((128, CB_W), np.float32)
    cbm[:, CB_X:CB_X + 128] = 1.0 / 1024.0
    cbm[:, CB_BLK:CB_BLK + 128] = np.where(s // 64 == t // 64, 1.0 / 64.0, 0.0)
    cbm[:, CB_D:CB_D + 128] = 1.0 / 128.0
    cbm[:, CB_G:CB_G + 128] = 1.0 / 8192.0
    cbm[:, CB_ONE:CB_ONE + 128] = 1.0
    cbm[:, CB_ID:CB_ID + 128] = np.eye(128, dtype=np.float32)
    return cf, cbm.astype(ml_dtypes.bfloat16)


def _head_cols(h):
    gq = np.arange(0, 64) + h * 64
    gk = 256 + np.arange(0, 64) + h * 64
    gv = 512 + np.arange(0, 128) + h * 128
    glr = 1024 + np.arange(16)
    gz = 1040 + np.arange(128) + h * 128
    dq = 1552 + np.arange(128) + h * 128
    dk = 2064 + np.arange(128) + h * 128
    dv = 2576 + np.arange(128) + h * 128
    dz = 3088 + np.arange(128) + h * 128
    return np.concatenate([dq, dk, dz, gz, gv, gq, gk, dv, glr])


def _pp(inp, l, h):
    p = np.zeros((128, NPP), np.float32)
    p[:, 0:8] = inp["norm_g"][l].reshape(8, 128).T
    cw = inp["conv_w"][l]
    ch_qk = np.concatenate([np.arange(64) + h * 64, 256 + np.arange(64) + h * 64])
    p[:, 8:12] = cw[:, ch_qk].T
    p[:, 12:16] = cw[:, 512 + h * 128 + np.arange(128)].T
    p[:, 16] = inp["gla_norm_g"][l]
    p[:, 17] = np.tile(inp["qn_g"][l], 2)
    p[:, 18] = np.tile(inp["kn_g"][l], 2)
    p[:, 19] = inp["diff_norm_g"][l]
    p[0:64, 20] = inp["lam_q1"][l]
    p[0:64, 21] = inp["lam_k1"][l]
    p[0:64, 22] = inp["lam_q2"][l]
    p[0:64, 23] = inp["lam_k2"][l]
    return p


def _wgk(inp, l, h):
    w = np.zeros((17, 64), np.float32)
    w[0:16] = inp["w_gk"][l][:, h * 64:(h + 1) * 64]
    w[16] = inp["b_gk"][l][h * 64:(h + 1) * 64]
    return w


def _wout_perm(inp, l):
    rows = np.concatenate([np.concatenate([np.arange(128) + h * 128, 512 + np.arange(128) + h * 128]) for h in range(4)])
    return np.ascontiguousarray(inp["w_out"][l][rows, :])


_PROG_CACHE = {}


def _get_prog(S, n_mix, has_prev, do_final, layer_ids, fused=False):
    key = (S, n_mix, has_prev, do_final, tuple(layer_ids), fused)
    if key not in _PROG_CACHE:
        _PROG_CACHE[key] = build_program(S, n_mix, has_prev, do_final, layer_ids, fused)
    return _PROG_CACHE[key]


def run_unfused(inp, S):
    cf, cbm = _consts()
    B = inp["x"].shape[0]
    xT = [np.ascontiguousarray(inp["x"][b, :S].T) for b in range(B)]
    cT = [np.ascontiguousarray(inp["c"][b].reshape(8, 128).T) for b in range(B)]
    oT = None
    for l in range(DEPTH):
        has_prev = l > 0
        nc = _get_prog(S, 1, has_prev, False, [l])
        in_maps = []
        for core in range(8):
            b, h = core // 4, core % 4
            m = {"xT": xT[b], "cT": cT[b], "cf": cf, "cb": cbm,
                 "w_in": np.ascontiguousarray(inp["w_in"][l][:, _head_cols(h)])[None],
                 "pp": _pp(inp, l, h)[None], "wgk": _wgk(inp, l, h)[None]}
            if has_prev:
                m["w_ada"] = np.ascontiguousarray(inp["w_ada"][l - 1:l + 1])
                m["b_adaT"] = np.stack([inp["b_ada"][k].reshape(24, 128).T for k in (l - 1, l)])
                m["w_out"] = _wout_perm(inp, l - 1)[None]
                m["oT_in"] = oT[b][None]
            else:
                m["w_ada"] = np.ascontiguousarray(inp["w_ada"][l:l + 1])
                m["b_adaT"] = np.stack([inp["b_ada"][l].reshape(24, 128).T])
            in_maps.append(m)
        res = run_bass_kernel_spmd(nc, in_maps, core_ids=list(range(8))).results
        oT = [np.concatenate([res[b * 4 + h]["ohT"][0] for h in range(4)], axis=0) for b in range(B)]
        if has_prev:
            xT = [res[b * 4]["xT_out"] for b in range(B)]
    nc = _get_prog(S, 0, True, True, [DEPTH - 1])
    in_maps = []
    for core in range(8):
        b = core // 4
        in_maps.append({"xT": xT[b], "cT": cT[b], "cf": cf, "cb": cbm,
                        "w_ada": np.ascontiguousarray(inp["w_ada"][DEPTH - 1:DEPTH]),
                        "b_adaT": np.stack([inp["b_ada"][DEPTH - 1].reshape(24, 128).T]),
                        "w_out": _wout_perm(inp, DEPTH - 1)[None], "oT_in": oT[b][None]})
    res = run_bass_kernel_spmd(nc, in_maps, core_ids=list(range(8))).results
    out = np.stack([res[b * 4]["yT"].T for b in range(B)])
    return np.ascontiguousarray(out)


def fused_in_maps(inp, S):
    cf, cbm = _consts()
    B = inp["x"].shape[0]
    xT = [np.ascontiguousarray(inp["x"][b, :S].T) for b in range(B)]
    cT = [np.ascontiguousarray(inp["c"][b].reshape(8, 128).T) for b in range(B)]
    w_ada = np.ascontiguousarray(inp["w_ada"])
    b_adaT = np.stack([inp["b_ada"][l].reshape(24, 128).T for l in range(DEPTH)])
    w_out = np.stack([_wout_perm(inp, l) for l in range(DEPTH)])
    in_maps = []
    for core in range(8):
        b, h = core // 4, core % 4
        cols = _head_cols(h)
        in_maps.append({
            "xT": xT[b], "cT": cT[b], "cf": cf, "cb": cbm, "w_ada": w_ada, "b_adaT": b_adaT,
            "w_in": np.stack([inp["w_in"][l][:, cols] for l in range(DEPTH)]),
            "pp": np.stack([_pp(inp, l, h) for l in range(DEPTH)]),
            "wgk": np.stack([_wgk(inp, l, h) for l in range(DEPTH)]),
            "w_out": w_out})
    return in_maps


def run_fused(inp, S):
    B = inp["x"].shape[0]
    nc = _get_prog(S, DEPTH, False, True, list(range(DEPTH)), True)
    res = run_bass_kernel_spmd(nc, fused_in_maps(inp, S), core_ids=list(range(8))).results
    return np.ascontiguousarray(np.stack([res[b * 4]["yT"].T for b in range(B)]))


def kernel(**inputs):
    inp = {k: np.asarray(v) for k, v in inputs.items()}
    return run_fused(inp, inp["x"].shape[1])
```

```python
import contextlib
import math

import ml_dtypes
import numpy as np

import concourse.bass as bass
import concourse.mybir as mybir
from concourse.bass_utils import run_bass_kernel_spmd

F32 = mybir.dt.float32
BF16 = mybir.dt.bfloat16
AF = mybir.ActivationFunctionType
ALU = mybir.AluOpType

D = 1024
TQ = 512
NCOL = 912
EPS = 1e-6
DEPTH = 4
C_DQ, C_DK, C_DZ, C_GZ, C_GV, C_GQK, C_DV, C_GLR = 0, 128, 256, 384, 512, 640, 768, 896
CF_ID, CF_U, CF_ONE, CF_CIND = 0, 128, 256, 384
CF_W = 386
CB_X, CB_BLK, CB_D, CB_G, CB_ONE, CB_ID = 0, 128, 256, 384, 512, 640
CB_W = 768
NPP = 24
DBG = set()


class _Ins:
    __slots__ = ("eng", "fn", "deps", "idx", "is_dma", "sig", "tick", "key", "cum", "waits", "ep", "inc")

    def __init__(self, eng, fn, is_dma=False, key=None):
        self.eng = eng
        self.fn = fn
        self.deps = []
        self.idx = -1
        self.is_dma = is_dma
        self.sig = False
        self.tick = 0
        self.key = key
        self.cum = 0
        self.waits = []
        self.ep = 0
        self.inc = 16


class Sched:
    ENGS = ("sp", "pe", "act", "dve", "pool")

    def __init__(self, nc):
        self.nc = nc
        self.q = {e: [] for e in self.ENGS}
        self.res = {}
        self.dma_keys = {}
        self.ep = 0
        self.defer = False
        self.now = 0
        self.delta = 3
        self.pending = {e: [] for e in self.ENGS}
        self.rs_of = {}
        self.slots = {}
        self.last_rs = {}

    def _track(self, ins, reads, writes):
        deps = {}
        for r in reads:
            st = self.res.get(r)
            if st is not None and st[0] is not None:
                deps[id(st[0])] = (st[0], True)
        for w in writes:
            st = self.res.get(w)
            if st is not None:
                if st[0] is not None and id(st[0]) not in deps:
                    deps[id(st[0])] = (st[0], False)
                for rd in st[1]:
                    if id(rd) not in deps:
                        deps[id(rd)] = (rd, False)
        for r in reads:
            st = self.res.setdefault(r, [None, []])
            st[1].append(ins)
        for w in writes:
            self.res[w] = [ins, []]
        deps.pop(id(ins), None)
        ins.deps = list(deps.values())

    RATE = {"pe": 2, "act": 1}

    def _place(self, ins):
        if not self.defer:
            ins.idx = len(self.q[ins.eng])
            self.q[ins.eng].append(ins)
            return
        e = ins.eng
        rs = max(self.now + 1, self.last_rs.get(e, 0))
        for (d, raw) in ins.deps:
            drs = self.rs_of.get(id(d))
            if drs is not None:
                rs = max(rs, drs + (0 if d.eng == e and not d.is_dma else self.delta))
        lim = self.RATE.get(e)
        if lim is not None:
            while self.slots.get((e, rs), 0) >= lim:
                rs += 1
            self.slots[(e, rs)] = self.slots.get((e, rs), 0) + 1
        self.last_rs[e] = rs
        self.rs_of[id(ins)] = rs
        self.pending[e].append((rs, ins))

    def step(self):
        self.now += 1
        for e in self.ENGS:
            p = self.pending[e]
            k = 0
            while k < len(p) and p[k][0] <= self.now:
                ins = p[k][1]
                ins.idx = len(self.q[e])
                self.q[e].append(ins)
                k += 1
            if k:
                del p[:k]

    def flush(self):
        while any(self.pending[e] for e in self.ENGS):
            self.step()
        self.rs_of = {}
        self.slots = {}
        self.last_rs = {}

    def op(self, eng, fn, reads=(), writes=()):
        ins = _Ins(eng, fn)
        ins.ep = self.ep
        self._track(ins, reads, writes)
        self._place(ins)
        return ins

    def dma(self, queue, fn, key, reads=(), writes=(), inc=16):
        ins = _Ins(queue, fn, is_dma=True, key=key)
        ins.ep = self.ep
        ins.inc = inc
        self._track(ins, reads, writes)
        self.dma_keys[key] = self.dma_keys.get(key, 0) + (inc if inc else 1)
        ins.cum = self.dma_keys[key]
        self._place(ins)
        return ins

    def emit(self):
        nc = self.nc
        for e in self.ENGS:
            for ins in self.q[e]:
                need = []
                best = {}
                bestk = {}
                for (d, raw) in ins.deps:
                    if d.is_dma:
                        bk = bestk.get(d.key)
                        if bk is None or d.cum > bk.cum:
                            bestk[d.key] = d
                        continue
                    if d.eng == ins.eng and not ins.is_dma:
                        assert d.idx < ins.idx, "same-engine dependency placed after its consumer"
                        if e == "pe" or not raw:
                            continue
                    b = best.get(d.eng)
                    if b is None or d.idx > b.idx:
                        best[d.eng] = d
                for d in bestk.values():
                    need.append(d)
                for d in best.values():
                    need.append(d)
                ins.waits = need
        for e in self.ENGS:
            seen = {}
            for ins in self.q[e]:
                keep = []
                for d in ins.waits:
                    if d.is_dma:
                        k = ("k", d.key)
                        if seen.get(k, -1) >= d.cum:
                            continue
                        seen[k] = d.cum
                    else:
                        k = ("e", d.eng)
                        if seen.get(k, -1) >= d.idx:
                            continue
                        seen[k] = d.idx
                        d.sig = True
                    keep.append(d)
                ins.waits = keep
        eps = set()
        for e in self.ENGS:
            t = {}
            for ins in self.q[e]:
                if ins.sig and not ins.is_dma:
                    t[ins.ep] = t.get(ins.ep, 0) + 1
                    ins.tick = t[ins.ep]
                    eps.add((e, ins.ep))
        self.max_ticks = {}
        for e in self.ENGS:
            for ins in self.q[e]:
                if ins.tick:
                    self.max_ticks[(e, ins.ep)] = max(self.max_ticks.get((e, ins.ep), 0), ins.tick)
        self.n_ins = {e: len(self.q[e]) for e in self.ENGS}
        stack = contextlib.ExitStack()
        esem = {k: stack.enter_context(nc.semaphore("s_%s_%d" % k)) for k in sorted(eps)}
        ksem = {}
        for n, k in enumerate(self.dma_keys):
            ksem[k] = stack.enter_context(nc.semaphore("k%d" % n))
        q = self.q
        dma_keys = self.dma_keys

        def run(e, h):
            for ins in q[e]:
                for d in ins.waits:
                    if d.is_dma:
                        h.wait_ge(ksem[d.key], d.cum)
                    else:
                        h.wait_ge(esem[(d.eng, d.ep)], d.tick)
                bi = ins.fn(h)
                if ins.is_dma:
                    if ins.inc:
                        bi.then_inc(ksem[ins.key], ins.inc)
                    else:
                        bi.then_inc(ksem[ins.key])
                elif ins.sig:
                    bi.then_inc(esem[(e, ins.ep)], 1)
            if e == "sp":
                for k, v in dma_keys.items():
                    h.wait_ge(ksem[k], v)

        with stack:
            with nc.Block() as block:
                @block.sync
                def _(h):
                    run("sp", h)

                @block.tensor
                def _(h):
                    run("pe", h)

                @block.scalar
                def _(h):
                    run("act", h)

                @block.vector
                def _(h):
                    run("dve", h)

                @block.gpsimd
                def _(h):
                    run("pool", h)


def build_program(S, n_mix, has_prev, do_final, layer_ids, fused=False):
    assert S % TQ == 0
    NT = S // TQ
    n_out = (1 if has_prev else 0) + max(n_mix - 1, 0) + (1 if (do_final and n_mix > 0) else 0)
    if n_mix == 0:
        n_out = 1
    n_ada = n_out + n_mix if not fused else DEPTH
    nc = bass.Bass("TRN2", target_bir_lowering=False)
    dt_in = lambda n, s, d: nc.dram_tensor(n, s, d, kind="ExternalInput").ap()
    dt_out = lambda n, s, d: nc.dram_tensor(n, s, d, kind="ExternalOutput").ap()
    dt_int = lambda n, s, d: nc.dram_tensor(n, s, d, kind="Internal").ap()

    xT_in = dt_in("xT", [D, S], F32)
    cT_in = dt_in("cT", [128, 8], F32)
    cf_in = dt_in("cf", [128, CF_W], F32)
    cb_in = dt_in("cb", [128, CB_W], BF16)
    wada_in = dt_in("w_ada", [n_ada, D, 3 * D], F32)
    bada_in = dt_in("b_adaT", [n_ada, 128, 24], F32)
    if n_mix > 0:
        win_in = dt_in("w_in", [n_mix, D, NCOL], F32)
        pp_in = dt_in("pp", [n_mix, 128, NPP], F32)
        wgk_in = dt_in("wgk", [n_mix, 17, 64], F32)
    if n_out > 0:
        wout_in = dt_in("w_out", [n_out, D, D], F32)
    if has_prev or n_mix == 0:
        oT_in = dt_in("oT_in", [1, D, S], BF16)
    CW = min(2048, S) if fused else S
    NCH = S // CW
    TPC = CW // TQ
    if n_mix > 0:
        if fused:
            ohT = dt_int("ohT", [NCH, 256, CW], BF16)
            oT_fulls = [dt_int("oT_full%d" % k, [NCH, D, CW], BF16) for k in range(2)]
        else:
            ohT = dt_out("ohT", [NCH, 256, CW], BF16)
    if do_final:
        yT = dt_out("yT", [D, S], F32)
    xw = None
    if n_mix > 0 and n_out - (1 if do_final else 0) > 0:
        xw = dt_int("xw", [D, S], F32) if (fused or do_final) else dt_out("xT_out", [D, S], F32)

    def tview(ap, i):
        return ap.rearrange("(kc p) s -> p kc s", p=128)[:, :, i * TQ:(i + 1) * TQ]

    with contextlib.ExitStack() as st:
        def sb(n, s, d):
            return st.enter_context(nc.sbuf_tensor("sb_" + n, s, d))

        xt = sb("xt", [128, 8, TQ], F32)
        ot = sb("ot", [128, 8, TQ], BF16)
        hT = sb("hT", [128, 8, TQ], BF16)
        sqs = sb("sqs", [128, 2, TQ], BF16)
        wout = sb("wout", [128, 8, D], BF16)
        wstage = sb("wstage", [128, 2, D], F32)
        cf = sb("cf", [128, CF_W], F32)
        cb = sb("cb", [128, CB_W], BF16)
        cT = sb("cT", [128, 8], F32)
        cact = sb("cact", [128, 8], F32)
        ctmp = sb("ctmp", [128, 8], F32)
        modT = sb("modT", [128, n_ada, 24], F32)
        badaT = sb("badaT", [128, n_ada, 24], F32)
        FS = {k: sb("F%d" % k, [128, TQ], F32) for k in (0, 3, 4, 5, 6, 8, 9, 10, 11)}
        if n_mix > 0:
            win = sb("win", [128, 8, NCOL], BF16)
            KT = sb("KT", [128, S], BF16)
            V = sb("V", [128, S // 128, 129], BF16)
            QT = sb("QT", [128, 2, 2, TQ], BF16)
            FZ = [sb("FZ%d" % k, [128, TQ], F32) for k in range(2)]
            AS = [sb("AS%d" % k, [128, 4, 128], F32) for k in range(3)]
            ON = sb("ON", [128, 4, 128], BF16)
            rc = sb("rc", [128, 2, 4], F32)
            ssq = sb("ssq", [128, 8], F32)
            NPT = 6
            Pt = sb("Pt", [128, NPT, TQ], BF16)
            C0 = sb("C0", [128, TQ + 3], F32)
            C1 = sb("C1", [128, TQ + 3], F32)
            G0 = sb("G0", [32, TQ], F32)
            B0 = sb("B0", [128, TQ], BF16)
            L0 = sb("L0", [128, 256], F32)
            L1 = sb("L1", [128, 256], F32)
            kdec = sb("kdec", [128, 256], BF16)
            vtok = sb("vtok", [128, TQ], BF16)
            a8 = sb("a8", [64, 8], F32)
            Sring = sb("Sring", [64, 8, 128], F32)
            oh = sb("oh", [128, 2, 2, TQ], BF16)
            pp = sb("pp", [128, n_mix, NPP], F32)
            dp = sb("dp", [128, n_mix, 32], F32)
            wgk = sb("wgk", [32, n_mix, 64], F32)
            lamt = sb("lamt", [128, n_mix, 4], F32)
        gates = sb("gates", [128, max(n_out, 1), 8], F32)
        ps = st.enter_context(nc.psum_tensor("ps", [128, 8, TQ], F32))

        SCH = Sched(nc)
        ring = [0]

        def nb():
            ring[0] = (ring[0] + 1) % 4
            return ring[0]

        ringA = [0]
        ringB = [0]

        def nbA():
            ringA[0] = (ringA[0] + 1) % 3
            return ringA[0]

        def nbB():
            ringB[0] = (ringB[0] + 1) % 3
            return 3 + ringB[0]

        def bank(b):
            return ("bank", b)

        def mm(out, lhsT, rhs, start, stop, reads, writes):
            SCH.op("pe", lambda h: h.matmul(out, lhsT=lhsT, rhs=rhs, start=start, stop=stop,
                                            skip_group_check=True), reads, writes)

        def act(out, in_, func, reads, writes, scale=1.0, bias=0.0):
            SCH.op("act", lambda h: h.activation(out=out, in_=in_, func=func, bias=bias, scale=scale),
                   reads, writes)

        def tt(eng, out, in0, in1, op, reads, writes):
            SCH.op(eng, lambda h: h.tensor_tensor(out=out, in0=in0, in1=in1, op=op), reads, writes)

        def ts(eng, out, in0, s1, s2, op0, op1, reads, writes):
            if s2 is None:
                SCH.op(eng, lambda h: h.tensor_scalar(out=out, in0=in0, scalar1=s1, scalar2=None, op0=op0),
                       reads, writes)
            else:
                SCH.op(eng, lambda h: h.tensor_scalar(out=out, in0=in0, scalar1=s1, scalar2=s2, op0=op0, op1=op1),
                       reads, writes)

        def stt(eng, out, in0, scalar, in1, op0, op1, reads, writes):
            SCH.op(eng, lambda h: h.scalar_tensor_tensor(out=out, in0=in0, scalar=scalar, in1=in1, op0=op0, op1=op1),
                   reads, writes)

        def cp(eng, out, in_, reads, writes):
            if eng == "act":
                SCH.op("act", lambda h: h.copy(out=out, in_=in_), reads, writes)
            else:
                SCH.op(eng, lambda h: h.tensor_copy(out=out, in_=in_), reads, writes)

        def rsqrt_mean(dst, src_ps, reads, writes_name):
            act(dst, src_ps, AF.Ln, reads, [writes_name], scale=1.0, bias=EPS)
            act(dst, dst, AF.Exp, [writes_name], [writes_name], scale=-0.5)

        def sigmoid_neg(dst, src, reads, name):
            act(dst, src, AF.Exp, reads, [name], scale=-1.0)
            act(dst, dst, AF.Ln, [name], [name], scale=1.0, bias=1.0)
            act(dst, dst, AF.Exp, [name], [name], scale=-1.0)

        SCH.dma("sp", lambda h: h.dma_start(out=cf[:], in_=cf_in), "c_cf", writes=["cf"])
        SCH.dma("sp", lambda h: h.dma_start(out=cb[:], in_=cb_in), "c_cb", writes=["cb"])
        SCH.dma("sp", lambda h: h.dma_start(out=cT[:], in_=cT_in), "c_cT", writes=["cT"])
        SCH.dma("sp", lambda h: h.dma_start(out=badaT[:], in_=bada_in.rearrange("a p c -> p a c")), "c_bada",
                writes=["badaT"])
        if n_mix > 0:
            SCH.dma("sp", lambda h: h.dma_start(out=pp[:], in_=pp_in.rearrange("l p c -> p l c")), "c_pp", writes=["pp"])
            SCH.op("dve", lambda h: h.memset(wgk[:], 0.0), writes=["wgk"])
            SCH.dma("sp", lambda h: h.dma_start(out=wgk[0:17, :, :], in_=wgk_in.rearrange("l k c -> k l c")), "c_wgk",
                    reads=[], writes=["wgk"])
            SCH.op("dve", lambda h: h.memset(G0[:], 1.0), writes=["G0"])
            for par_ in range(2):
                SCH.op("pool", lambda h, par_=par_: h.memset(QT[:, par_, :, :], 0.0), writes=[("QT", par_)])
            SCH.op("pool", lambda h: h.memset(V[:, :, 128:129], 1.0), writes=["Vones"])
        sigmoid_neg(ctmp[:], cT[:], ["cT"], "ctmp")
        tt("dve", cact[:], cT[:], ctmp[:], ALU.mult, ["cT", "ctmp"], ["cact"])

        def emit_ada(a):
            for blk in range(6):
                SCH.dma("sp", lambda h, blk=blk: h.dma_start(
                    out=xt[:], in_=wada_in[a].rearrange("(kc p) n -> p kc n", p=128)[:, :, blk * 512:(blk + 1) * 512]),
                    "xl", writes=[("xt", dc) for dc in range(8)])
                for cc in range(4):
                    col = blk * 4 + cc
                    for kc in range(8):
                        mm(ps[:, 7, col:col + 1], xt[:, kc, cc * 128:(cc + 1) * 128], cact[:, kc:kc + 1],
                           kc == 0, kc == 7, [("xt", kc), "cact"], [bank(7)])
            tt("dve", modT[:, a, :], ps[:, 7, 0:24], badaT[:, a, :], ALU.add, [bank(7), "badaT"], [("modT", a)])

        astage = sb("astage", [128, 1, 8, 128], F32)

        def gen_ada_bg(a, cc):
            slot = 0
            SCH.dma("sp", lambda h: h.dma_start(
                out=astage[:, slot, :, :], in_=wada_in[a].rearrange("(kc p) n -> p kc n", p=128)[:, :, cc * 128:(cc + 1) * 128]),
                ("ast", slot), writes=[("astage", slot)])
            yield
            b = nbA()
            for kc in range(8):
                mm(ps[:, b, 0:1], astage[:, slot, kc, :], cact[:, kc:kc + 1], kc == 0, kc == 7,
                   [("astage", slot), "cact"], [bank(b)])
            tt("dve", modT[:, a, cc:cc + 1], ps[:, b, 0:1], badaT[:, a, cc:cc + 1], ALU.add, [bank(b), "badaT"], [("modT", a)])
            yield

        def load_weight(dst, src2d, ncols, resname):
            for kc in range(8):
                slot = kc % 2
                SCH.dma("sp", lambda h, kc=kc, slot=slot: h.dma_start(
                    out=wstage[:, slot, 0:ncols], in_=src2d[kc * 128:(kc + 1) * 128, :]),
                    ("wst", slot), writes=[("wstage", slot)])
                cp("pool", dst[:, kc, :], wstage[:, slot, 0:ncols], [("wstage", slot)], [resname])

        def emit_load_x(i, src, src_name):
            SCH.dma("sp", lambda h: h.dma_start(out=xt[:], in_=tview(src, i)), "xl",
                    reads=[(src_name, i)], writes=[("xt", dc) for dc in range(8)])

        def gen_outproj(i, o_src, o_name, gate_idx, dst, dst_name):
            SCH.dma("sp", lambda h: h.dma_start(out=ot[:], in_=tview(o_src[i // TPC], i % TPC)), "ol",
                    reads=[(o_name, i)], writes=["ot"])
            yield
            for dc in range(8):
                b = nbA()
                for kc in range(8):
                    mm(ps[:, b, :], wout[:, kc, dc * 128:(dc + 1) * 128], ot[:, kc, :], kc == 0, kc == 7,
                       ["ot", "wout"], [bank(b)])
                stt("dve", xt[:, dc, :], ps[:, b, :], gates[:, gate_idx, dc:dc + 1], xt[:, dc, :], ALU.mult, ALU.add,
                    [bank(b), ("xt", dc), ("gates", gate_idx)], [("xt", dc)])
                yield
            SCH.dma("pool", lambda h: h.dma_start(out=tview(dst, i), in_=xt[:]), "xs",
                    reads=[("xt", dc) for dc in range(8)], writes=[(dst_name, i)])

        def gen_A(i, li, x_src, x_src_name, prev_args, extra=None):
            ppl = lambda c0, c1: pp[:, li, c0:c1]
            dpl = lambda c0, c1: dp[:, li, c0:c1]
            t0 = i * TQ
            par = i % 2
            emit_load_x(i, x_src, x_src_name)
            if prev_args is not None:
                yield from gen_outproj(i, *prev_args)
            bx = nbA()
            for kc in range(8):
                sl = kc % 2
                tt("pool", sqs[:, sl, :], xt[:, kc, :], xt[:, kc, :], ALU.mult, [("xt", kc)], [("sqs", sl)])
                mm(ps[:, bx, :], cb[:, CB_X:CB_X + 128], sqs[:, sl, :], kc == 0, kc == 7, [("sqs", sl), "cb"], [bank(bx)])
                if kc % 2 == 1:
                    yield
            rsqrt_mean(FS[0][:], ps[:, bx, :], [bank(bx)], "F0")
            yield
            for kc in range(8):
                fa = 10 + (kc % 2)
                tt("dve", FS[fa][:], xt[:, kc, :], FS[0][:], ALU.mult, [("xt", kc), "F0"], ["F%d" % fa])
                ts("pool", hT[:, kc, :], FS[fa][:], dpl(kc, kc + 1), dpl(8 + kc, 9 + kc), ALU.mult, ALU.add,
                   ["F%d" % fa, ("dp", li)], [("hT", kc)])
                if kc % 2 == 1:
                    yield

            def proj(col0, M, b):
                for kc in range(8):
                    mm(ps[0:M, b, :], win[:, kc, col0:col0 + M], hT[:, kc, :], kc == 0, kc == 7,
                       [("hT", kc), "win"], [bank(b)])

            for which, col0 in (("q", C_DQ), ("k", C_DK)):
                b = nbA()
                proj(col0, 128, b)
                yield
                act(B0[:], ps[:, b, :], AF.Square, [bank(b)], ["B0"])
                b5 = nbA()
                mm(ps[:, b5, :], cb[:, CB_BLK:CB_BLK + 128], B0[:], True, True, ["B0", "cb"], [bank(b5)])
                yield
                rsqrt_mean(FS[4][:], ps[:, b5, :], [bank(b5)], "F4")
                if which == "q":
                    for mq in range(2):
                        rs = slice(64 * mq, 64 * mq + 64)
                        stt("dve", QT[rs, par, mq, :], ps[rs, b, :], dp[rs, li, 24:25], FS[4][rs, :], ALU.mult, ALU.mult,
                            [bank(b), "F4", ("dp", li)], [("QT", par)])
                else:
                    stt("dve", KT[:, t0:t0 + TQ], ps[:, b, :], ppl(18, 19), FS[4][:], ALU.mult, ALU.mult,
                        [bank(b), "F4", "pp"], [("KT", i)])
                yield
            for col0, dst, dname, fe in ((C_DZ, FZ[par], ("FZ", par), 5), (C_GZ, FS[3], "F3", 6)):
                b = nbA()
                proj(col0, 128, b)
                yield
                sigmoid_neg(FS[fe][:], ps[:, b, :], [bank(b)], "F%d" % fe)
                tt("dve", dst[:], ps[:, b, :], FS[fe][:], ALU.mult, [bank(b), "F%d" % fe], [dname])
                yield
            b = nbA()
            proj(C_GV, 128, b)
            cp("dve", C1[:, 3:3 + TQ], ps[:, b, :], [bank(b)], ["C1"])
            yield
            b = nbA()
            proj(C_GQK, 128, b)
            cp("dve", C0[:, 3:3 + TQ], ps[:, b, :], [bank(b)], ["C0"])
            yield
            b = nbA()
            proj(C_GLR, 16, b)
            cp("dve", G0[0:16, :], ps[0:16, b, :], [bank(b)], ["G0"])
            yield
            b = nbA()
            for j in range(4):
                for kc in range(8):
                    mm(ps[:, b, j * 128:(j + 1) * 128], hT[:, kc, j * 128:(j + 1) * 128], win[:, kc, C_DV:C_DV + 128],
                       kc == 0, kc == 7, [("hT", kc), "win"], [bank(b)])
                if j % 2 == 1:
                    yield
            for j in range(4):
                cp("dve", V[:, 4 * i + j, 0:128], ps[:, b, j * 128:(j + 1) * 128], [bank(b)], [("V", i)])
            yield

            for (Cb, cname, w0, fo, fe) in ((C0, "C0", 8, 8, 4), (C1, "C1", 12, 9, 5)):
                fon = "F%d" % fo
                ts("dve", FS[fo][:], Cb[:, 0:TQ], ppl(w0, w0 + 1), None, ALU.mult, None, [cname, "pp"], [fon])
                for j in range(1, 4):
                    stt("dve", FS[fo][:], Cb[:, j:j + TQ], ppl(w0 + j, w0 + j + 1), FS[fo][:], ALU.mult, ALU.add,
                        [cname, "pp", fon], [fon])
                cp("pool", Cb[:, 0:3], Cb[:, TQ:TQ + 3], [cname], [cname])
                yield
                sigmoid_neg(FS[fe][:], FS[fo][:], [fon], "F%d" % fe)
                tt("dve", FS[fo][:], FS[fo][:], FS[fe][:], ALU.mult, [fon, "F%d" % fe], [fon])
                yield
            sqk, sv = FS[8], FS[9]
            bg = nbA()
            for j in range(4):
                mm(ps[:, bg, j * 64:(j + 1) * 64], G0[0:17, j * 128:(j + 1) * 128], wgk[0:17, li, :], True, True,
                   ["G0", "wgk"], [bank(bg)])
            yield
            act(L0[:], ps[:, bg, 0:256], AF.Exp, [bank(bg)], ["L0"], scale=-1.0)
            act(L0[:], L0[:], AF.Ln, ["L0"], ["L0"], scale=1.0, bias=1.0)
            yield
            bd = nbA()
            for j in range(4):
                mm(ps[:, bd, j * 64:(j + 1) * 64], cf[:, CF_U:CF_U + 128], L0[:, j * 64:(j + 1) * 64], True, True,
                   ["cf", "L0"], [bank(bd)])
            yield
            act(L1[:], ps[:, bd, 0:256], AF.Exp, [bank(bd)], ["L1"])
            bk = nbA()
            for j in range(4):
                mm(ps[:, bk, j * 64:(j + 1) * 64], sqk[64:128, j * 128:(j + 1) * 128], cf[64:128, CF_ID + 64:CF_ID + 128],
                   True, True, ["F8", "cf"], [bank(bk)])
            yield
            tt("dve", kdec[:], ps[:, bk, 0:256], L1[:], ALU.mult, [bank(bk), "L1"], ["kdec"])
            bv = nbA()
            for j in range(4):
                mm(ps[:, bv, j * 128:(j + 1) * 128], sv[:, j * 128:(j + 1) * 128], cf[:, CF_ID:CF_ID + 128],
                   True, True, ["F9", "cf"], [bank(bv)])
            yield
            cp("dve", vtok[:], ps[:, bv, :], [bank(bv)], ["vtok"])
            bb = nbA()
            for j in range(4):
                mm(ps[0:64, bb, 2 * j:2 * j + 2], L0[:, j * 64:(j + 1) * 64], cf[:, CF_CIND:CF_CIND + 2], True, True,
                   ["L0", "cf"], [bank(bb)])
            yield
            act(a8[:], ps[0:64, bb, 0:8], AF.Exp, [bank(bb)], ["a8"])
            ubs = [nbA(), nbA()]
            for c in range(8):
                j, hh = c // 2, c % 2
                ub = ubs[c % 2]
                col = (c // 2) * 128
                mm(ps[0:64, ub, col:col + 128], kdec[64 * hh:64 * hh + 64, j * 64:(j + 1) * 64],
                   vtok[64 * hh:64 * hh + 64, j * 128:(j + 1) * 128], True, True, ["kdec", "vtok"], [bank(ub)])
            yield
            obk = nbA()
            for c in range(8):
                ub = ubs[c % 2]
                col = (c // 2) * 128
                pc = (c - 1) % 8
                stt("dve", Sring[:, c, :], Sring[:, pc, :], a8[:, c:c + 1], ps[0:64, ub, col:col + 128], ALU.mult, ALU.add,
                    [("S", pc), "a8", bank(ub)], [("S", c)])
                mm(ps[:, obk, c * 64:(c + 1) * 64], Sring[:, c, :], sqk[0:64, c * 64:(c + 1) * 64], True, True,
                   [("S", c), "F8"], [bank(obk)])
                if c % 2 == 1:
                    yield
            act(B0[:], ps[:, obk, :], AF.Square, [bank(obk)], ["B0"])
            bs = nbA()
            mm(ps[:, bs, :], cb[:, CB_G:CB_G + 128], B0[:], True, True, ["B0", "cb"], [bank(bs)])
            yield
            rsqrt_mean(FS[4][:], ps[:, bs, :], [bank(bs)], "F4")
            stt("dve", FS[5][:], ps[:, obk, :], dpl(26, 27), FS[4][:], ALU.mult, ALU.mult, [bank(obk), "F4", ("dp", li)], ["F5"])
            tt("pool", oh[:, par, 0, :], FS[5][:], FS[3][:], ALU.mult, ["F5", "F3"], [("oh", par, 0)])
            yield
            if extra is not None:
                yield from extra

        pslot = [0]

        def gen_B(i, li):
            dpl = lambda c0, c1: dp[:, li, c0:c1]
            par = i % 2
            nkt = 4 * (i + 1)
            LA = 2
            OB, SBK = 6, 7
            infos = [{}, {}]

            def emit_st(m, kt):
                diag = kt >= 4 * i
                q0 = 128 * (kt - 4 * i) if diag else 0
                b = nbB()
                mm(ps[:, b, q0:TQ], KT[:, kt * 128:(kt + 1) * 128],
                   QT[:, par, m, q0:TQ], True, True, [("KT", kt // 4), ("QT", par)], [bank(b)])
                infos[m][kt] = (b, q0, diag)

            OBX = (6, 7)

            def finalize_map(m):
                for qs in range(4):
                    bk, c0 = OBX[qs // 2], (qs % 2) * 129
                    SCH.op("dve", lambda h, bk=bk, c0=c0, qs=qs: h.reciprocal(out=rc[:, m, qs:qs + 1], in_=ps[:, bk, c0 + 128:c0 + 129]),
                           [bank(bk)], [("rc", m)])
                for qs in range(4):
                    bk, c0 = OBX[qs // 2], (qs % 2) * 129
                    ts("dve", AS[m][:, qs, :], ps[:, bk, c0:c0 + 128], rc[:, m, qs:qs + 1], None, ALU.mult, None,
                       [bank(bk), ("rc", m)], ["A%d" % m])

            for kt in range(min(LA, nkt)):
                emit_st(0, kt)
            for m in range(2):
                for kt in range(nkt):
                    b, q0, diag = infos[m].pop(kt)
                    sl = pslot[0]
                    pslot[0] = (sl + 1) % NPT
                    if not diag:
                        act(Pt[:, sl, q0:TQ], ps[:, b, q0:TQ], AF.Exp, [bank(b)], [("Pt", sl)])
                    else:
                        act(Pt[:, sl, q0:TQ], ps[:, b, q0:TQ], AF.Exp, [bank(b)], [("Pt", sl)])
                        SCH.op("pool", lambda h, sl=sl, q0=q0: h.memset(Pt[64:128, sl, q0:q0 + 64], 0.0),
                               [], [("Pt", sl)])
                    for qs in range(q0 // 128, 4):
                        bk, c0 = OBX[qs // 2], (qs % 2) * 129
                        mm(ps[:, bk, c0:c0 + 129], Pt[:, sl, qs * 128:(qs + 1) * 128], V[:, kt, :],
                           (kt == 0 and qs % 2 == 0), False, [("V", kt // 4), "Vones", ("Pt", sl)], [bank(bk)])
                    if kt + LA < nkt:
                        emit_st(m, kt + LA)
                    elif m == 0 and (kt + LA - nkt) < min(LA, nkt):
                        emit_st(1, kt + LA - nkt)
                    yield
                finalize_map(m)
                yield
            stt("dve", AS[0][:], AS[1][:], dpl(27, 28), AS[0][:], ALU.mult, ALU.add, ["A1", "A0", ("dp", li)], ["A0"])
            tt("dve", AS[2][:], AS[0][:], AS[0][:], ALU.mult, ["A0"], ["A2"])
            SCH.op("dve", lambda h: h.tensor_reduce(out=ssq[:, 0:4], in_=AS[2][:], axis=mybir.AxisListType.X, op=ALU.add),
                   ["A2"], ["ssq"])
            yield
            act(ssq[:, 4:8], ssq[:, 0:4], AF.Ln, ["ssq"], ["ssq2"], scale=1.0 / 128.0, bias=EPS)
            act(ssq[:, 4:8], ssq[:, 4:8], AF.Exp, ["ssq2"], ["ssq2"], scale=-0.5)
            for qs in range(4):
                ts("dve", ON[:, qs, :], AS[0][:, qs, :], ssq[:, 4 + qs:5 + qs], None, ALU.mult, None, ["A0", "ssq2"], ["ON"])
            bs = nbB()
            for qs in range(4):
                mm(ps[:, bs, qs * 128:(qs + 1) * 128], ON[:, qs, :], cb[:, CB_ID:CB_ID + 128], True, True,
                   ["ON", "cb"], [bank(bs)])
            yield
            stt("dve", oh[:, par, 1, :], ps[:, bs, :], dpl(25, 26), FZ[par][:], ALU.mult, ALU.mult,
                [bank(bs), ("dp", li), ("FZ", par)], [("oh", par, 1)])
            tc0 = (i % TPC) * TQ
            SCH.dma("pool", lambda h: h.dma_start(
                out=ohT[i // TPC].rearrange("(g p) s -> p g s", p=128)[:, :, tc0:tc0 + TQ], in_=oh[:, par, :, :]),
                ("ohs", par), reads=[("oh", par, 0), ("oh", par, 1)], writes=[("ohT", i)])
            if fused and (i % TPC) == TPC - 1:
                c = i // TPC
                k = li % 2
                SCH.dma("pool", lambda h, k=k, c=c: h.collective_compute(
                    "AllGather", ALU.bypass, replica_groups=[[0, 1, 2, 3], [4, 5, 6, 7]],
                    ins=[ohT[c]], outs=[oT_fulls[k][c]]),
                    ("cc", li, c), reads=[("ohT", ii) for ii in range(c * TPC, (c + 1) * TPC)],
                    writes=[(("ofull", k), ii) for ii in range(c * TPC, (c + 1) * TPC)], inc=None)
            yield

        NA_EST = [62]

        def drain(g):
            n = 0
            for _ in g:
                n += 1
            return n

        def run_layer_tiles(li, x_src, x_src_name, prev_args, extra_of, tail_gen):
            for i in range(NT):
                gB = gen_B(i, li)
                hold = 0
                if i + 1 < NT:
                    gA = gen_A(i + 1, li, x_src, x_src_name, prev_args, extra_of(i + 1))
                else:
                    gA = tail_gen
                    hold = 4 * (i + 1) + 6
                step = 0
                if gA is not None and hold == 0:
                    SCH.defer = True
                    drain(gA)
                    SCH.defer = False
                    gA = None
                for _ in gB:
                    step += 1
                    SCH.step()
                    if gA is not None and step == hold:
                        SCH.defer = True
                        drain(gA)
                        SCH.defer = False
                        gA = None
                if gA is not None:
                    drain(gA)
                SCH.flush()

        def emit_layer_params(li, a_idx, lam_init):
            ppl = lambda c0, c1: pp[:, li, c0:c1]
            dpl = lambda c0, c1: dp[:, li, c0:c1]
            r = [("dp", li)]
            stt("dve", dpl(0, 8), modT[:, a_idx, 8:16], 1.0, ppl(0, 8), ALU.add, ALU.mult, [("modT", a_idx), "pp"], r)
            cp("dve", dpl(8, 16), modT[:, a_idx, 0:8], [("modT", a_idx)], r)
            ts("dve", dpl(24, 25), ppl(17, 18), 0.125, None, ALU.mult, None, ["pp"], r)
            ts("dve", dpl(25, 26), ppl(19, 20), 1.0 - lam_init, None, ALU.mult, None, ["pp"], r)
            ts("dve", dpl(26, 27), ppl(16, 17), 0.125, None, ALU.mult, None, ["pp"], r)
            tt("dve", lamt[:, li, 0:1], ppl(20, 21), ppl(21, 22), ALU.mult, ["pp"], [("lamt", li)])
            tt("dve", lamt[:, li, 1:2], ppl(22, 23), ppl(23, 24), ALU.mult, ["pp"], [("lamt", li)])
            bl = nbA()
            mm(ps[:, bl, 32:34], cf[:, CF_ONE:CF_ONE + 128], lamt[:, li, 0:2], True, True, [("lamt", li), "cf"], [bank(bl)])
            act(lamt[:, li, 2:4], ps[:, bl, 32:34], AF.Exp, [bank(bl)], [("lamt2", li)])
            tt("dve", dpl(27, 28), lamt[:, li, 3:4], lamt[:, li, 2:3], ALU.subtract, [("lamt2", li)], r)
            ts("dve", dpl(27, 28), dpl(27, 28), -lam_init, None, ALU.add, None, r, r)

        assert fused and n_mix >= 1 and do_final and not has_prev
        emit_ada(layer_ids[0])

        def layer_ctx(li):
            l = layer_ids[li]
            if li == 0:
                return xT_in, "xin", None
            o_src, o_name = oT_fulls[(li - 1) % 2], ("ofull", (li - 1) % 2)
            xs, xn = (xT_in, "xin") if li == 1 else (xw, "xw")
            return xs, xn, (o_src, o_name, li - 1, xw, "xw")

        def extra_of_layer(li):
            def f(i):
                if li + 1 >= n_mix:
                    return None
                lo, hi = (i * 24) // NT, ((i + 1) * 24) // NT
                if lo == hi:
                    return None

                def g():
                    for cc in range(lo, hi):
                        yield from gen_ada_bg(layer_ids[li + 1], cc)
                return g()
            return f

        def gen_layer_start(li):
            l = layer_ids[li]
            lam_init = 0.8 - 0.6 * math.exp(-0.3 * l)
            xs, xn, prev_args = layer_ctx(li)
            if prev_args is not None:
                load_weight(wout, wout_in[li - 1], D, "wout")
                cp("dve", gates[:, li - 1, :], modT[:, layer_ids[li - 1], 16:24], [("modT", layer_ids[li - 1])],
                   [("gates", li - 1)])
                yield
            load_weight(win, win_in[li], NCOL, "win")
            yield
            emit_layer_params(li, l, lam_init)
            SCH.op("dve", lambda h: h.memset(Sring[:, 7, :], 0.0), writes=[("S", 7)])
            SCH.op("dve", lambda h: h.memset(C0[:, 0:3], 0.0), writes=["C0"])
            SCH.op("dve", lambda h: h.memset(C1[:, 0:3], 0.0), writes=["C1"])
            yield
            yield from gen_A(0, li, xs, xn, prev_args, extra_of_layer(li)(0))

        def gen_final_start():
            l_last = layer_ids[n_mix - 1]
            load_weight(wout, wout_in[n_mix - 1], D, "wout")
            cp("dve", gates[:, n_mix - 1, :], modT[:, l_last, 16:24], [("modT", l_last)], [("gates", n_mix - 1)])
            yield
            xs, xn = (xT_in, "xin") if n_mix == 1 else (xw, "xw")
            emit_load_x(0, xs, xn)
            yield from gen_outproj(0, oT_fulls[(n_mix - 1) % 2], ("ofull", (n_mix - 1) % 2), n_mix - 1, yT, "yT")

        SCH.ep = 1
        n0 = drain(gen_layer_start(0))
        NA_EST[0] = max(n0 - 3, 40)
        for li in range(n_mix):
            SCH.ep = li + 1
            xs, xn, prev_args = layer_ctx(li)
            tail = gen_layer_start(li + 1) if li + 1 < n_mix else gen_final_start()
            if NCH > 1 and 'nooverlap' not in DBG:
                run_layer_tiles(li, xs, xn, prev_args, extra_of_layer(li), tail)
            else:
                run_layer_tiles(li, xs, xn, prev_args, extra_of_layer(li), None)
                drain(tail)
        SCH.ep = n_mix + 1
        xs, xn = (xT_in, "xin") if n_mix == 1 else (xw, "xw")
        fin_args = (oT_fulls[(n_mix - 1) % 2], ("ofull", (n_mix - 1) % 2), n_mix - 1, yT, "yT")
        for i in range(1, NT):
            emit_load_x(i, xs, xn)
            drain(gen_outproj(i, *fin_args))
        SCH.emit()
    return nc


def _consts():
    cf = np.zeros((128, CF_W), np.float32)
    cf[:, CF_ID:CF_ID + 128] = np.eye(128, dtype=np.float32)
    s = np.arange(128)[:, None]
    t = np.arange(128)[None, :]
    cf[:, CF_U:CF_U + 128] = np.where((s > t) & (s // 64 == t // 64), -1.0 / 16.0, 0.0)
    cf[:, CF_ONE:CF_ONE + 128] = 1.0
    cf[:, CF_CIND:CF_CIND + 2] = np.where(s // 64 == np.arange(2)[None, :], -1.0 / 16.0, 0.0)
    cbm = np.zeros((128, CB_W), np.float32)
    cbm[:, CB_X:CB_X + 128] = 1.0 / 1024.0
    cbm[:, CB_BLK:CB_BLK + 128] = np.where(s // 64 == t // 64, 1.0 / 64.0, 0.0)
    cbm[:, CB_D:CB_D + 128] = 1.0 / 128.0
    cbm[:, CB_G:CB_G + 128] = 1.0 / 8192.0
    cbm[:, CB_ONE:CB_ONE + 128] = 1.0
    cbm[:, CB_ID:CB_ID + 128] = np.eye(128, dtype=np.float32)
    return cf, cbm.astype(ml_dtypes.bfloat16)


def _head_cols(h):
    gq = np.arange(0, 64) + h * 64
    gk = 256 + np.arange(0, 64) + h * 64
    gv = 512 + np.arange(0, 128) + h * 128
    glr = 1024 + np.arange(16)
    gz = 1040 + np.arange(128) + h * 128
    dq = 1552 + np.arange(128) + h * 128
    dk = 2064 + np.arange(128) + h * 128
    dv = 2576 + np.arange(128) + h * 128
    dz = 3088 + np.arange(128) + h * 128
    return np.concatenate([dq, dk, dz, gz, gv, gq, gk, dv, glr])


def _pp(inp, l, h):
    p = np.zeros((128, NPP), np.float32)
    p[:, 0:8] = inp["norm_g"][l].reshape(8, 128).T
    cw = inp["conv_w"][l]
    ch_qk = np.concatenate([np.arange(64) + h * 64, 256 + np.arange(64) + h * 64])
    p[:, 8:12] = cw[:, ch_qk].T
    p[:, 12:16] = cw[:, 512 + h * 128 + np.arange(128)].T
    p[:, 16] = inp["gla_norm_g"][l]
    p[:, 17] = np.tile(inp["qn_g"][l], 2)
    p[:, 18] = np.tile(inp["kn_g"][l], 2)
    p[:, 19] = inp["diff_norm_g"][l]
    p[0:64, 20] = inp["lam_q1"][l]
    p[0:64, 21] = inp["lam_k1"][l]
    p[0:64, 22] = inp["lam_q2"][l]
    p[0:64, 23] = inp["lam_k2"][l]
    return p


def _wgk(inp, l, h):
    w = np.zeros((17, 64), np.float32)
    w[0:16] = inp["w_gk"][l][:, h * 64:(h + 1) * 64]
    w[16] = inp["b_gk"][l][h * 64:(h + 1) * 64]
    return w


def _wout_perm(inp, l):
    rows = np.concatenate([np.concatenate([np.arange(128) + h * 128, 512 + np.arange(128) + h * 128]) for h in range(4)])
    return np.ascontiguousarray(inp["w_out"][l][rows, :])


_PROG_CACHE = {}


def _get_prog(S, n_mix, has_prev, do_final, layer_ids, fused=False):
    key = (S, n_mix, has_prev, do_final, tuple(layer_ids), fused)
    if key not in _PROG_CACHE:
        _PROG_CACHE[key] = build_program(S, n_mix, has_prev, do_final, layer_ids, fused)
    return _PROG_CACHE[key]


def run_unfused(inp, S):
    cf, cbm = _consts()
    B = inp["x"].shape[0]
    xT = [np.ascontiguousarray(inp["x"][b, :S].T) for b in range(B)]
    cT = [np.ascontiguousarray(inp["c"][b].reshape(8, 128).T) for b in range(B)]
    oT = None
    for l in range(DEPTH):
        has_prev = l > 0
        nc = _get_prog(S, 1, has_prev, False, [l])
        in_maps = []
        for core in range(8):
            b, h = core // 4, core % 4
            m = {"xT": xT[b], "cT": cT[b], "cf": cf, "cb": cbm,
                 "w_in": np.ascontiguousarray(inp["w_in"][l][:, _head_cols(h)])[None],
                 "pp": _pp(inp, l, h)[None], "wgk": _wgk(inp, l, h)[None]}
            if has_prev:
                m["w_ada"] = np.ascontiguousarray(inp["w_ada"][l - 1:l + 1])
                m["b_adaT"] = np.stack([inp["b_ada"][k].reshape(24, 128).T for k in (l - 1, l)])
                m["w_out"] = _wout_perm(inp, l - 1)[None]
                m["oT_in"] = oT[b][None]
            else:
                m["w_ada"] = np.ascontiguousarray(inp["w_ada"][l:l + 1])
                m["b_adaT"] = np.stack([inp["b_ada"][l].reshape(24, 128).T])
            in_maps.append(m)
        res = run_bass_kernel_spmd(nc, in_maps, core_ids=list(range(8))).results
        oT = [np.concatenate([res[b * 4 + h]["ohT"][0] for h in range(4)], axis=0) for b in range(B)]
        if has_prev:
            xT = [res[b * 4]["xT_out"] for b in range(B)]
    nc = _get_prog(S, 0, True, True, [DEPTH - 1])
    in_maps = []
    for core in range(8):
        b = core // 4
        in_maps.append({"xT": xT[b], "cT": cT[b], "cf": cf, "cb": cbm,
                        "w_ada": np.ascontiguousarray(inp["w_ada"][DEPTH - 1:DEPTH]),
                        "b_adaT": np.stack([inp["b_ada"][DEPTH - 1].reshape(24, 128).T]),
                        "w_out": _wout_perm(inp, DEPTH - 1)[None], "oT_in": oT[b][None]})
    res = run_bass_kernel_spmd(nc, in_maps, core_ids=list(range(8))).results
    out = np.stack([res[b * 4]["yT"].T for b in range(B)])
    return np.ascontiguousarray(out)


def fused_in_maps(inp, S):
    cf, cbm = _consts()
    B = inp["x"].shape[0]
    xT = [np.ascontiguousarray(inp["x"][b, :S].T) for b in range(B)]
    cT = [np.ascontiguousarray(inp["c"][b].reshape(8, 128).T) for b in range(B)]
    w_ada = np.ascontiguousarray(inp["w_ada"])
    b_adaT = np.stack([inp["b_ada"][l].reshape(24, 128).T for l in range(DEPTH)])
    w_out = np.stack([_wout_perm(inp, l) for l in range(DEPTH)])
    in_maps = []
    for core in range(8):
        b, h = core // 4, core % 4
        cols = _head_cols(h)
        in_maps.append({
            "xT": xT[b], "cT": cT[b], "cf": cf, "cb": cbm, "w_ada": w_ada, "b_adaT": b_adaT,
            "w_in": np.stack([inp["w_in"][l][:, cols] for l in range(DEPTH)]),
            "pp": np.stack([_pp(inp, l, h) for l in range(DEPTH)]),
            "wgk": np.stack([_wgk(inp, l, h) for l in range(DEPTH)]),
            "w_out": w_out})
    return in_maps


def run_fused(inp, S):
    B = inp["x"].shape[0]
    nc = _get_prog(S, DEPTH, False, True, list(range(DEPTH)), True)
    res = run_bass_kernel_spmd(nc, fused_in_maps(inp, S), core_ids=list(range(8))).results
    return np.ascontiguousarray(np.stack([res[b * 4]["yT"].T for b in range(B)]))


def kernel(**inputs):
    inp = {k: np.asarray(v) for k, v in inputs.items()}
    return run_fused(inp, inp["x"].shape[1])
```

```python
import contextlib
import math

import ml_dtypes
import numpy as np

import concourse.bass as bass
import concourse.mybir as mybir
from concourse.bass_utils import run_bass_kernel_spmd

F32 = mybir.dt.float32
BF16 = mybir.dt.bfloat16
AF = mybir.ActivationFunctionType
ALU = mybir.AluOpType

D = 1024
TQ = 512
NCOL = 912
EPS = 1e-6
DEPTH = 4
C_DQ, C_DK, C_DZ, C_GZ, C_GV, C_GQK, C_DV, C_GLR = 0, 128, 256, 384, 512, 640, 768, 896
CF_ID, CF_U, CF_ONE, CF_CIND = 0, 128, 256, 384
CF_W = 386
CB_X, CB_BLK, CB_D, CB_G, CB_ONE, CB_ID = 0, 128, 256, 384, 512, 640
CB_W = 768
NPP = 24
DBG = set()


class _Ins:
    __slots__ = ("eng", "fn", "deps", "idx", "is_dma", "sig", "tick", "key", "cum", "waits", "ep", "inc")

    def __init__(self, eng, fn, is_dma=False, key=None):
        self.eng = eng
        self.fn = fn
        self.deps = []
        self.idx = -1
        self.is_dma = is_dma
        self.sig = False
        self.tick = 0
        self.key = key
        self.cum = 0
        self.waits = []
        self.ep = 0
        self.inc = 16


class Sched:
    ENGS = ("sp", "pe", "act", "dve", "pool")

    def __init__(self, nc):
        self.nc = nc
        self.q = {e: [] for e in self.ENGS}
        self.res = {}
        self.dma_keys = {}
        self.ep = 0
        self.defer = False
        self.now = 0
        self.delta = 3
        self.pending = {e: [] for e in self.ENGS}
        self.rs_of = {}
        self.slots = {}
        self.last_rs = {}

    def _track(self, ins, reads, writes):
        deps = {}
        for r in reads:
            st = self.res.get(r)
            if st is not None and st[0] is not None:
                deps[id(st[0])] = (st[0], True)
        for w in writes:
            st = self.res.get(w)
            if st is not None:
                if st[0] is not None and id(st[0]) not in deps:
                    deps[id(st[0])] = (st[0], False)
                for rd in st[1]:
                    if id(rd) not in deps:
                        deps[id(rd)] = (rd, False)
        for r in reads:
            st = self.res.setdefault(r, [None, []])
            st[1].append(ins)
        for w in writes:
            self.res[w] = [ins, []]
        deps.pop(id(ins), None)
        ins.deps = list(deps.values())

    RATE = {"pe": 2, "act": 1}

    def _place(self, ins):
        if not self.defer:
            ins.idx = len(self.q[ins.eng])
            self.q[ins.eng].append(ins)
            return
        e = ins.eng
        rs = max(self.now + 1, self.last_rs.get(e, 0))
        for (d, raw) in ins.deps:
            drs = self.rs_of.get(id(d))
            if drs is not None:
                rs = max(rs, drs + (0 if d.eng == e and not d.is_dma else self.delta))
        lim = self.RATE.get(e)
        if lim is not None:
            while self.slots.get((e, rs), 0) >= lim:
                rs += 1
            self.slots[(e, rs)] = self.slots.get((e, rs), 0) + 1
        self.last_rs[e] = rs
        self.rs_of[id(ins)] = rs
        self.pending[e].append((rs, ins))

    def step(self):
        self.now += 1
        for e in self.ENGS:
            p = self.pending[e]
            k = 0
            while k < len(p) and p[k][0] <= self.now:
                ins = p[k][1]
                ins.idx = len(self.q[e])
                self.q[e].append(ins)
                k += 1
            if k:
                del p[:k]

    def flush(self):
        while any(self.pending[e] for e in self.ENGS):
            self.step()
        self.rs_of = {}
        self.slots = {}
        self.last_rs = {}

    def op(self, eng, fn, reads=(), writes=()):
        ins = _Ins(eng, fn)
        ins.ep = self.ep
        self._track(ins, reads, writes)
        self._place(ins)
        return ins

    def dma(self, queue, fn, key, reads=(), writes=(), inc=16):
        ins = _Ins(queue, fn, is_dma=True, key=key)
        ins.ep = self.ep
        ins.inc = inc
        self._track(ins, reads, writes)
        self.dma_keys[key] = self.dma_keys.get(key, 0) + (inc if inc else 1)
        ins.cum = self.dma_keys[key]
        self._place(ins)
        return ins

    def emit(self):
        nc = self.nc
        for e in self.ENGS:
            for ins in self.q[e]:
                need = []
                best = {}
                bestk = {}
                for (d, raw) in ins.deps:
                    if d.is_dma:
                        bk = bestk.get(d.key)
                        if bk is None or d.cum > bk.cum:
                            bestk[d.key] = d
                        continue
                    if d.eng == ins.eng and not ins.is_dma:
                        assert d.idx < ins.idx, "same-engine dependency placed after its consumer"
                        if e == "pe" or not raw:
                            continue
                    b = best.get(d.eng)
                    if b is None or d.idx > b.idx:
                        best[d.eng] = d
                for d in bestk.values():
                    need.append(d)
                for d in best.values():
                    need.append(d)
                ins.waits = need
        for e in self.ENGS:
            seen = {}
            for ins in self.q[e]:
                keep = []
                for d in ins.waits:
                    if d.is_dma:
                        k = ("k", d.key)
                        if seen.get(k, -1) >= d.cum:
                            continue
                        seen[k] = d.cum
                    else:
                        k = ("e", d.eng)
                        if seen.get(k, -1) >= d.idx:
                            continue
                        seen[k] = d.idx
                        d.sig = True
                    keep.append(d)
                ins.waits = keep
        eps = set()
        for e in self.ENGS:
            t = {}
            for ins in self.q[e]:
                if ins.sig and not ins.is_dma:
                    t[ins.ep] = t.get(ins.ep, 0) + 1
                    ins.tick = t[ins.ep]
                    eps.add((e, ins.ep))
        self.max_ticks = {}
        for e in self.ENGS:
            for ins in self.q[e]:
                if ins.tick:
                    self.max_ticks[(e, ins.ep)] = max(self.max_ticks.get((e, ins.ep), 0), ins.tick)
        self.n_ins = {e: len(self.q[e]) for e in self.ENGS}
        stack = contextlib.ExitStack()
        esem = {k: stack.enter_context(nc.semaphore("s_%s_%d" % k)) for k in sorted(eps)}
        ksem = {}
        for n, k in enumerate(self.dma_keys):
            ksem[k] = stack.enter_context(nc.semaphore("k%d" % n))
        q = self.q
        dma_keys = self.dma_keys

        def run(e, h):
            for ins in q[e]:
                for d in ins.waits:
                    if d.is_dma:
                        h.wait_ge(ksem[d.key], d.cum)
                    else:
                        h.wait_ge(esem[(d.eng, d.ep)], d.tick)
                bi = ins.fn(h)
                if ins.is_dma:
                    if ins.inc:
                        bi.then_inc(ksem[ins.key], ins.inc)
                    else:
                        bi.then_inc(ksem[ins.key])
                elif ins.sig:
                    bi.then_inc(esem[(e, ins.ep)], 1)
            if e == "sp":
                for k, v in dma_keys.items():
                    h.wait_ge(ksem[k], v)

        with stack:
            with nc.Block() as block:
                @block.sync
                def _(h):
                    run("sp", h)

                @block.tensor
                def _(h):
                    run("pe", h)

                @block.scalar
                def _(h):
                    run("act", h)

                @block.vector
                def _(h):
                    run("dve", h)

                @block.gpsimd
                def _(h):
                    run("pool", h)


def build_program(S, n_mix, has_prev, do_final, layer_ids, fused=False):
    assert S % TQ == 0
    NT = S // TQ
    n_out = (1 if has_prev else 0) + max(n_mix - 1, 0) + (1 if (do_final and n_mix > 0) else 0)
    if n_mix == 0:
        n_out = 1
    n_ada = n_out + n_mix if not fused else DEPTH
    nc = bass.Bass("TRN2", target_bir_lowering=False)
    dt_in = lambda n, s, d: nc.dram_tensor(n, s, d, kind="ExternalInput").ap()
    dt_out = lambda n, s, d: nc.dram_tensor(n, s, d, kind="ExternalOutput").ap()
    dt_int = lambda n, s, d: nc.dram_tensor(n, s, d, kind="Internal").ap()

    xT_in = dt_in("xT", [D, S], F32)
    cT_in = dt_in("cT", [128, 8], F32)
    cf_in = dt_in("cf", [128, CF_W], F32)
    cb_in = dt_in("cb", [128, CB_W], BF16)
    wada_in = dt_in("w_ada", [n_ada, D, 3 * D], F32)
    bada_in = dt_in("b_adaT", [n_ada, 128, 24], F32)
    if n_mix > 0:
        win_in = dt_in("w_in", [n_mix, D, NCOL], F32)
        pp_in = dt_in("pp", [n_mix, 128, NPP], F32)
        wgk_in = dt_in("wgk", [n_mix, 17, 64], F32)
    if n_out > 0:
        wout_in = dt_in("w_out", [n_out, D, D], F32)
    if has_prev or n_mix == 0:
        oT_in = dt_in("oT_in", [1, D, S], BF16)
    CW = min(2048, S) if fused else S
    NCH = S // CW
    TPC = CW // TQ
    if n_mix > 0:
        if fused:
            ohT = dt_int("ohT", [NCH, 256, CW], BF16)
            oT_fulls = [dt_int("oT_full%d" % k, [NCH, D, CW], BF16) for k in range(2)]
        else:
            ohT = dt_out("ohT", [NCH, 256, CW], BF16)
    if do_final:
        yT = dt_out("yT", [D, S], F32)
    xw = None
    if n_mix > 0 and n_out - (1 if do_final else 0) > 0:
        xw = dt_int("xw", [D, S], F32) if (fused or do_final) else dt_out("xT_out", [D, S], F32)

    def tview(ap, i):
        return ap.rearrange("(kc p) s -> p kc s", p=128)[:, :, i * TQ:(i + 1) * TQ]

    with contextlib.ExitStack() as st:
        def sb(n, s, d):
            return st.enter_context(nc.sbuf_tensor("sb_" + n, s, d))

        xt = sb("xt", [128, 8, TQ], F32)
        ot = sb("ot", [128, 8, TQ], BF16)
        hT = sb("hT", [128, 8, TQ], BF16)
        sqs = sb("sqs", [128, 2, TQ], BF16)
        wout = sb("wout", [128, 8, D], BF16)
        wstage = sb("wstage", [128, 2, D], F32)
        cf = sb("cf", [128, CF_W], F32)
        cb = sb("cb", [128, CB_W], BF16)
        cT = sb("cT", [128, 8], F32)
        cact = sb("cact", [128, 8], F32)
        ctmp = sb("ctmp", [128, 8], F32)
        modT = sb("modT", [128, n_ada, 24], F32)
        badaT = sb("badaT", [128, n_ada, 24], F32)
        FS = {k: sb("F%d" % k, [128, TQ], F32) for k in (0, 3, 4, 5, 6, 8, 9, 10, 11)}
        if n_mix > 0:
            win = sb("win", [128, 8, NCOL], BF16)
            KT = sb("KT", [128, S], BF16)
            V = sb("V", [128, S // 128, 129], BF16)
            QT = sb("QT", [128, 2, 2, TQ], BF16)
            FZ = [sb("FZ%d" % k, [128, TQ], F32) for k in range(2)]
            AS = [sb("AS%d" % k, [128, 4, 128], F32) for k in range(3)]
            ON = sb("ON", [128, 4, 128], BF16)
            rc = sb("rc", [128, 2, 4], F32)
            ssq = sb("ssq", [128, 8], F32)
            NPT = 6
            Pt = sb("Pt", [128, NPT, TQ], BF16)
            C0 = sb("C0", [128, TQ + 3], F32)
            C1 = sb("C1", [128, TQ + 3], F32)
            G0 = sb("G0", [32, TQ], F32)
            B0 = sb("B0", [128, TQ], BF16)
            L0 = sb("L0", [128, 256], F32)
            L1 = sb("L1", [128, 256], F32)
            kdec = sb("kdec", [128, 256], BF16)
            vtok = sb("vtok", [128, TQ], BF16)
            a8 = sb("a8", [64, 8], F32)
            Sring = sb("Sring", [64, 8, 128], F32)
            oh = sb("oh", [128, 2, 2, TQ], BF16)
            pp = sb("pp", [128, n_mix, NPP], F32)
            dp = sb("dp", [128, n_mix, 32], F32)
            wgk = sb("wgk", [32, n_mix, 64], F32)
            lamt = sb("lamt", [128, n_mix, 4], F32)
        gates = sb("gates", [128, max(n_out, 1), 8], F32)
        ps = st.enter_context(nc.psum_tensor("ps", [128, 8, TQ], F32))

        SCH = Sched(nc)
        ring = [0]

        def nb():
            ring[0] = (ring[0] + 1) % 4
            return ring[0]

        ringA = [0]
        ringB = [0]

        def nbA():
            ringA[0] = (ringA[0] + 1) % 3
            return ringA[0]

        def nbB():
            ringB[0] = (ringB[0] + 1) % 3
            return 3 + ringB[0]

        def bank(b):
            return ("bank", b)

        def mm(out, lhsT, rhs, start, stop, reads, writes):
            SCH.op("pe", lambda h: h.matmul(out, lhsT=lhsT, rhs=rhs, start=start, stop=stop,
                                            skip_group_check=True), reads, writes)

        def act(out, in_, func, reads, writes, scale=1.0, bias=0.0):
            SCH.op("act", lambda h: h.activation(out=out, in_=in_, func=func, bias=bias, scale=scale),
                   reads, writes)

        def tt(eng, out, in0, in1, op, reads, writes):
            SCH.op(eng, lambda h: h.tensor_tensor(out=out, in0=in0, in1=in1, op=op), reads, writes)

        def ts(eng, out, in0, s1, s2, op0, op1, reads, writes):
            if s2 is None:
                SCH.op(eng, lambda h: h.tensor_scalar(out=out, in0=in0, scalar1=s1, scalar2=None, op0=op0),
                       reads, writes)
            else:
                SCH.op(eng, lambda h: h.tensor_scalar(out=out, in0=in0, scalar1=s1, scalar2=s2, op0=op0, op1=op1),
                       reads, writes)

        def stt(eng, out, in0, scalar, in1, op0, op1, reads, writes):
            SCH.op(eng, lambda h: h.scalar_tensor_tensor(out=out, in0=in0, scalar=scalar, in1=in1, op0=op0, op1=op1),
                   reads, writes)

        def cp(eng, out, in_, reads, writes):
            if eng == "act":
                SCH.op("act", lambda h: h.copy(out=out, in_=in_), reads, writes)
            else:
                SCH.op(eng, lambda h: h.tensor_copy(out=out, in_=in_), reads, writes)

        def rsqrt_mean(dst, src_ps, reads, writes_name):
            act(dst, src_ps, AF.Ln, reads, [writes_name], scale=1.0, bias=EPS)
            act(dst, dst, AF.Exp, [writes_name], [writes_name], scale=-0.5)

        def sigmoid_neg(dst, src, reads, name):
            act(dst, src, AF.Exp, reads, [name], scale=-1.0)
            act(dst, dst, AF.Ln, [name], [name], scale=1.0, bias=1.0)
            act(dst, dst, AF.Exp, [name], [name], scale=-1.0)

        SCH.dma("sp", lambda h: h.dma_start(out=cf[:], in_=cf_in), "c_cf", writes=["cf"])
        SCH.dma("sp", lambda h: h.dma_start(out=cb[:], in_=cb_in), "c_cb", writes=["cb"])
        SCH.dma("sp", lambda h: h.dma_start(out=cT[:], in_=cT_in), "c_cT", writes=["cT"])
        SCH.dma("sp", lambda h: h.dma_start(out=badaT[:], in_=bada_in.rearrange("a p c -> p a c")), "c_bada",
                writes=["badaT"])
        if n_mix > 0:
            SCH.dma("sp", lambda h: h.dma_start(out=pp[:], in_=pp_in.rearrange("l p c -> p l c")), "c_pp", writes=["pp"])
            SCH.op("dve", lambda h: h.memset(wgk[:], 0.0), writes=["wgk"])
            SCH.dma("sp", lambda h: h.dma_start(out=wgk[0:17, :, :], in_=wgk_in.rearrange("l k c -> k l c")), "c_wgk",
                    reads=[], writes=["wgk"])
            SCH.op("dve", lambda h: h.memset(G0[:], 1.0), writes=["G0"])
            for par_ in range(2):
                SCH.op("pool", lambda h, par_=par_: h.memset(QT[:, par_, :, :], 0.0), writes=[("QT", par_)])
            SCH.op("pool", lambda h: h.memset(V[:, :, 128:129], 1.0), writes=["Vones"])
        sigmoid_neg(ctmp[:], cT[:], ["cT"], "ctmp")
        tt("dve", cact[:], cT[:], ctmp[:], ALU.mult, ["cT", "ctmp"], ["cact"])

        def emit_ada(a):
            for blk in range(6):
                SCH.dma("sp", lambda h, blk=blk: h.dma_start(
                    out=xt[:], in_=wada_in[a].rearrange("(kc p) n -> p kc n", p=128)[:, :, blk * 512:(blk + 1) * 512]),
                    "xl", writes=[("xt", dc) for dc in range(8)])
                for cc in range(4):
                    col = blk * 4 + cc
                    for kc in range(8):
                        mm(ps[:, 7, col:col + 1], xt[:, kc, cc * 128:(cc + 1) * 128], cact[:, kc:kc + 1],
                           kc == 0, kc == 7, [("xt", kc), "cact"], [bank(7)])
            tt("dve", modT[:, a, :], ps[:, 7, 0:24], badaT[:, a, :], ALU.add, [bank(7), "badaT"], [("modT", a)])

        astage = sb("astage", [128, 1, 8, 128], F32)

        def gen_ada_bg(a, cc):
            slot = 0
            SCH.dma("sp", lambda h: h.dma_start(
                out=astage[:, slot, :, :], in_=wada_in[a].rearrange("(kc p) n -> p kc n", p=128)[:, :, cc * 128:(cc + 1) * 128]),
                ("ast", slot), writes=[("astage", slot)])
            yield
            b = nbA()
            for kc in range(8):
                mm(ps[:, b, 0:1], astage[:, slot, kc, :], cact[:, kc:kc + 1], kc == 0, kc == 7,
                   [("astage", slot), "cact"], [bank(b)])
            tt("dve", modT[:, a, cc:cc + 1], ps[:, b, 0:1], badaT[:, a, cc:cc + 1], ALU.add, [bank(b), "badaT"], [("modT", a)])
            yield

        def load_weight(dst, src2d, ncols, resname):
            for kc in range(8):
                slot = kc % 2
                SCH.dma("sp", lambda h, kc=kc, slot=slot: h.dma_start(
                    out=wstage[:, slot, 0:ncols], in_=src2d[kc * 128:(kc + 1) * 128, :]),
                    ("wst", slot), writes=[("wstage", slot)])
                cp("pool", dst[:, kc, :], wstage[:, slot, 0:ncols], [("wstage", slot)], [resname])

        def emit_load_x(i, src, src_name):
            SCH.dma("sp", lambda h: h.dma_start(out=xt[:], in_=tview(src, i)), "xl",
                    reads=[(src_name, i)], writes=[("xt", dc) for dc in range(8)])

        def gen_outproj(i, o_src, o_name, gate_idx, dst, dst_name):
            SCH.dma("sp", lambda h: h.dma_start(out=ot[:], in_=tview(o_src[i // TPC], i % TPC)), "ol",
                    reads=[(o_name, i)], writes=["ot"])
            yield
            for dc in range(8):
                b = nbA()
                for kc in range(8):
                    mm(ps[:, b, :], wout[:, kc, dc * 128:(dc + 1) * 128], ot[:, kc, :], kc == 0, kc == 7,
                       ["ot", "wout"], [bank(b)])
                stt("dve", xt[:, dc, :], ps[:, b, :], gates[:, gate_idx, dc:dc + 1], xt[:, dc, :], ALU.mult, ALU.add,
                    [bank(b), ("xt", dc), ("gates", gate_idx)], [("xt", dc)])
                yield
            SCH.dma("pool", lambda h: h.dma_start(out=tview(dst, i), in_=xt[:]), "xs",
                    reads=[("xt", dc) for dc in range(8)], writes=[(dst_name, i)])

        def gen_A(i, li, x_src, x_src_name, prev_args, extra=None):
            ppl = lambda c0, c1: pp[:, li, c0:c1]
            dpl = lambda c0, c1: dp[:, li, c0:c1]
            t0 = i * TQ
            par = i % 2
            emit_load_x(i, x_src, x_src_name)
            if prev_args is not None:
                yield from gen_outproj(i, *prev_args)
            bx = nbA()
            for kc in range(8):
                sl = kc % 2
                tt("pool", sqs[:, sl, :], xt[:, kc, :], xt[:, kc, :], ALU.mult, [("xt", kc)], [("sqs", sl)])
                mm(ps[:, bx, :], cb[:, CB_X:CB_X + 128], sqs[:, sl, :], kc == 0, kc == 7, [("sqs", sl), "cb"], [bank(bx)])
                if kc % 2 == 1:
                    yield
            rsqrt_mean(FS[0][:], ps[:, bx, :], [bank(bx)], "F0")
            yield
            for kc in range(8):
                fa = 10 + (kc % 2)
                tt("dve", FS[fa][:], xt[:, kc, :], FS[0][:], ALU.mult, [("xt", kc), "F0"], ["F%d" % fa])
                ts("pool", hT[:, kc, :], FS[fa][:], dpl(kc, kc + 1), dpl(8 + kc, 9 + kc), ALU.mult, ALU.add,
                   ["F%d" % fa, ("dp", li)], [("hT", kc)])
                if kc % 2 == 1:
                    yield

            def proj(col0, M, b):
                for kc in range(8):
                    mm(ps[0:M, b, :], win[:, kc, col0:col0 + M], hT[:, kc, :], kc == 0, kc == 7,
                       [("hT", kc), "win"], [bank(b)])

            for which, col0 in (("q", C_DQ), ("k", C_DK)):
                b = nbA()
                proj(col0, 128, b)
                yield
                act(B0[:], ps[:, b, :], AF.Square, [bank(b)], ["B0"])
                b5 = nbA()
                mm(ps[:, b5, :], cb[:, CB_BLK:CB_BLK + 128], B0[:], True, True, ["B0", "cb"], [bank(b5)])
                yield
                rsqrt_mean(FS[4][:], ps[:, b5, :], [bank(b5)], "F4")
                if which == "q":
                    for mq in range(2):
                        rs = slice(64 * mq, 64 * mq + 64)
                        stt("dve", QT[rs, par, mq, :], ps[rs, b, :], dp[rs, li, 24:25], FS[4][rs, :], ALU.mult, ALU.mult,
                            [bank(b), "F4", ("dp", li)], [("QT", par)])
                else:
                    stt("dve", KT[:, t0:t0 + TQ], ps[:, b, :], ppl(18, 19), FS[4][:], ALU.mult, ALU.mult,
                        [bank(b), "F4", "pp"], [("KT", i)])
                yield
            for col0, dst, dname, fe in ((C_DZ, FZ[par], ("FZ", par), 5), (C_GZ, FS[3], "F3", 6)):
                b = nbA()
                proj(col0, 128, b)
                yield
                sigmoid_neg(FS[fe][:], ps[:, b, :], [bank(b)], "F%d" % fe)
                tt("dve", dst[:], ps[:, b, :], FS[fe][:], ALU.mult, [bank(b), "F%d" % fe], [dname])
                yield
            b = nbA()
            proj(C_GV, 128, b)
            cp("dve", C1[:, 3:3 + TQ], ps[:, b, :], [bank(b)], ["C1"])
            yield
            b = nbA()
            proj(C_GQK, 128, b)
            cp("dve", C0[:, 3:3 + TQ], ps[:, b, :], [bank(b)], ["C0"])
            yield
            b = nbA()
            proj(C_GLR, 16, b)
            cp("dve", G0[0:16, :], ps[0:16, b, :], [bank(b)], ["G0"])
            yield
            b = nbA()
            for j in range(4):
                for kc in range(8):
                    mm(ps[:, b, j * 128:(j + 1) * 128], hT[:, kc, j * 128:(j + 1) * 128], win[:, kc, C_DV:C_DV + 128],
                       kc == 0, kc == 7, [("hT", kc), "win"], [bank(b)])
                if j % 2 == 1:
                    yield
            for j in range(4):
                cp("dve", V[:, 4 * i + j, 0:128], ps[:, b, j * 128:(j + 1) * 128], [bank(b)], [("V", i)])
            yield

            for (Cb, cname, w0, fo, fe) in ((C0, "C0", 8, 8, 4), (C1, "C1", 12, 9, 5)):
                fon = "F%d" % fo
                ts("dve", FS[fo][:], Cb[:, 0:TQ], ppl(w0, w0 + 1), None, ALU.mult, None, [cname, "pp"], [fon])
                for j in range(1, 4):
                    stt("dve", FS[fo][:], Cb[:, j:j + TQ], ppl(w0 + j, w0 + j + 1), FS[fo][:], ALU.mult, ALU.add,
                        [cname, "pp", fon], [fon])
                cp("pool", Cb[:, 0:3], Cb[:, TQ:TQ + 3], [cname], [cname])
                yield
                sigmoid_neg(FS[fe][:], FS[fo][:], [fon], "F%d" % fe)
                tt("dve", FS[fo][:], FS[fo][:], FS[fe][:], ALU.mult, [fon, "F%d" % fe], [fon])
                yield
            sqk, sv = FS[8], FS[9]
            bg = nbA()
            for j in range(4):
                mm(ps[:, bg, j * 64:(j + 1) * 64], G0[0:17, j * 128:(j + 1) * 128], wgk[0:17, li, :], True, True,
                   ["G0", "wgk"], [bank(bg)])
            yield
            act(L0[:], ps[:, bg, 0:256], AF.Exp, [bank(bg)], ["L0"], scale=-1.0)
            act(L0[:], L0[:], AF.Ln, ["L0"], ["L0"], scale=1.0, bias=1.0)
            yield
            bd = nbA()
            for j in range(4):
                mm(ps[:, bd, j * 64:(j + 1) * 64], cf[:, CF_U:CF_U + 128], L0[:, j * 64:(j + 1) * 64], True, True,
                   ["cf", "L0"], [bank(bd)])
            yield
            act(L1[:], ps[:, bd, 0:256], AF.Exp, [bank(bd)], ["L1"])
            bk = nbA()
            for j in range(4):
                mm(ps[:, bk, j * 64:(j + 1) * 64], sqk[64:128, j * 128:(j + 1) * 128], cf[64:128, CF_ID + 64:CF_ID + 128],
                   True, True, ["F8", "cf"], [bank(bk)])
            yield
            tt("dve", kdec[:], ps[:, bk, 0:256], L1[:], ALU.mult, [bank(bk), "L1"], ["kdec"])
            bv = nbA()
            for j in range(4):
                mm(ps[:, bv, j * 128:(j + 1) * 128], sv[:, j * 128:(j + 1) * 128], cf[:, CF_ID:CF_ID + 128],
                   True, True, ["F9", "cf"], [bank(bv)])
            yield
            cp("dve", vtok[:], ps[:, bv, :], [bank(bv)], ["vtok"])
            bb = nbA()
            for j in range(4):
                mm(ps[0:64, bb, 2 * j:2 * j + 2], L0[:, j * 64:(j + 1) * 64], cf[:, CF_CIND:CF_CIND + 2], True, True,
                   ["L0", "cf"], [bank(bb)])
            yield
            act(a8[:], ps[0:64, bb, 0:8], AF.Exp, [bank(bb)], ["a8"])
            ubs = [nbA(), nbA()]
            for c in range(8):
                j, hh = c // 2, c % 2
                ub = ubs[c % 2]
                col = (c // 2) * 128
                mm(ps[0:64, ub, col:col + 128], kdec[64 * hh:64 * hh + 64, j * 64:(j + 1) * 64],
                   vtok[64 * hh:64 * hh + 64, j * 128:(j + 1) * 128], True, True, ["kdec", "vtok"], [bank(ub)])
            yield
            obk = nbA()
            for c in range(8):
                ub = ubs[c % 2]
                col = (c // 2) * 128
                pc = (c - 1) % 8
                stt("dve", Sring[:, c, :], Sring[:, pc, :], a8[:, c:c + 1], ps[0:64, ub, col:col + 128], ALU.mult, ALU.add,
                    [("S", pc), "a8", bank(ub)], [("S", c)])
                mm(ps[:, obk, c * 64:(c + 1) * 64], Sring[:, c, :], sqk[0:64, c * 64:(c + 1) * 64], True, True,
                   [("S", c), "F8"], [bank(obk)])
                if c % 2 == 1:
                    yield
            act(B0[:], ps[:, obk, :], AF.Square, [bank(obk)], ["B0"])
            bs = nbA()
            mm(ps[:, bs, :], cb[:, CB_G:CB_G + 128], B0[:], True, True, ["B0", "cb"], [bank(bs)])
            yield
            rsqrt_mean(FS[4][:], ps[:, bs, :], [bank(bs)], "F4")
            stt("dve", FS[5][:], ps[:, obk, :], dpl(26, 27), FS[4][:], ALU.mult, ALU.mult, [bank(obk), "F4", ("dp", li)], ["F5"])
            tt("pool", oh[:, par, 0, :], FS[5][:], FS[3][:], ALU.mult, ["F5", "F3"], [("oh", par, 0)])
            yield
            if extra is not None:
                yield from extra

        pslot = [0]

        def gen_B(i, li):
            dpl = lambda c0, c1: dp[:, li, c0:c1]
            par = i % 2
            nkt = 4 * (i + 1)
            LA = 2
            OB, SBK = 6, 7
            infos = [{}, {}]

            def emit_st(m, kt):
                diag = kt >= 4 * i
                q0 = 128 * (kt - 4 * i) if diag else 0
                b = nbB()
                mm(ps[:, b, q0:TQ], KT[:, kt * 128:(kt + 1) * 128],
                   QT[:, par, m, q0:TQ], True, True, [("KT", kt // 4), ("QT", par)], [bank(b)])
                infos[m][kt] = (b, q0, diag)

            OBX = (6, 7)

            def finalize_map(m):
                for qs in range(4):
                    bk, c0 = OBX[qs // 2], (qs % 2) * 129
                    SCH.op("dve", lambda h, bk=bk, c0=c0, qs=qs: h.reciprocal(out=rc[:, m, qs:qs + 1], in_=ps[:, bk, c0 + 128:c0 + 129]),
                           [bank(bk)], [("rc", m)])
                for qs in range(4):
                    bk, c0 = OBX[qs // 2], (qs % 2) * 129
                    ts("dve", AS[m][:, qs, :], ps[:, bk, c0:c0 + 128], rc[:, m, qs:qs + 1], None, ALU.mult, None,
                       [bank(bk), ("rc", m)], ["A%d" % m])

            for kt in range(min(LA, nkt)):
                emit_st(0, kt)
            for m in range(2):
                for kt in range(nkt):
                    b, q0, diag = infos[m].pop(kt)
                    sl = pslot[0]
                    pslot[0] = (sl + 1) % NPT
                    if not diag:
                        act(Pt[:, sl, q0:TQ], ps[:, b, q0:TQ], AF.Exp, [bank(b)], [("Pt", sl)])
                    else:
                        act(Pt[:, sl, q0:TQ], ps[:, b, q0:TQ], AF.Exp, [bank(b)], [("Pt", sl)])
                        SCH.op("pool", lambda h, sl=sl, q0=q0: h.memset(Pt[64:128, sl, q0:q0 + 64], 0.0),
                               [], [("Pt", sl)])
                    if kt + LA < nkt:
                        emit_st(m, kt + LA)
                    elif m == 0 and (kt + LA - nkt) < min(LA, nkt):
                        emit_st(1, kt + LA - nkt)
                    for qs in range(q0 // 128, 4):
                        bk, c0 = OBX[qs // 2], (qs % 2) * 129
                        mm(ps[:, bk, c0:c0 + 129], Pt[:, sl, qs * 128:(qs + 1) * 128], V[:, kt, :],
                           (kt == 0 and qs % 2 == 0), False, [("V", kt // 4), "Vones", ("Pt", sl)], [bank(bk)])
                    yield
                finalize_map(m)
                yield
            stt("dve", AS[0][:], AS[1][:], dpl(27, 28), AS[0][:], ALU.mult, ALU.add, ["A1", "A0", ("dp", li)], ["A0"])
            tt("dve", AS[2][:], AS[0][:], AS[0][:], ALU.mult, ["A0"], ["A2"])
            SCH.op("dve", lambda h: h.tensor_reduce(out=ssq[:, 0:4], in_=AS[2][:], axis=mybir.AxisListType.X, op=ALU.add),
                   ["A2"], ["ssq"])
            yield
            act(ssq[:, 4:8], ssq[:, 0:4], AF.Ln, ["ssq"], ["ssq2"], scale=1.0 / 128.0, bias=EPS)
            act(ssq[:, 4:8], ssq[:, 4:8], AF.Exp, ["ssq2"], ["ssq2"], scale=-0.5)
            for qs in range(4):
                ts("dve", ON[:, qs, :], AS[0][:, qs, :], ssq[:, 4 + qs:5 + qs], None, ALU.mult, None, ["A0", "ssq2"], ["ON"])
            bs = nbB()
            for qs in range(4):
                mm(ps[:, bs, qs * 128:(qs + 1) * 128], ON[:, qs, :], cb[:, CB_ID:CB_ID + 128], True, True,
                   ["ON", "cb"], [bank(bs)])
            yield
            stt("dve", oh[:, par, 1, :], ps[:, bs, :], dpl(25, 26), FZ[par][:], ALU.mult, ALU.mult,
                [bank(bs), ("dp", li), ("FZ", par)], [("oh", par, 1)])
            tc0 = (i % TPC) * TQ
            SCH.dma("pool", lambda h: h.dma_start(
                out=ohT[i // TPC].rearrange("(g p) s -> p g s", p=128)[:, :, tc0:tc0 + TQ], in_=oh[:, par, :, :]),
                ("ohs", par), reads=[("oh", par, 0), ("oh", par, 1)], writes=[("ohT", i)])
            if fused and (i % TPC) == TPC - 1:
                c = i // TPC
                k = li % 2
                SCH.dma("pool", lambda h, k=k, c=c: h.collective_compute(
                    "AllGather", ALU.bypass, replica_groups=[[0, 1, 2, 3], [4, 5, 6, 7]],
                    ins=[ohT[c]], outs=[oT_fulls[k][c]]),
                    ("cc", li, c), reads=[("ohT", ii) for ii in range(c * TPC, (c + 1) * TPC)],
                    writes=[(("ofull", k), ii) for ii in range(c * TPC, (c + 1) * TPC)], inc=None)
            yield

        NA_EST = [62]

        def drain(g):
            n = 0
            for _ in g:
                n += 1
            return n

        def run_layer_tiles(li, x_src, x_src_name, prev_args, extra_of, tail_gen):
            for i in range(NT):
                gB = gen_B(i, li)
                hold = 0
                if i + 1 < NT:
                    gA = gen_A(i + 1, li, x_src, x_src_name, prev_args, extra_of(i + 1))
                else:
                    gA = tail_gen
                    hold = 4 * (i + 1) + 6
                step = 0
                if gA is not None and hold == 0:
                    SCH.defer = True
                    drain(gA)
                    SCH.defer = False
                    gA = None
                for _ in gB:
                    step += 1
                    SCH.step()
                    if gA is not None and step == hold:
                        SCH.defer = True
                        drain(gA)
                        SCH.defer = False
                        gA = None
                if gA is not None:
                    drain(gA)
                SCH.flush()

        def emit_layer_params(li, a_idx, lam_init):
            ppl = lambda c0, c1: pp[:, li, c0:c1]
            dpl = lambda c0, c1: dp[:, li, c0:c1]
            r = [("dp", li)]
            stt("dve", dpl(0, 8), modT[:, a_idx, 8:16], 1.0, ppl(0, 8), ALU.add, ALU.mult, [("modT", a_idx), "pp"], r)
            cp("dve", dpl(8, 16), modT[:, a_idx, 0:8], [("modT", a_idx)], r)
            ts("dve", dpl(24, 25), ppl(17, 18), 0.125, None, ALU.mult, None, ["pp"], r)
            ts("dve", dpl(25, 26), ppl(19, 20), 1.0 - lam_init, None, ALU.mult, None, ["pp"], r)
            ts("dve", dpl(26, 27), ppl(16, 17), 0.125, None, ALU.mult, None, ["pp"], r)
            tt("dve", lamt[:, li, 0:1], ppl(20, 21), ppl(21, 22), ALU.mult, ["pp"], [("lamt", li)])
            tt("dve", lamt[:, li, 1:2], ppl(22, 23), ppl(23, 24), ALU.mult, ["pp"], [("lamt", li)])
            bl = nbA()
            mm(ps[:, bl, 32:34], cf[:, CF_ONE:CF_ONE + 128], lamt[:, li, 0:2], True, True, [("lamt", li), "cf"], [bank(bl)])
            act(lamt[:, li, 2:4], ps[:, bl, 32:34], AF.Exp, [bank(bl)], [("lamt2", li)])
            tt("dve", dpl(27, 28), lamt[:, li, 3:4], lamt[:, li, 2:3], ALU.subtract, [("lamt2", li)], r)
            ts("dve", dpl(27, 28), dpl(27, 28), -lam_init, None, ALU.add, None, r, r)

        assert fused and n_mix >= 1 and do_final and not has_prev
        emit_ada(layer_ids[0])

        def layer_ctx(li):
            l = layer_ids[li]
            if li == 0:
                return xT_in, "xin", None
            o_src, o_name = oT_fulls[(li - 1) % 2], ("ofull", (li - 1) % 2)
            xs, xn = (xT_in, "xin") if li == 1 else (xw, "xw")
            return xs, xn, (o_src, o_name, li - 1, xw, "xw")

        def extra_of_layer(li):
            def f(i):
                if li + 1 >= n_mix:
                    return None
                lo, hi = (i * 24) // NT, ((i + 1) * 24) // NT
                if lo == hi:
                    return None

                def g():
                    for cc in range(lo, hi):
                        yield from gen_ada_bg(layer_ids[li + 1], cc)
                return g()
            return f

        def gen_layer_start(li):
            l = layer_ids[li]
            lam_init = 0.8 - 0.6 * math.exp(-0.3 * l)
            xs, xn, prev_args = layer_ctx(li)
            if prev_args is not None:
                load_weight(wout, wout_in[li - 1], D, "wout")
                cp("dve", gates[:, li - 1, :], modT[:, layer_ids[li - 1], 16:24], [("modT", layer_ids[li - 1])],
                   [("gates", li - 1)])
                yield
            load_weight(win, win_in[li], NCOL, "win")
            yield
            emit_layer_params(li, l, lam_init)
            SCH.op("dve", lambda h: h.memset(Sring[:, 7, :], 0.0), writes=[("S", 7)])
            SCH.op("dve", lambda h: h.memset(C0[:, 0:3], 0.0), writes=["C0"])
            SCH.op("dve", lambda h: h.memset(C1[:, 0:3], 0.0), writes=["C1"])
            yield
            yield from gen_A(0, li, xs, xn, prev_args, extra_of_layer(li)(0))

        def gen_final_start():
            l_last = layer_ids[n_mix - 1]
            load_weight(wout, wout_in[n_mix - 1], D, "wout")
            cp("dve", gates[:, n_mix - 1, :], modT[:, l_last, 16:24], [("modT", l_last)], [("gates", n_mix - 1)])
            yield
            xs, xn = (xT_in, "xin") if n_mix == 1 else (xw, "xw")
            emit_load_x(0, xs, xn)
            yield from gen_outproj(0, oT_fulls[(n_mix - 1) % 2], ("ofull", (n_mix - 1) % 2), n_mix - 1, yT, "yT")

        SCH.ep = 1
        n0 = drain(gen_layer_start(0))
        NA_EST[0] = max(n0 - 3, 40)
        for li in range(n_mix):
            SCH.ep = li + 1
            xs, xn, prev_args = layer_ctx(li)
            tail = gen_layer_start(li + 1) if li + 1 < n_mix else gen_final_start()
            if NCH > 1 and 'nooverlap' not in DBG:
                run_layer_tiles(li, xs, xn, prev_args, extra_of_layer(li), tail)
            else:
                run_layer_tiles(li, xs, xn, prev_args, extra_of_layer(li), None)
                drain(tail)
        SCH.ep = n_mix + 1
        xs, xn = (xT_in, "xin") if n_mix == 1 else (xw, "xw")
        fin_args = (oT_fulls[(n_mix - 1) % 2], ("ofull", (n_mix - 1) % 2), n_mix - 1, yT, "yT")
        for i in range(1, NT):
            emit_load_x(i, xs, xn)
            drain(gen_outproj(i, *fin_args))
        SCH.emit()
    return nc


def _consts():
    cf = np.zeros((128, CF_W), np.float32)
    cf[:, CF_ID:CF_ID + 128] = np.eye(128, dtype=np.float32)
    s = np.arange(128)[:, None]
    t = np.arange(128)[None, :]
    cf[:, CF_U:CF_U + 128] = np.where((s > t) & (s // 64 == t // 64), -1.0 / 16.0, 0.0)
    cf[:, CF_ONE:CF_ONE + 128] = 1.0
    cf[:, CF_CIND:CF_CIND + 2] = np.where(s // 64 == np.arange(2)[None, :], -1.0 / 16.0, 0.0)
    cbm = np.zeros((128, CB_W), np.float32)
    cbm[:, CB_X:CB_X + 128] = 1.0 / 1024.0
    cbm[:, CB_BLK:CB_BLK + 128] = np.where(s // 64 == t // 64, 1.0 / 64.0, 0.0)
    cbm[:, CB_D:CB_D + 128] = 1.0 / 128.0
    cbm[:, CB_G:CB_G + 128] = 1.0 / 8192.0
    cbm[:, CB_ONE:CB_ONE + 128] = 1.0
    cbm[:, CB_ID:CB_ID + 128] = np.eye(128, dtype=np.float32)
    return cf, cbm.astype(ml_dtypes.bfloat16)


def _head_cols(h):
    gq = np.arange(0, 64) + h * 64
    gk = 256 + np.arange(0, 64) + h * 64
    gv = 512 + np.arange(0, 128) + h * 128
    glr = 1024 + np.arange(16)
    gz = 1040 + np.arange(128) + h * 128
    dq = 1552 + np.arange(128) + h * 128
    dk = 2064 + np.arange(128) + h * 128
    dv = 2576 + np.arange(128) + h * 128
    dz = 3088 + np.arange(128) + h * 128
    return np.concatenate([dq, dk, dz, gz, gv, gq, gk, dv, glr])


def _pp(inp, l, h):
    p = np.zeros((128, NPP), np.float32)
    p[:, 0:8] = inp["norm_g"][l].reshape(8, 128).T
    cw = inp["conv_w"][l]
    ch_qk = np.concatenate([np.arange(64) + h * 64, 256 + np.arange(64) + h * 64])
    p[:, 8:12] = cw[:, ch_qk].T
    p[:, 12:16] = cw[:, 512 + h * 128 + np.arange(128)].T
    p[:, 16] = inp["gla_norm_g"][l]
    p[:, 17] = np.tile(inp["qn_g"][l], 2)
    p[:, 18] = np.tile(inp["kn_g"][l], 2)
    p[:, 19] = inp["diff_norm_g"][l]
    p[0:64, 20] = inp["lam_q1"][l]
    p[0:64, 21] = inp["lam_k1"][l]
    p[0:64, 22] = inp["lam_q2"][l]
    p[0:64, 23] = inp["lam_k2"][l]
    return p


def _wgk(inp, l, h):
    w = np.zeros((17, 64), np.float32)
    w[0:16] = inp["w_gk"][l][:, h * 64:(h + 1) * 64]
    w[16] = inp["b_gk"][l][h * 64:(h + 1) * 64]
    return w


def _wout_perm(inp, l):
    rows = np.concatenate([np.concatenate([np.arange(128) + h * 128, 512 + np.arange(128) + h * 128]) for h in range(4)])
    return np.ascontiguousarray(inp["w_out"][l][rows, :])


_PROG_CACHE = {}


def _get_prog(S, n_mix, has_prev, do_final, layer_ids, fused=False):
    key = (S, n_mix, has_prev, do_final, tuple(layer_ids), fused)
    if key not in _PROG_CACHE:
        _PROG_CACHE[key] = build_program(S, n_mix, has_prev, do_final, layer_ids, fused)
    return _PROG_CACHE[key]


def run_unfused(inp, S):
    cf, cbm = _consts()
    B = inp["x"].shape[0]
    xT = [np.ascontiguousarray(inp["x"][b, :S].T) for b in range(B)]
    cT = [np.ascontiguousarray(inp["c"][b].reshape(8, 128).T) for b in range(B)]
    oT = None
    for l in range(DEPTH):
        has_prev = l > 0
        nc = _get_prog(S, 1, has_prev, False, [l])
        in_maps = []
        for core in range(8):
            b, h = core // 4, core % 4
            m = {"xT": xT[b], "cT": cT[b], "cf": cf, "cb": cbm,
                 "w_in": np.ascontiguousarray(inp["w_in"][l][:, _head_cols(h)])[None],
                 "pp": _pp(inp, l, h)[None], "wgk": _wgk(inp, l, h)[None]}
            if has_prev:
                m["w_ada"] = np.ascontiguousarray(inp["w_ada"][l - 1:l + 1])
                m["b_adaT"] = np.stack([inp["b_ada"][k].reshape(24, 128).T for k in (l - 1, l)])
                m["w_out"] = _wout_perm(inp, l - 1)[None]
                m["oT_in"] = oT[b][None]
            else:
                m["w_ada"] = np.ascontiguousarray(inp["w_ada"][l:l + 1])
                m["b_adaT"] = np.stack([inp["b_ada"][l].reshape(24, 128).T])
            in_maps.append(m)
        res = run_bass_kernel_spmd(nc, in_maps, core_ids=list(range(8))).results
        oT = [np.concatenate([res[b * 4 + h]["ohT"][0] for h in range(4)], axis=0) for b in range(B)]
        if has_prev:
            xT = [res[b * 4]["xT_out"] for b in range(B)]
    nc = _get_prog(S, 0, True, True, [DEPTH - 1])
    in_maps = []
    for core in range(8):
        b = core // 4
        in_maps.append({"xT": xT[b], "cT": cT[b], "cf": cf, "cb": cbm,
                        "w_ada": np.ascontiguousarray(inp["w_ada"][DEPTH - 1:DEPTH]),
                        "b_adaT": np.stack([inp["b_ada"][DEPTH - 1].reshape(24, 128).T]),
                        "w_out": _wout_perm(inp, DEPTH - 1)[None], "oT_in": oT[b][None]})
    res = run_bass_kernel_spmd(nc, in_maps, core_ids=list(range(8))).results
    out = np.stack([res[b * 4]["yT"].T for b in range(B)])
    return np.ascontiguousarray(out)


def fused_in_maps(inp, S):
    cf, cbm = _consts()
    B = inp["x"].shape[0]
    xT = [np.ascontiguousarray(inp["x"][b, :S].T) for b in range(B)]
    cT = [np.ascontiguousarray(inp["c"][b].reshape(8, 128).T) for b in range(B)]
    w_ada = np.ascontiguousarray(inp["w_ada"])
    b_adaT = np.stack([inp["b_ada"][l].reshape(24, 128).T for l in range(DEPTH)])
    w_out = np.stack([_wout_perm(inp, l) for l in range(DEPTH)])
    in_maps = []
    for core in range(8):
        b, h = core // 4, core % 4
        cols = _head_cols(h)
        in_maps.append({
            "xT": xT[b], "cT": cT[b], "cf": cf, "cb": cbm, "w_ada": w_ada, "b_adaT": b_adaT,
            "w_in": np.stack([inp["w_in"][l][:, cols] for l in range(DEPTH)]),
            "pp": np.stack([_pp(inp, l, h) for l in range(DEPTH)]),
            "wgk": np.stack([_wgk(inp, l, h) for l in range(DEPTH)]),
            "w_out": w_out})
    return in_maps


def run_fused(inp, S):
    B = inp["x"].shape[0]
    nc = _get_prog(S, DEPTH, False, True, list(range(DEPTH)), True)
    res = run_bass_kernel_spmd(nc, fused_in_maps(inp, S), core_ids=list(range(8))).results
    return np.ascontiguousarray(np.stack([res[b * 4]["yT"].T for b in range(B)]))


def kernel(**inputs):
    inp = {k: np.asarray(v) for k, v in inputs.items()}
    return run_fused(inp, inp["x"].shape[1])
```
